# Optimizing a Trainium2 kernel written in Bass

```python
import math
import jax
import jax.numpy as jnp
from jax import lax
import numpy as np

D_MODEL = 1024
BATCH = 4
SEQ = 4096
DEPTH = 4

GRID_W = 64
CTX_LEN = 256

SSD_HEADS = 16
SSD_HEAD_DIM = 64
SSD_INNER = SSD_HEADS * SSD_HEAD_DIM
SSD_GROUPS = 2
SSD_STATE = 128
SSD_CONV = 5
SSD_CHUNK = 128
SSD_XBC = SSD_INNER + 2 * SSD_GROUPS * SSD_STATE

HG_HEADS = 8
HG_DK = 128
HG_DV = D_MODEL // HG_HEADS
HG_INNER_K = HG_HEADS * HG_DK
HG_INNER_V = HG_HEADS * HG_DV
HG_CHUNK = 64

IN_SIZES = (SSD_INNER, SSD_XBC, 2 * SSD_HEADS, HG_INNER_K, 2 * HG_INNER_K, HG_INNER_V, HG_INNER_V)
IN_TOTAL = sum(IN_SIZES)
MIX_OUT = SSD_INNER + HG_INNER_V

CONF_KERNEL = 31
D_FF = 2816
FFN_CONV = 3

N_EVEN = (DEPTH + 1) // 2
N_ODD = DEPTH // 2
ALPHA = (2 * DEPTH) ** 0.25
BETA = (8 * DEPTH) ** -0.25
EPS = 1e-5
F32 = jnp.float32

kernel_name = 'hybrid_ssd_hgrn2_conformer_dit_trunk'


def _split(t, sizes):
    return jnp.split(t, np.cumsum(sizes)[:-1].tolist(), axis=-1)


def _flip(t, rev):
    return t[:, ::-1] if rev else t


def layer_norm(t, g, b):
    tf = t.astype(F32)
    mu = tf.mean(-1, keepdims=True)
    var = jnp.square(tf - mu).mean(-1, keepdims=True)
    return ((tf - mu) * lax.rsqrt(var + EPS) * g.astype(F32) + b.astype(F32)).astype(t.dtype)


def rms_norm(t, w, groups):
    tf = t.astype(F32)
    tg = tf.reshape(tf.shape[:-1] + (groups, tf.shape[-1] // groups))
    tg = tg * lax.rsqrt(jnp.square(tg).mean(-1, keepdims=True) + EPS)
    return tg.reshape(tf.shape) * w.astype(F32)


def modulate(t, shift, scale):
    return t * (1 + scale) + shift


def dwconv1d(t, w, b):
    k = w.shape[0]
    y = lax.conv_general_dilated(t, w[:, None, :].astype(t.dtype), (1,), [(k // 2, k // 2)],
                                 dimension_numbers=('NWC', 'WIO', 'NWC'), feature_group_count=t.shape[-1])
    return y + b.astype(t.dtype)


def dwconv2d(t, w, b, rows, cols):
    bsz, n, ch = t.shape
    k = w.shape[0]
    y = lax.conv_general_dilated(t.reshape(bsz, rows, cols, ch), w[:, :, None, :].astype(t.dtype), (1, 1),
                                 [(k // 2, k // 2), (k // 2, k // 2)],
                                 dimension_numbers=('NHWC', 'HWIO', 'NHWC'), feature_group_count=ch)
    return y.reshape(bsz, n, ch) + b.astype(t.dtype)


def segsum(a):
    t = a.shape[-1]
    cs = jnp.cumsum(a, -1)
    mask = jnp.tril(jnp.ones((t, t), dtype=bool))
    return jnp.where(mask, cs[..., :, None] - cs[..., None, :], -jnp.inf)


def ssd_scan(xs, dt, a_coef, bm, cm, h0):
    b, l, h, p = xs.shape
    g, n = bm.shape[2:]
    j = h // g
    c = SSD_CHUNK
    nc = l // c
    xdt = (xs.astype(F32) * dt[..., None]).reshape(b, nc, c, g, j, p)
    a = (dt * a_coef).reshape(b, nc, c, h).transpose(0, 3, 1, 2)
    bc = bm.astype(F32).reshape(b, nc, c, g, n)
    cc = cm.astype(F32).reshape(b, nc, c, g, n)
    a_cum = jnp.cumsum(a, -1)
    decay = jnp.exp(segsum(a)).reshape(b, g, j, nc, c, c)
    cb = jnp.einsum('bclgn,bcsgn->bgcls', cc, bc)
    y_diag = jnp.einsum('bgcls,bgjcls,bcsgjp->bclgjp', cb, decay, xdt)
    decay_states = jnp.exp(a_cum[..., -1:] - a_cum).reshape(b, g, j, nc, c)
    states = jnp.einsum('bcsgn,bgjcs,bcsgjp->bcgjpn', bc, decay_states, xdt).reshape(b, nc, h, p, n)
    states = jnp.concatenate([h0[:, None].astype(F32), states], axis=1)
    decay_chunk = jnp.exp(segsum(jnp.pad(a_cum[..., -1], ((0, 0), (0, 0), (1, 0)))))
    states = jnp.einsum('bhzc,bchpn->bzhpn', decay_chunk, states)
    start, final = states[:, :-1], states[:, -1]
    y_off = jnp.einsum('bclgn,bcgjpn,bgjcl->bclgjp', cc, start.reshape(b, nc, g, j, p, n),
                       jnp.exp(a_cum).reshape(b, g, j, nc, c))
    return (y_diag + y_off).reshape(b, l, h, p), final


def gla_scan(q, k, v, log_f, s0):
    b, l, h, dk = q.shape
    dv = v.shape[-1]
    c = HG_CHUNK
    nc = l // c
    q, k, v, log_f = [t.astype(F32).reshape(b, nc, c, h, t.shape[-1]) for t in (q, k, v, log_f)]

    def step(s, inp):
        q_t, k_t, v_t, lf_t = inp
        s = jnp.exp(lf_t)[..., None] * s + k_t[..., None] * v_t[..., None, :]
        return s, jnp.einsum('bchk,bchkv->bchv', q_t, s)

    s_loc, o_loc = lax.scan(step, jnp.zeros((b, nc, h, dk, dv), F32),
                            tuple(jnp.moveaxis(t, 2, 0) for t in (q, k, v, log_f)))
    o_loc = jnp.moveaxis(o_loc, 0, 2)
    a_cum = jnp.cumsum(log_f, axis=2)

    def chunk_step(s, inp):
        s_end_local, a_tot = inp
        return jnp.exp(a_tot)[..., None] * s + s_end_local, s

    s_final, s_start = lax.scan(chunk_step, s0.astype(F32),
                                (jnp.moveaxis(s_loc, 1, 0), jnp.moveaxis(a_cum[:, :, -1], 1, 0)))
    s_start = jnp.moveaxis(s_start, 0, 1)
    o = o_loc + jnp.einsum('bcthk,bchkv->bcthv', q * jnp.exp(a_cum), s_start)
    return o.reshape(b, l, h, dv), s_final


def hgrn_gates(f_logit, lb, first):
    a = f_logit.astype(F32)
    if first:
        return jax.nn.log_sigmoid(a), jax.nn.sigmoid(-a)
    lb = lb.reshape(HG_HEADS, HG_DK)
    return jnp.log(lb + (1 - lb) * jax.nn.sigmoid(a)), (1 - lb) * jax.nn.sigmoid(-a)


def even_mixer(h, hc, w_in, conv_w, conv_b, dt_bias, a_log, d_skip, ssd_nw, lb, first, hg_nw, w_out, ctx_out):
    def branch_inputs(t):
        bsz, L, _ = t.shape
        z, xbc, dt_raw, q, f, v, g = _split(t @ w_in, IN_SIZES)
        xbc = jax.nn.silu(dwconv1d(xbc, conv_w, conv_b))
        xs, bm, cm = _split(xbc, (SSD_INNER, SSD_GROUPS * SSD_STATE, SSD_GROUPS * SSD_STATE))
        return {'z': z, 'g': g,
                'dt': dt_raw.astype(F32).reshape(bsz, L, 2, SSD_HEADS),
                'xs': xs.reshape(bsz, L, SSD_HEADS, SSD_HEAD_DIM),
                'bm': bm.reshape(bsz, L, SSD_GROUPS, SSD_STATE),
                'cm': cm.reshape(bsz, L, SSD_GROUPS, SSD_STATE),
                'q': jax.nn.silu(q).reshape(bsz, L, HG_HEADS, HG_DK),
                'f': f.reshape(bsz, L, 2, HG_HEADS, HG_DK),
                'v': v.reshape(bsz, L, HG_HEADS, HG_DV)}

    lat, cx = branch_inputs(h), branch_inputs(hc)
    bsz = h.shape[0]
    dsk = d_skip.astype(F32)[:, None]
    y_lat, y_ctx = dsk * lat['xs'].astype(F32), dsk * cx['xs'].astype(F32)
    o_lat_parts, o_ctx_parts = [], []
    for d in range(2):
        rev = d == 1
        a_coef = -jnp.exp(a_log[d].astype(F32))
        dt_l = jax.nn.softplus(lat['dt'][:, :, d] + dt_bias[d].astype(F32))
        dt_c = jax.nn.softplus(cx['dt'][:, :, d] + dt_bias[d].astype(F32))
        zeros_ssd = jnp.zeros((bsz, SSD_HEADS, SSD_HEAD_DIM, SSD_STATE), F32)
        yc_d, s_ctx = ssd_scan(_flip(cx['xs'], rev), _flip(dt_c, rev), a_coef,
                               _flip(cx['bm'], rev), _flip(cx['cm'], rev), zeros_ssd)
        yl_d, _ = ssd_scan(_flip(lat['xs'], rev), _flip(dt_l, rev), a_coef,
                           _flip(lat['bm'], rev), _flip(lat['cm'], rev), s_ctx)
        y_lat = y_lat + _flip(yl_d, rev)
        y_ctx = y_ctx + _flip(yc_d, rev)
        lf_l, k_l = hgrn_gates(lat['f'][:, :, d], lb, first)
        lf_c, k_c = hgrn_gates(cx['f'][:, :, d], lb, first)
        zeros_hg = jnp.zeros((bsz, HG_HEADS, HG_DK, HG_DV), F32)
        oc_d, st_ctx = gla_scan(_flip(cx['q'], rev), _flip(k_c, rev), _flip(cx['v'], rev),
                                _flip(lf_c, rev), zeros_hg)
        ol_d, _ = gla_scan(_flip(lat['q'], rev), _flip(k_l, rev), _flip(lat['v'], rev),
                           _flip(lf_l, rev), st_ctx)
        o_lat_parts.append(_flip(ol_d, rev))
        o_ctx_parts.append(_flip(oc_d, rev))

    def merge(s, y, o):
        bsz_, L = s['z'].shape[:2]
        ys = rms_norm(y.reshape(bsz_, L, SSD_INNER) * jax.nn.silu(s['z'].astype(F32)), ssd_nw, SSD_GROUPS)
        os_ = rms_norm(o.reshape(bsz_, L, HG_INNER_V), hg_nw, HG_HEADS) * jax.nn.silu(s['g'].astype(F32))
        return jnp.concatenate([ys, os_], axis=-1).astype(h.dtype) @ w_out

    y = merge(lat, y_lat, o_lat_parts[0] + o_lat_parts[1])
    yc = merge(cx, y_ctx, o_ctx_parts[0] + o_ctx_parts[1]) if ctx_out else None
    return y, yc


def conformer_conv(t, w1, b1, dw_w, dw_b, ln_g, ln_b, w2, b2):
    a, gate = jnp.split(t @ w1 + b1, 2, axis=-1)
    u = dwconv1d(a * jax.nn.sigmoid(gate), dw_w, dw_b)
    u = jax.nn.silu(layer_norm(u, ln_g, ln_b))
    return u @ w2 + b2


def conv_glu_ffn(t, w_up, conv_w, conv_b, w_down, rows, cols):
    v, gate = jnp.split(t @ w_up, 2, axis=-1)
    gate = dwconv2d(gate, conv_w, conv_b, rows, cols)
    return (jax.nn.gelu(gate, approximate=False) * v) @ w_down


def setup_inputs(seed: int = 0) -> dict:
    key = jax.random.key(seed)
    keys = iter(jax.random.split(key, 32))

    def nrm(shape, scale):
        return jax.random.normal(next(keys), shape, F32) * scale

    def unif(shape, lo, hi):
        return jax.random.uniform(next(keys), shape, F32, lo, hi)

    d = D_MODEL
    inputs = {}
    inputs['x'] = nrm((BATCH, SEQ, d), 1.0)
    inputs['c'] = nrm((BATCH, d), 1.0)
    inputs['ctx'] = nrm((BATCH, CTX_LEN, d), 1.0)
    inputs['c_ctx'] = nrm((d,), 1.0)
    inputs['ada_w'] = nrm((DEPTH, d, 6 * d), d ** -0.5)
    inputs['ada_b'] = nrm((DEPTH, 6 * d), 0.02)
    inputs['post_ln_g'] = 1.0 + nrm((DEPTH, 2, d), 0.02)
    inputs['post_ln_b'] = nrm((DEPTH, 2, d), 0.02)
    inputs['mix_w_in'] = nrm((N_EVEN, d, IN_TOTAL), d ** -0.5)
    inputs['ssd_conv_w'] = nrm((N_EVEN, SSD_CONV, SSD_XBC), SSD_CONV ** -0.5)
    inputs['ssd_conv_b'] = nrm((N_EVEN, SSD_XBC), 0.02)
    dt0 = jnp.exp(unif((N_EVEN, 2, SSD_HEADS), math.log(1e-3), math.log(1e-1)))
    inputs['ssd_dt_bias'] = dt0 + jnp.log(-jnp.expm1(-dt0))
    inputs['ssd_a_log'] = jnp.log(unif((N_EVEN, 2, SSD_HEADS), 1.0, 16.0))
    inputs['ssd_d'] = 1.0 + nrm((N_EVEN, SSD_HEADS), 0.02)
    inputs['ssd_norm_w'] = 1.0 + nrm((N_EVEN, SSD_INNER), 0.02)
    inputs['hg_lb_raw'] = nrm((N_EVEN, HG_INNER_K), 0.5)
    inputs['hg_norm_w'] = 1.0 + nrm((N_EVEN, HG_INNER_V), 0.02)
    inputs['mix_w_out'] = nrm((N_EVEN, MIX_OUT, d), MIX_OUT ** -0.5 * BETA)
    inputs['conf_w1'] = nrm((N_ODD, d, 2 * d), d ** -0.5)
    inputs['conf_b1'] = nrm((N_ODD, 2 * d), 0.02)
    inputs['conf_dw_w'] = nrm((N_ODD, CONF_KERNEL, d), CONF_KERNEL ** -0.5)
    inputs['conf_dw_b'] = nrm((N_ODD, d), 0.02)
    inputs['conf_ln_g'] = 1.0 + nrm((N_ODD, d), 0.02)
    inputs['conf_ln_b'] = nrm((N_ODD, d), 0.02)
    inputs['conf_w2'] = nrm((N_ODD, d, d), d ** -0.5 * BETA)
    inputs['conf_b2'] = nrm((N_ODD, d), 0.02)
    inputs['ffn_w_up'] = nrm((DEPTH, d, 2 * D_FF), d ** -0.5)
    inputs['ffn_conv_w'] = nrm((DEPTH, FFN_CONV, FFN_CONV, D_FF), 1.0 / FFN_CONV)
    inputs['ffn_conv_b'] = nrm((DEPTH, D_FF), 0.02)
    inputs['ffn_w_down'] = nrm((DEPTH, D_FF, d), D_FF ** -0.5 * BETA)
    return inputs


def reference(x, c, ctx, c_ctx, ada_w, ada_b, post_ln_g, post_ln_b, mix_w_in, ssd_conv_w, ssd_conv_b,
              ssd_dt_bias, ssd_a_log, ssd_d, ssd_norm_w, hg_lb_raw, hg_norm_w, mix_w_out,
              conf_w1, conf_b1, conf_dw_w, conf_dw_b, conf_ln_g, conf_ln_b, conf_w2, conf_b2,
              ffn_w_up, ffn_conv_w, ffn_conv_b, ffn_w_down):
    rows = x.shape[1] // GRID_W
    lb_p = jax.nn.softmax(hg_lb_raw.astype(F32), axis=0)
    lb_all = jnp.cumsum(lb_p, axis=0) - lb_p[0]
    s_c = jax.nn.silu(c)
    s_ctx = jax.nn.silu(c_ctx)
    xc = ctx
    for l in range(DEPTH):
        m = (s_c @ ada_w[l] + ada_b[l]).reshape(-1, 6, 1, D_MODEL)
        mc = (s_ctx @ ada_w[l] + ada_b[l]).reshape(6, D_MODEL)
        ctx_next = any(j % 2 == 0 for j in range(l + 1, DEPTH))
        h = modulate(x, m[:, 0], m[:, 1])
        if l % 2 == 0:
            e = l // 2
            hc = modulate(xc, mc[0], mc[1])
            y, yc = even_mixer(h, hc, mix_w_in[e], ssd_conv_w[e], ssd_conv_b[e], ssd_dt_bias[e], ssd_a_log[e],
                               ssd_d[e], ssd_norm_w[e], lb_all[e], e == 0, hg_norm_w[e], mix_w_out[e], ctx_next)
        else:
            o = l // 2
            conf = (conf_w1[o], conf_b1[o], conf_dw_w[o], conf_dw_b[o], conf_ln_g[o], conf_ln_b[o],
                    conf_w2[o], conf_b2[o])
            y = conformer_conv(h, *conf)
            yc = conformer_conv(modulate(xc, mc[0], mc[1]), *conf) if ctx_next else None
        ffn = (ffn_w_up[l], ffn_conv_w[l], ffn_conv_b[l], ffn_w_down[l])
        x = layer_norm(ALPHA * x + m[:, 2] * y, post_ln_g[l, 0], post_ln_b[l, 0])
        x = layer_norm(ALPHA * x + m[:, 5] * conv_glu_ffn(modulate(x, m[:, 3], m[:, 4]), *ffn, rows, GRID_W),
                       post_ln_g[l, 1], post_ln_b[l, 1])
        if ctx_next:
            xc = layer_norm(ALPHA * xc + mc[2] * yc, post_ln_g[l, 0], post_ln_b[l, 0])
            xc = layer_norm(ALPHA * xc + mc[5] * conv_glu_ffn(modulate(xc, mc[3], mc[4]), *ffn, 1, xc.shape[1]),
                            post_ln_g[l, 1], post_ln_b[l, 1])
    return x
```

```python
from contextlib import ExitStack
import numpy as np
import concourse.bass as bass
import concourse.mybir as mybir
from concourse.bass_utils import run_bass_kernel_spmd

F32 = mybir.dt.float32
BF16 = mybir.dt.bfloat16
AF = mybir.ActivationFunctionType
ALU = mybir.AluOpType

D = 1024
KC = 8
DEPTH = 4
D_FF = 2816
FC = 22
GRID_W = 64
IN_TOTAL = 7712
ALPHA = (2 * DEPTH) ** 0.25
EPS = 1e-5
CONF_K = 31


class Tok:
    __slots__ = ("sem", "value")

    def __init__(self, sem, value):
        self.sem = sem
        self.value = value


class DSem:
    def __init__(self, h):
        self.h = h
        self.count = 0


class Buf:
    def __init__(self, kb, name, t=None, dram=False):
        self.kb = kb
        self.name = name
        self.t = t
        self.dram = dram
        self.writers = []
        self.readers = []
        self.dsem = None
        self.inflight = []

    def __getitem__(self, idx):
        return self.t[idx]


class Eng:
    def __init__(self, name, same_sync):
        self.name = name
        self.ops = []
        self.count = 0
        self.sem = None
        self.waited = {}
        self.same_sync = same_sync


class KB:
    def __init__(self, nc, stack):
        self.nc = nc
        self.stack = stack
        self.pe = Eng("tensor", False)
        self.act = Eng("scalar", True)
        self.dve = Eng("vector", True)
        self.pool = Eng("gpsimd", True)
        self.sp = Eng("sync", False)
        self.engs = [self.pe, self.act, self.dve, self.pool, self.sp]
        self.nsem = 0
        self.dsem_free = []
        self.dsem_used = []
        self.pstack = None
        self.pbufs = []
        self.dram_bufs = []
        for e in self.engs:
            e.sem = self.new_sem("c_" + e.name)
            e.h = getattr(nc, e.name)

    def phase_begin(self):
        self.pstack = ExitStack()
        self.pbufs = []

    def phase_end(self):
        toks = [Tok(e.sem, e.count) for e in self.engs if e.count]
        toks += [Tok(d.h, d.count) for d in self.dsem_used + self.dsem_free if d.count]
        for e in self.engs:
            self._emit_waits(e, toks)
        for b in self.pbufs:
            if b.dsem is not None:
                self.dsem_used.remove(b.dsem)
                self.dsem_free.append(b.dsem)
                b.dsem = None
        for b in self.dram_bufs:
            b.writers = []
            b.readers = []
        self.pstack.close()
        self.pstack = None
        self.pbufs = []

    def get_dsem(self):
        if self.dsem_free:
            d = self.dsem_free.pop()
        else:
            d = DSem(self.new_sem(f"d{self.nsem}"))
        self.dsem_used.append(d)
        return d

    def new_sem(self, name):
        self.nsem += 1
        return self.stack.enter_context(self.nc.semaphore(name))

    def sbuf(self, name, shape, dtype):
        st = self.pstack if self.pstack is not None else self.stack
        self.nsb = getattr(self, "nsb", 0) + 1
        t = st.enter_context(self.nc.sbuf_tensor(f"{name}_{self.nsb}", list(shape), dtype))
        b = Buf(self, name, t)
        if self.pstack is not None:
            self.pbufs.append(b)
        return b

    def psum(self, name, shape, dtype):
        t = self.stack.enter_context(self.nc.psum_tensor(name, list(shape), dtype))
        return Buf(self, name, t)

    def dram(self, name, shape, dtype, kind="Internal"):
        t = self.nc.dram_tensor(name, list(shape), dtype, kind=kind)
        b = Buf(self, name, t.ap(), dram=True)
        self.dram_bufs.append(b)
        return b

    def share_dsem(self, bufs):
        return bufs

    def _deps(self, reads, writes):
        deps = []
        for b in reads:
            deps += b.writers
        for b in writes:
            deps += b.writers
            deps += b.readers
        return deps

    def _emit_waits(self, eng, deps):
        need = {}
        for t in deps:
            if t.sem is eng.sem and not eng.same_sync:
                continue
            k = id(t.sem)
            if k not in need or need[k].value < t.value:
                need[k] = t
        for k, t in need.items():
            if eng.waited.get(k, 0) >= t.value:
                continue
            eng.waited[k] = t.value
            eng.h.wait_ge(t.sem, t.value)

    def _commit(self, tok, reads, writes):
        for b in reads:
            b.readers = [r for r in b.readers if r.sem is not tok.sem] + [tok]
        for b in writes:
            b.writers = [tok]
            b.readers = []

    def op(self, eng, fn, reads=(), writes=()):
        reads = [b for b in reads if b is not None]
        writes = [b for b in writes if b is not None]
        self._emit_waits(eng, self._deps(reads, writes))
        eng.count += 1
        tok = Tok(eng.sem, eng.count)
        fn(eng.h).then_inc(eng.sem, 1)
        self._commit(tok, reads, writes)
        return tok

    def dma(self, eng, out_ap, in_ap, src, dst, **kw):
        sb = dst if not dst.dram else src
        if sb.dsem is None:
            sb.dsem = self.get_dsem()
        deps = self._deps([src], [dst])
        self._emit_waits(eng, deps)
        sb.dsem.count += 16
        tok = Tok(sb.dsem.h, sb.dsem.count)
        depset = set(id(d) for d in deps)
        keep = []
        for t in sb.inflight:
            if id(t) not in depset:
                t.value = tok.value
                keep.append(t)
        keep.append(tok)
        sb.inflight = keep[-8:]
        eng.h.dma_start(out=out_ap, in_=in_ap, **kw).then_inc(sb.dsem.h, 16)
        self.ndma = getattr(self, "ndma", 0) + 1
        self._commit(tok, [src], [dst])
        return tok

    def finish(self, final_toks):
        self._emit_waits(self.sp, final_toks)


def _cp(v):
    v = np.asarray(v, np.float32)
    return np.ascontiguousarray(v.reshape(-1, 128).T)


class PP:
    def __init__(self):
        self.cols = []
        self.off = {}
        self.n = 0

    def add(self, name, arr):
        arr = np.asarray(arr, np.float32)
        assert arr.shape[0] == 128
        arr = arr.reshape(128, -1)
        self.off[name] = (self.n, arr.shape[1])
        self.n += arr.shape[1]
        self.cols.append(arr)

    def pack(self):
        return np.ascontiguousarray(np.concatenate(self.cols, axis=1))


def pack_pp(inp):
    pp = PP()
    for l in range(DEPTH):
        pp.add(f"ada_b{l}", _cp(inp["ada_b"][l]))
        for j in range(2):
            pp.add(f"plg{l}_{j}", _cp(inp["post_ln_g"][l, j]))
            pp.add(f"plb{l}_{j}", _cp(inp["post_ln_b"][l, j]))
        w = inp["ffn_conv_w"][l].reshape(9, D_FF)
        pp.add(f"fcw{l}", np.stack([_cp(w[k]) for k in range(9)], axis=1))
        pp.add(f"fcb{l}", _cp(inp["ffn_conv_b"][l]))
    for o in range(2):
        pp.add(f"cb1{o}", _cp(inp["conf_b1"][o]))
        w = inp["conf_dw_w"][o]
        pp.add(f"cdw{o}", np.stack([_cp(w[k]) for k in range(CONF_K)], axis=1))
        pp.add(f"cdb{o}", _cp(inp["conf_dw_b"][o]))
        pp.add(f"clg{o}", _cp(inp["conf_ln_g"][o]))
        pp.add(f"clb{o}", _cp(inp["conf_ln_b"][o]))
        pp.add(f"cb2{o}", _cp(inp["conf_b2"][o]))
    for e in range(2):
        w = inp["ssd_conv_w"][e]
        pp.add(f"scw{e}", np.stack([_cp(w[k]) for k in range(5)], axis=1))
        pp.add(f"scb{e}", _cp(inp["ssd_conv_b"][e]))
        pp.add(f"lbr{e}", _cp(inp["hg_lb_raw"][e]))
        pp.add(f"hnw{e}", _cp(inp["hg_norm_w"][e]))
        pp.add(f"snw{e}", _cp(inp["ssd_norm_w"][e]))
        pp.add(f"sdd{e}", _cp(np.repeat(inp["ssd_d"][e], 64)))
    return pp


def make_consts():
    c = {}
    c["ident"] = np.eye(128, dtype=np.float32)
    c["onesm"] = np.full((128, 128), 1.0 / 1024, np.float32)
    i = np.arange(128)
    tf = (i[:, None] <= i[None, :]).astype(np.float32)
    tb = (i[:, None] >= i[None, :]).astype(np.float32)
    blk = (i[:, None] // 32 == i[None, :] // 32).astype(np.float32)
    rp = np.ones((128, 1024), np.float32)
    rp[:, ::32] = 0.0
    c["cmat"] = np.ascontiguousarray(np.stack([c["ident"], c["onesm"], tf, tb, np.ones((128, 128), np.float32),
                                               tf * blk, tb * blk], axis=1))
    c["rp"] = rp
    return c


class Gen:
    def __init__(self, nc, stack, T, M, layers, dbg=()):
        self.nc = nc
        self.T = T
        self.M = M
        self.layers = layers
        self.dbg = set(dbg)
        self.kb = KB(nc, stack)
        self.stack = stack
        self.dbg_out = {}
        self.uid = 0

    def ext_in(self, name, shape, dtype=F32):
        ap = self.nc.dram_tensor(name, list(shape), dtype, kind="ExternalInput").ap()
        return Buf(self.kb, name, ap, dram=True)

    def scratch(self, name, shape, dtype):
        kind = "ExternalOutput" if name in self.dbg else "Internal"
        b = self.kb.dram(name, shape, dtype, kind=kind)
        return b

    def pool(self, name, n, shape, dtype):
        return [self.kb.sbuf(f"{name}{i}", shape, dtype) for i in range(n)]

    def ps(self):
        b = self.psb[self.psi % len(self.psb)]
        self.psi += 1
        return b

    def ACT(self, out, in_, func, reads, writes, bias=None, scale=None):
        kw = {}
        if bias is not None:
            kw["bias"] = bias
        if scale is not None:
            kw["scale"] = scale
        return self.kb.op(self.kb.act, lambda e: e.activation(out=out, in_=in_, func=func, **kw), reads, writes)

    def TS(self, eng, out, in0, s1, s2, op0, op1, reads, writes):
        if s2 is None:
            return self.kb.op(eng, lambda e: e.tensor_scalar(out=out, in0=in0, scalar1=s1, scalar2=None, op0=op0), reads, writes)
        return self.kb.op(eng, lambda e: e.tensor_scalar(out=out, in0=in0, scalar1=s1, scalar2=s2, op0=op0, op1=op1), reads, writes)

    def TT(self, eng, out, in0, in1, op, reads, writes):
        return self.kb.op(eng, lambda e: e.tensor_tensor(out=out, in0=in0, in1=in1, op=op), reads, writes)

    def STT(self, out, in0, scalar, in1, op0, op1, reads, writes):
        return self.kb.op(self.kb.dve, lambda e: e.scalar_tensor_tensor(out=out, in0=in0, scalar=scalar, in1=in1, op0=op0, op1=op1), reads, writes)

    def COPY(self, eng, out, in_, reads, writes):
        return self.kb.op(eng, lambda e: e.tensor_copy(out=out, in_=in_), reads, writes)

    def MM(self, out, lhsT, rhs, start, stop, reads, writes):
        return self.kb.op(self.kb.pe, lambda e: e.matmul(out, lhsT=lhsT, rhs=rhs, start=start, stop=stop), reads, writes)

    def LOAD(self, dst_ap, src_ap, src, dst, **kw):
        return self.kb.dma(self.kb.sp, dst_ap, src_ap, src, dst, **kw)

    def STORE(self, dst_ap, src_ap, src, dst, **kw):
        return self.kb.dma(self.kb.pool, dst_ap, src_ap, src, dst, **kw)

    def ppc(self, name, idx=None):
        off, n = self.pp_off[name]
        if idx is None:
            return self.ppt[:, off:off + n]
        return self.ppt[:, off + idx:off + idx + 1]

    def setup(self, pp_off, npp):
        kb = self.kb
        T, M = self.T, self.M
        self.pp_off = pp_off
        self.in_x = self.ext_in("x_cm", [D, T])
        self.in_ctx = self.ext_in("ctx_cm", [D, M])
        self.in_cvec = self.ext_in("cvec", [128, KC, 2])
        self.in_pp = self.ext_in("pp", [128, npp])
        self.in_cmat = self.ext_in("cmat", [128, 7, 128])
        self.in_rp = self.ext_in("rp", [128, 1024])
        self.in_tmb = self.ext_in("tmb", [128, 2, 64])
        self.in_mix_in = self.ext_in("mix_w_in", [2, D, IN_TOTAL])
        self.in_mix_out = self.ext_in("mix_w_out", [2, 2 * D, D])
        self.in_ada_w = self.ext_in("ada_w", [DEPTH, D, 6 * D])
        self.in_conf_w1 = self.ext_in("conf_w1", [2, D, 2 * D])
        self.in_conf_w2 = self.ext_in("conf_w2", [2, D, D])
        self.in_ffn_up = self.ext_in("ffn_w_up", [DEPTH, D, 2 * D_FF])
        self.in_ffn_down = self.ext_in("ffn_w_down", [DEPTH, D_FF, D])
        self.out_y = Buf(kb, "y_cm", self.nc.dram_tensor("y_cm", [D, T], F32, kind="ExternalOutput").ap(), dram=True)

        self.pb = [kb.psum(f"psb{i}", [128, 512], F32) for i in range(8)]
        self.psb = self.pb[:6]
        self.psi = 0
        self.ps_mean = self.pb[6]
        self.ps_msq = self.pb[7]

        self.ppb = kb.sbuf("ppt", [128, npp], F32)
        self.ppt = self.ppb.t
        self.LOAD(self.ppb[:], self.in_pp[:], self.in_pp, self.ppb)
        self.cst_f = kb.sbuf("cst_f", [128, 7, 128], F32)
        self.LOAD(self.cst_f[:], self.in_cmat[:], self.in_cmat, self.cst_f)
        self.cst_b = kb.sbuf("cst_b", [128, 7, 128], BF16)
        self.COPY(kb.dve, self.cst_b[:], self.cst_f[:], [self.cst_f], [self.cst_b])
        self.ident_b = self.cst_b[:, 0, :]
        self.onesm_b = self.cst_b[:, 1, :]
        self.tmb = kb.sbuf("tmb", [128, 2, 64], F32)
        self.LOAD(self.tmb[:], self.in_tmb[:], self.in_tmb, self.tmb)

        self.X = self.scratch("X", [D, T], F32)
        self.XC = self.scratch("XC", [D, M], F32)

        self.mod = kb.sbuf("mod", [128, DEPTH, 48, 2], F32)
        self.sc1 = kb.sbuf("sc1", [128, DEPTH, 2, KC, 2], F32)

    def ada(self):
        kb = self.kb
        kb.phase_begin()
        cv = kb.sbuf("cv", [128, KC, 2], F32)
        self.LOAD(cv[:], self.in_cvec[:], self.in_cvec, cv)
        sv = kb.sbuf("sv", [128, KC, 2], F32)
        self.ACT(sv[:], cv[:], AF.Silu, [cv], [sv])
        wst = self.pool("adaw", 2, [128, KC, 512], F32)
        it = 0
        for l in self.layers:
            pacc = self.ps()
            for nb in range(12):
                w = wst[it % 2]
                it += 1
                self.LOAD(w[:], self.in_ada_w[l].rearrange("(c p) n -> p c n", p=128)[:, :, nb * 512:(nb + 1) * 512],
                          self.in_ada_w, w)
                for cc in range(4):
                    col = nb * 4 + cc
                    for k in range(KC):
                        self.MM(pacc[:, col * 2:col * 2 + 2], w[:, k, cc * 128:(cc + 1) * 128], sv[:, k, :],
                                k == 0, k == KC - 1, [w, sv], [pacc])
            ab = self.ppc(f"ada_b{l}")
            self.TT(kb.dve, self.mod[:, l, :, :], pacc[:, 0:96].rearrange("p (c t) -> p c t", t=2),
                    ab.unsqueeze(2).to_broadcast([128, 48, 2]), ALU.add, [pacc, self.ppb], [self.mod])
            for j, si in enumerate((1, 4)):
                self.TS(kb.dve, self.sc1[:, l, j, :, :], self.mod[:, l, si * 8:(si + 1) * 8, :], 1.0, None,
                        ALU.add, None, [self.mod], [self.sc1])
        kb.phase_end()

    def mvec(self, l, j, c, seg):
        return self.mod[:, l, j * 8 + c, seg:seg + 1]

    def ln_alloc(self):
        kb = self.kb
        self.ln_rb = kb.sbuf("ln_rb", [128, KC, 512], BF16)
        self.ln_rsq = kb.sbuf("ln_rsq", [128, KC, 512], BF16)
        self.ln_mean = kb.sbuf("ln_mean", [128, 512], F32)
        self.ln_t = kb.sbuf("ln_t", [128, 512], F32)
        self.ln_rstd = kb.sbuf("ln_rstd", [128, 512], F32)
        self.ln_xc = kb.sbuf("ln_xc", [128, KC, 512], F32)

    def ln_tile(self, r, N, gname, bname, func, out):
        kb = self.kb
        rb, rsq, mean, tt, rstd, xc = self.ln_rb, self.ln_rsq, self.ln_mean, self.ln_t, self.ln_rstd, self.ln_xc
        self.COPY(kb.pool, rb[:, :, :N], r[:, :, :N], [r], [rb])
        self.ACT(rsq[:, :, :N], r[:, :, :N], AF.Square, [r], [rsq])
        pm, pq = self.ps_mean, self.ps_msq
        for c in range(KC):
            self.MM(pm[:, :N], self.onesm_b, rb[:, c, :N], c == 0, c == KC - 1, [self.cst_b, rb], [pm])
        for c in range(KC):
            self.MM(pq[:, :N], self.onesm_b, rsq[:, c, :N], c == 0, c == KC - 1, [self.cst_b, rsq], [pq])
        self.ACT(mean[:, :N], pm[:, :N], AF.Copy, [pm], [mean])
        self.ACT(tt[:, :N], pm[:, :N], AF.Square, [pm], [tt])
        self.TT(kb.dve, tt[:, :N], pq[:, :N], tt[:, :N], ALU.subtract, [pq, tt], [tt])
        self.ACT(tt[:, :N], tt[:, :N], AF.Ln, [tt, self.eps_t], [tt], bias=self.eps_ap)
        self.ACT(rstd[:, :N], tt[:, :N], AF.Exp, [tt], [rstd], scale=-0.5)
        self.TT(kb.dve, xc[:, :, :N], r[:, :, :N], mean[:, :N].unsqueeze(1).to_broadcast([128, KC, N]),
                ALU.subtract, [r, mean], [xc])
        self.TT(kb.dve, xc[:, :, :N], xc[:, :, :N], rstd[:, :N].unsqueeze(1).to_broadcast([128, KC, N]),
                ALU.mult, [xc, rstd], [xc])
        for c in range(KC):
            self.ACT(out[:, c, :N], xc[:, c, :N], func, [xc, self.ppb], [out],
                     bias=self.ppc(bname, c), scale=self.ppc(gname, c))

    def load_w(self, dst, src_buf, src_ap, kchunks, ncols, blk=256, dstbuf=None):
        kb = self.kb
        dstbuf = dstbuf if dstbuf is not None else dst
        if getattr(self, "_ws_phase", None) is not kb.pstack or self._ws_blk < blk:
            self._ws = self.pool("wstage", 2, [128, 8, blk], F32)
            self._ws_phase = kb.pstack
            self._ws_blk = blk
        wstage = self._ws
        v = src_ap.rearrange("(c p) n -> p c n", p=128)
        i = 0
        for k0 in range(0, kchunks, 8):
            k1 = min(k0 + 8, kchunks)
            for n0 in range(0, ncols, blk):
                n1 = min(n0 + blk, ncols)
                st = wstage[i % 2]
                self.LOAD(st[:, :k1 - k0, :n1 - n0], v[:, k0:k1, n0:n1], src_buf, st)
                eng = kb.pool if i % 2 == 0 else kb.dve
                i += 1
                self.COPY(eng, dst[:, k0:k1, n0:n1], st[:, :k1 - k0, :n1 - n0], [st], [dstbuf])

    def segs(self, with_ctx):
        s = [(0, self.X, self.T)]
        if with_ctx:
            s.append((1, self.XC, self.M))
        return s

    def tiles(self, with_ctx):
        out = []
        for seg, Xb, Tn in self.segs(with_ctx):
            for i in range((Tn + 511) // 512):
                out.append((seg, Xb, i * 512, min(512, Tn - i * 512)))
        return out

    @staticmethod
    def cm(buf):
        return buf.t.rearrange("(c p) t -> p c t", p=128)

    def modulate(self, xt, hb, N, l, j, seg):
        for c in range(KC):
            self.ACT(hb[:, c, :N], xt[:, c, :N], AF.Identity, [xt, self.sc1, self.mod], [hb],
                     bias=self.mvec(l, 3 * j, c, seg), scale=self.sc1[:, l, j, c, seg:seg + 1])

    def residual(self, pst, c, xt, rt, N, l, gate_j, seg, bias_ap=None):
        kb = self.kb
        g = self.mvec(l, gate_j, c, seg)
        tmp = self.rtmp[self.rti % 2]
        self.rti += 1
        if bias_ap is not None:
            self.TS(kb.dve, tmp[:, :N], pst[:, :N], bias_ap, g, ALU.add, ALU.mult, [pst, self.mod, self.ppb], [tmp])
        else:
            self.TS(kb.dve, tmp[:, :N], pst[:, :N], g, None, ALU.mult, None, [pst, self.mod], [tmp])
        self.STT(rt[:, c, :N], xt[:, c, :N], ALPHA, tmp[:, :N], ALU.mult, ALU.add, [xt, tmp], [rt])

    def build_diag(self, dg, n, wcol_fn):
        kb = self.kb
        for k in range(n):
            eng = kb.dve if k % 2 == 0 else kb.pool
            self.TS(eng, dg[:, k, :], self.ident_b, wcol_fn(k), None, ALU.mult, None, [self.cst_b, self.ppb], [dg])

    def conformer(self, l, with_ctx):
        kb = self.kb
        o = l // 2
        kb.phase_begin()
        w1 = kb.sbuf("w1", [128, KC, 2 * D], BF16)
        self.load_w(w1, self.in_conf_w1, self.in_conf_w1[o], KC, 2 * D)
        xts = self.pool("xt", 2, [128, KC, 512], F32)
        hbs = self.pool("hb", 2, [128, KC, 512], BF16)
        stg = self.pool("stg", 2, [128, KC, 512], BF16)
        sgs = self.pool("sg", 2, [128, 512], F32)
        for it, (seg, Xb, t0, N) in enumerate(self.tiles(with_ctx)):
            GL = self.GLp[seg]
            xt, hb, st = xts[it % 2], hbs[it % 2], stg[it % 2]
            self.LOAD(xt[:, :, :N], self.cm(Xb)[:, :, t0:t0 + N], Xb, xt)
            self.modulate(xt, hb, N, l, 0, seg)
            for c in range(KC):
                pa = self.ps()
                pg = self.ps()
                for k in range(KC):
                    self.MM(pa[:, :N], w1[:, k, c * 128:(c + 1) * 128], hb[:, k, :N], k == 0, k == KC - 1, [w1, hb], [pa])
                for k in range(KC):
                    self.MM(pg[:, :N], w1[:, k, D + c * 128:D + (c + 1) * 128], hb[:, k, :N], k == 0, k == KC - 1, [w1, hb], [pg])
                sg = sgs[c % 2]
                self.ACT(sg[:, :N], pg[:, :N], AF.Sigmoid, [pg, self.ppb], [sg], bias=self.ppc(f"cb1{o}", 8 + c))
                self.STT(st[:, c, :N], pa[:, :N], self.ppc(f"cb1{o}", c), sg[:, :N], ALU.add, ALU.mult,
                         [pa, sg, self.ppb], [st])
            self.STORE(self.cm(GL)[:, :, 15 + t0:15 + t0 + N], st[:, :, :N], st, GL)
        kb.phase_end()
        kb.phase_begin()
        dgs = self.pool("dg", 2, [128, CONF_K, 128], BF16)
        gts = self.pool("gt", 3, [128, 512 + 30], BF16)
        uss = self.pool("us", 3, [128, 512], F32)
        off, _ = self.pp_off[f"cdw{o}"]
        it = 0
        for c in range(KC):
            dg = dgs[c % 2]
            self.build_diag(dg, CONF_K, lambda k: self.ppt[:, off + k * 8 + c:off + k * 8 + c + 1])
            for (seg, Xb, t0, N) in self.tiles(with_ctx):
                GL, CU = self.GLp[seg], self.CU[seg]
                gt, us = gts[it % 3], uss[it % 3]
                it += 1
                self.LOAD(gt[:, :N + 30], GL.t[c * 128:(c + 1) * 128, t0:t0 + N + 30], GL, gt)
                pc = self.ps()
                for k in range(CONF_K):
                    self.MM(pc[:, :N], dg[:, k, :], gt[:, k:k + N], k == 0, k == CONF_K - 1, [dg, gt], [pc])
                self.ACT(us[:, :N], pc[:, :N], AF.Identity, [pc, self.ppb], [us], bias=self.ppc(f"cdb{o}", c))
                self.STORE(CU.t[c * 128:(c + 1) * 128, t0:t0 + N], us[:, :N], us, CU)
        kb.phase_end()
        kb.phase_begin()
        w2 = kb.sbuf("w2", [128, KC, D], BF16)
        self.load_w(w2, self.in_conf_w2, self.in_conf_w2[o], KC, D)
        xts = self.pool("xt", 2, [128, KC, 512], F32)
        uts = self.pool("ut", 2, [128, KC, 512], F32)
        ub = kb.sbuf("ub", [128, KC, 512], BF16)
        rt = kb.sbuf("rt", [128, KC, 512], F32)
        self.rtmp = self.pool("rtmp", 2, [128, 512], F32)
        self.rti = 0
        for it, (seg, Xb, t0, N) in enumerate(self.tiles(with_ctx)):
            CU = self.CU[seg]
            xt, ut = xts[it % 2], uts[it % 2]
            self.LOAD(ut[:, :, :N], self.cm(CU)[:, :, t0:t0 + N], CU, ut)
            self.LOAD(xt[:, :, :N], self.cm(Xb)[:, :, t0:t0 + N], Xb, xt)
            self.ln_tile(ut, N, f"clg{o}", f"clb{o}", AF.Silu, ub)
            for c in range(KC):
                py = self.ps()
                for k in range(KC):
                    self.MM(py[:, :N], w2[:, k, c * 128:(c + 1) * 128], ub[:, k, :N], k == 0, k == KC - 1, [w2, ub], [py])
                self.residual(py, c, xt, rt, N, l, 2, seg, bias_ap=self.ppc(f"cb2{o}", c))
            self.ln_tile(rt, N, f"plg{l}_0", f"plb{l}_0", AF.Identity, xt)
            self.STORE(self.cm(Xb)[:, :, t0:t0 + N], xt[:, :, :N], xt, Xb)
        kb.phase_end()

    def ffn(self, l, with_ctx, final=False):
        kb = self.kb
        for half in range(2):
            kb.phase_begin()
            wu = kb.sbuf("wu", [128, KC, D_FF], BF16)
            self.load_w(wu, self.in_ffn_up, self.in_ffn_up[l][:, half * D_FF:(half + 1) * D_FF], KC, D_FF)
            xts = self.pool("xt", 2, [128, KC, 512], F32)
            hbs = self.pool("hb", 2, [128, KC, 512], BF16)
            stg2 = self.pool("stg2", 3, [128, 2, 512], BF16)
            s2i = 0
            for it, (seg, Xb, t0, N) in enumerate(self.tiles(with_ctx)):
                dst = (self.FV, self.FG)[half][seg]
                xt, hb = xts[it % 2], hbs[it % 2]
                self.LOAD(xt[:, :, :N], self.cm(Xb)[:, :, t0:t0 + N], Xb, xt)
                self.modulate(xt, hb, N, l, 1, seg)
                for g0 in range(0, FC, 2):
                    st = stg2[s2i % 3]
                    s2i += 1
                    for cc in range(2):
                        c = g0 + cc
                        pa = self.ps()
                        for k in range(KC):
                            self.MM(pa[:, :N], wu[:, k, c * 128:(c + 1) * 128], hb[:, k, :N], k == 0, k == KC - 1, [wu, hb], [pa])
                        if cc == 0:
                            self.ACT(st[:, cc, :N], pa[:, :N], AF.Copy, [pa], [st])
                        else:
                            self.COPY(kb.dve, st[:, cc, :N], pa[:, :N], [pa], [st])
                    self.STORE(self.cm(dst)[:, g0:g0 + 2, t0:t0 + N], st[:, :, :N], st, dst)
            kb.phase_end()
        kb.phase_begin()
        dgs = self.pool("dg", 2, [128, 18, 128], BF16)
        fgs = self.pool("fg", 3, [128, 2, 640], BF16)
        vts = self.pool("vt", 3, [128, 2, 512], BF16)
        stg2 = self.pool("stg2", 3, [128, 2, 512], BF16)
        ges = self.pool("ge", 2, [128, 512], F32)
        off, _ = self.pp_off[f"fcw{l}"]
        it = 0
        gi = 0
        for g0 in range(0, FC, 2):
            dg = dgs[(g0 // 2) % 2]
            self.build_diag(dg, 18, lambda kk: self.ppt[:, off + (kk % 9) * FC + g0 + kk // 9:off + (kk % 9) * FC + g0 + kk // 9 + 1])
            for seg, Xb, Tn in self.segs(with_ctx):
                FV, FG, U = self.FV[seg], self.FG[seg], self.U[seg]
                if seg == 0:
                    W, R = GRID_W, Tn // GRID_W
                    rows_per = 512 // W
                else:
                    W, R = Tn, 1
                    rows_per = 1
                for i in range((R + rows_per - 1) // rows_per):
                    r0 = i * rows_per
                    nr = min(rows_per, R - r0)
                    lo = max(r0 - 1, 0)
                    hi = min(r0 + nr + 1, R)
                    N = nr * W
                    fg, vt, st = fgs[it % 3], vts[it % 3], stg2[it % 3]
                    it += 1
                    self.LOAD(fg[:, :, :(hi - lo) * W], self.cm(FG)[:, g0:g0 + 2, lo * W:hi * W], FG, fg)
                    self.LOAD(vt[:, :, :N], self.cm(FV)[:, g0:g0 + 2, r0 * W:r0 * W + N], FV, vt)
                    for cc in range(2):
                        c = g0 + cc
                        pc = self.ps()
                        pcv = pc[:, :N].rearrange("p (r w) -> p r w", w=W)
                        fgv = fg[:, cc, :(hi - lo) * W].rearrange("p (r w) -> p r w", w=W)
                        taps = [(0, 0)] + [(dr, dc) for dr in (-1, 0, 1) for dc in (-1, 0, 1) if (dr, dc) != (0, 0)]
                        valid = []
                        for dr, dc in taps:
                            j0 = max(0, -(r0 + dr))
                            j1 = min(nr, R - r0 - dr)
                            if j1 > j0:
                                valid.append((dr, dc, j0, j1))
                        for ti, (dr, dc, j0, j1) in enumerate(valid):
                            k = (dr + 1) * 3 + (dc + 1)
                            oc0, oc1 = max(0, -dc), W - max(0, dc)
                            sr0 = r0 + j0 + dr - lo
                            self.MM(pcv[:, j0:j1, oc0:oc1], dg[:, cc * 9 + k, :],
                                    fgv[:, sr0:sr0 + (j1 - j0), oc0 + dc:oc1 + dc],
                                    ti == 0, ti == len(valid) - 1, [dg, fg], [pc])
                        ge = ges[gi % 2]
                        gi += 1
                        self.ACT(ge[:, :N], pc[:, :N], AF.Gelu, [pc, self.ppb], [ge], bias=self.ppc(f"fcb{l}", c))
                        self.TT(kb.dve, st[:, cc, :N], ge[:, :N], vt[:, cc, :N], ALU.mult, [ge, vt], [st])
                    self.STORE(self.cm(U)[:, g0:g0 + 2, r0 * W:r0 * W + N], st[:, :, :N], st, U)
        kb.phase_end()
        kb.phase_begin()
        wd = kb.sbuf("wd", [128, FC, D], BF16)
        self.load_w(wd, self.in_ffn_down, self.in_ffn_down[l], FC, D, blk=128)
        xts = self.pool("xt", 2, [128, KC, 512], F32)
        ut = kb.sbuf("ut", [128, FC, 512], BF16)
        rt = kb.sbuf("rt", [128, KC, 512], F32)
        self.rtmp = self.pool("rtmp", 2, [128, 512], F32)
        self.rti = 0
        for it, (seg, Xb, t0, N) in enumerate(self.tiles(with_ctx)):
            U = self.U[seg]
            xt = xts[it % 2]
            self.LOAD(ut[:, :, :N], self.cm(U)[:, :, t0:t0 + N], U, ut)
            self.LOAD(xt[:, :, :N], self.cm(Xb)[:, :, t0:t0 + N], Xb, xt)
            for c in range(KC):
                py = self.ps()
                for k in range(FC):
                    self.MM(py[:, :N], wd[:, k, c * 128:(c + 1) * 128], ut[:, k, :N], k == 0, k == FC - 1, [wd, ut], [py])
                self.residual(py, c, xt, rt, N, l, 5, seg)
            self.ln_tile(rt, N, f"plg{l}_1", f"plb{l}_1", AF.Identity, xt)
            dstb = self.out_y if (final and seg == 0) else Xb
            self.last_tok = self.STORE(self.cm(dstb)[:, :, t0:t0 + N], xt[:, :, :N], xt, dstb)
        kb.phase_end()

    def chunk_order(self, d):
        out = []
        for seg, Tn in ((1, self.M), (0, self.T)):
            idx = list(range(Tn // 128))
            if d == 1:
                idx = idx[::-1]
            out += [(seg, i * 128) for i in idx]
        return out

    def even_mixer(self, l, ctx_out):
        kb = self.kb
        e = l // 2
        win = self.in_mix_in[e]
        lbv, oml, noml = self.lbv[:, e, :], self.oml[:, e, :], self.noml[:, e, :]
        all_tiles = self.tiles(True)
        kb.phase_begin()
        w = kb.sbuf("wA", [128, KC, 3584], BF16)
        self.load_w(w[:, :, 0:2560], self.in_mix_in, win[:, 0:2560], KC, 2560, dstbuf=w)
        self.load_w(w[:, :, 2560:3584], self.in_mix_in, win[:, 2592:3616], KC, 1024, dstbuf=w)
        xts = self.pool("xt", 2, [128, KC, 512], F32)
        hbs = self.pool("hb", 2, [128, KC, 512], BF16)
        st8 = self.pool("st8", 2, [128, KC, 512], BF16)
        st12 = kb.sbuf("st12", [128, 12, 512], BF16)
        n8 = 0
        for it, (seg, Xb, t0, N) in enumerate(all_tiles):
            xt, hb = xts[it % 2], hbs[it % 2]
            self.LOAD(xt[:, :, :N], self.cm(Xb)[:, :, t0:t0 + N], Xb, xt)
            self.modulate(xt, hb, N, l, 0, seg)
            for which, dst in ((0, self.ZS[seg]), (1, self.QS[seg])):
                st = st8[n8 % 2]
                n8 += 1
                for c in range(KC):
                    col = (0 if which == 0 else 2560) + c * 128
                    pa = self.ps()
                    for k in range(KC):
                        self.MM(pa[:, :N], w[:, k, col:col + 128], hb[:, k, :N], k == 0, k == KC - 1, [w, hb], [pa])
                    self.ACT(st[:, c, :N], pa[:, :N], AF.Silu, [pa], [st])
                self.STORE(self.cm(dst)[:, :, t0:t0 + N], st[:, :, :N], st, dst)
            for c in range(12):
                col = 1024 + c * 128
                pa = self.ps()
                for k in range(KC):
                    self.MM(pa[:, :N], w[:, k, col:col + 128], hb[:, k, :N], k == 0, k == KC - 1, [w, hb], [pa])
                if c % 2 == 0:
                    self.COPY(kb.dve, st12[:, c, :N], pa[:, :N], [pa], [st12])
                else:
                    self.ACT(st12[:, c, :N], pa[:, :N], AF.Copy, [pa], [st12])
            self.STORE(self.cm(self.XBC[seg])[:, :, 2 + t0:2 + t0 + N], st12[:, :, :N], st12, self.XBC[seg])
        kb.phase_end()
        kb.phase_begin()
        w = kb.sbuf("wB", [128, KC, 2048], BF16)
        self.load_w(w, self.in_mix_in, win[:, 3616:5664], KC, 2048)
        xts = self.pool("xt", 2, [128, KC, 512], F32)
        hbs = self.pool("hb", 2, [128, KC, 512], BF16)
        sgs = self.pool("sg", 2, [128, 512], F32)
        tls = self.pool("tl", 2, [128, 512], F32)
        lfs = self.pool("lf", 3, [128, 512], F32)
        kks = self.pool("kk", 3, [128, 512], BF16)
        n = 0
        for it, (seg, Xb, t0, N) in enumerate(all_tiles):
            xt, hb = xts[it % 2], hbs[it % 2]
            self.LOAD(xt[:, :, :N], self.cm(Xb)[:, :, t0:t0 + N], Xb, xt)
            self.modulate(xt, hb, N, l, 0, seg)
            for d in range(2):
                for h in range(8):
                    col = d * 1024 + h * 128
                    pa = self.ps()
                    for k in range(KC):
                        self.MM(pa[:, :N], w[:, k, col:col + 128], hb[:, k, :N], k == 0, k == KC - 1, [w, hb], [pa])
                    sg, tl, lf, kk = sgs[n % 2], tls[n % 2], lfs[n % 3], kks[n % 3]
                    n += 1
                    self.ACT(sg[:, :N], pa[:, :N], AF.Sigmoid, [pa], [sg])
                    self.TS(kb.dve, tl[:, :N], sg[:, :N], oml[:, h:h + 1], lbv[:, h:h + 1], ALU.mult, ALU.add, [sg, self.lbb], [tl])
                    self.ACT(lf[:, :N], tl[:, :N], AF.Ln, [tl], [lf])
                    self.TS(kb.dve, kk[:, :N], sg[:, :N], noml[:, h:h + 1], oml[:, h:h + 1], ALU.mult, ALU.add, [sg, self.lbb], [kk])
                    self.STORE(self.LF[d][seg].t[h * 128:(h + 1) * 128, t0:t0 + N], lf[:, :N], lf, self.LF[d][seg])
                    self.STORE(self.KK[d][seg].t[h * 128:(h + 1) * 128, t0:t0 + N], kk[:, :N], kk, self.KK[d][seg])
        kb.phase_end()
        kb.phase_begin()
        w = kb.sbuf("wC", [128, KC, 2048 + 32], BF16)
        self.load_w(w[:, :, 0:2048], self.in_mix_in, win[:, 5664:7712], KC, 2048, dstbuf=w)
        self.load_w(w[:, :, 2048:2080], self.in_mix_in, win[:, 2560:2592], KC, 32, dstbuf=w)
        xts = self.pool("xt", 2, [128, KC, 512], F32)
        hbs = self.pool("hb", 2, [128, KC, 512], BF16)
        st8 = self.pool("st8", 2, [128, KC, 512], BF16)
        stv = self.pool("stv", 2, [128, 1024], BF16)
        dts = self.pool("dts", 2, [128, 4, 2, 32], F32)
        x1s = self.pool("x1", 2, [128, 32], F32)
        nv = 0
        for it, (seg, Xb, t0, N) in enumerate(all_tiles):
            xt, hb = xts[it % 2], hbs[it % 2]
            self.LOAD(xt[:, :, :N], self.cm(Xb)[:, :, t0:t0 + N], Xb, xt)
            self.modulate(xt, hb, N, l, 0, seg)
            dtt = dts[it % 2]
            for sub in range(N // 128):
                ts = slice(sub * 128, (sub + 1) * 128)
                pdt = self.ps()
                for k in range(KC):
                    self.MM(pdt[:, 0:32], hb[:, k, ts], w[:, k, 2048:2080], k == 0, k == KC - 1, [hb, w], [pdt])
                x1 = x1s[sub % 2]
                self.TT(kb.dve, x1[:], pdt[:, 0:32], self.tmb[:, 0, e * 32:(e + 1) * 32], ALU.add, [pdt, self.tmb], [x1])
                self.ACT(x1[:], x1[:], AF.Exp, [x1], [x1])
                self.ACT(dtt[:, sub, 1, :], x1[:], AF.Ln, [x1, self.eps_t], [dtt], bias=self.one_ap)
                self.TT(kb.dve, dtt[:, sub, 0, :], dtt[:, sub, 1, :], self.acoef[:, e, :], ALU.mult, [dtt, self.acb], [dtt])
                sv = stv[nv % 2]
                nv += 1
                for j in range(2):
                    pv = self.ps()
                    for k in range(KC):
                        self.MM(pv[:, :512], hb[:, k, ts], w[:, k, j * 512:(j + 1) * 512], k == 0, k == KC - 1, [hb, w], [pv])
                    if j == 0:
                        self.COPY(kb.dve, sv[:, 0:512], pv[:, :512], [pv], [sv])
                    else:
                        self.ACT(sv[:, 512:1024], pv[:, :512], AF.Copy, [pv], [sv])
                self.STORE(self.V[seg].t[t0 + sub * 128:t0 + (sub + 1) * 128, :], sv[:], sv, self.V[seg])
            ns = N // 128
            self.STORE(self.ADT[seg].t[t0:t0 + N].rearrange("(s p) a c -> p s a c", p=128), dtt[:, :ns], dtt, self.ADT[seg])
            st = st8[it % 2]
            for c in range(KC):
                col = 1024 + c * 128
                pa = self.ps()
                for k in range(KC):
                    self.MM(pa[:, :N], w[:, k, col:col + 128], hb[:, k, :N], k == 0, k == KC - 1, [w, hb], [pa])
                self.ACT(st[:, c, :N], pa[:, :N], AF.Silu, [pa], [st])
            self.STORE(self.cm(self.GS[seg])[:, :, t0:t0 + N], st[:, :, :N], st, self.GS[seg])
        kb.phase_end()
        kb.phase_begin()
        xins = self.pool("xin", 3, [128, 516], BF16)
        accs = self.pool("acc", 2, [128, 512], F32)
        xos = self.pool("xo", 3, [128, 512], BF16)
        tms = self.pool("tm", 3, [128, 4, 128], BF16)
        woff, _ = self.pp_off[f"scw{e}"]
        n = 0
        for c in range(12):
            for (seg, Xb, t0, N) in all_tiles:
                xin, acc, xo, tm = xins[n % 3], accs[n % 2], xos[n % 3], tms[n % 3]
                n += 1
                self.LOAD(xin[:, :N + 4], self.XBC[seg].t[c * 128:(c + 1) * 128, t0:t0 + N + 4], self.XBC[seg], xin)
                self.TS(kb.dve, acc[:, :N], xin[:, 0:N], self.ppt[:, woff + c:woff + c + 1], None, ALU.mult, None, [xin, self.ppb], [acc])
                for k in range(1, 5):
                    self.STT(acc[:, :N], xin[:, k:k + N], self.ppt[:, woff + k * 12 + c:woff + k * 12 + c + 1], acc[:, :N],
                             ALU.mult, ALU.add, [xin, acc, self.ppb], [acc])
                self.ACT(xo[:, :N], acc[:, :N], AF.Silu, [acc, self.ppb], [xo], bias=self.ppc(f"scb{e}", c))
                if c < 8:
                    self.STORE(self.XSC[seg].t[c * 128:(c + 1) * 128, t0:t0 + N], xo[:, :N], xo, self.XSC[seg])
                elif c < 10:
                    self.STORE(self.BT[seg].t[(c - 8) * 128:(c - 7) * 128, t0:t0 + N], xo[:, :N], xo, self.BT[seg])
                else:
                    self.STORE(self.CT[seg].t[(c - 10) * 128:(c - 9) * 128, t0:t0 + N], xo[:, :N], xo, self.CT[seg])
                if c < 10:
                    pt = self.ps()
                    ns = N // 128
                    for sub in range(ns):
                        self.MM(pt[:, sub * 128:(sub + 1) * 128], xo[:, sub * 128:(sub + 1) * 128], self.ident_b, True, True,
                                [xo, self.cst_b], [pt])
                    self.ACT(tm[:, :ns, :], pt[:, :N].rearrange("p (s c) -> p s c", c=128), AF.Copy, [pt], [tm])
                    if c < 8:
                        dst, cc, wd_ = self.XST[seg], c, D
                    else:
                        dst, cc, wd_ = self.BTM[seg], c - 8, 256
                    self.STORE(dst.t[t0:t0 + N, cc * 128:(cc + 1) * 128].rearrange("(s p) c -> p s c", p=128), tm[:, :ns, :], tm, dst)
        kb.phase_end()
        self.ssd_scan(e)
        self.gla_scan(e)
        self.merge(l, e, ctx_out)

    def ssd_scan(self, e):
        kb = self.kb
        pb = self.pb
        kb.phase_begin()
        HT = kb.sbuf("HT", [128, 2, 512], F32)
        Hb = kb.sbuf("Hb", [128, 2, 512], BF16)
        adts = self.pool("adt", 2, [128, 2, 32], F32)
        xss = self.pool("xs", 2, [128, 1024], BF16)
        bts = self.pool("bt", 2, [128, 2, 128], BF16)
        cts = self.pool("ct", 2, [128, 2, 128], BF16)
        btms = self.pool("btm", 2, [128, 256], BF16)
        sm = self.pool("sm", 2, [128, 4, 16], F32)
        Yt = kb.sbuf("Yt", [128, 16, 128], F32)
        GM = kb.sbuf("GM", [128, 2, 128], F32)
        Es = self.pool("E", 2, [128, 128], F32)
        Ers = self.pool("Er", 2, [128, 128], F32)
        Mhs = self.pool("Mh", 4, [128, 128], BF16)
        Css = self.pool("Cs", 4, [128, 128], BF16)
        xdt = kb.sbuf("xdt", [128, 1024], BF16)
        xdtw = kb.sbuf("xdtw", [128, 1024], BF16)
        ysts = self.pool("yst", 2, [128, KC, 128], F32)
        ones_f = self.cst_f[:, 4, :]
        it = 0
        for d in range(2):
            kb.op(kb.dve, lambda en: en.memset(HT[:], 0.0), [], [HT])
            kb.op(kb.dve, lambda en: en.memset(Hb[:], 0.0), [], [Hb])
            Tm = self.cst_f[:, 2 + d, :]
            for (seg, t0) in self.chunk_order(d):
                adt, xs, bt, ct, btm, smt, yst = adts[it % 2], xss[it % 2], bts[it % 2], cts[it % 2], btms[it % 2], sm[it % 2], ysts[it % 2]
                it += 1
                self.LOAD(adt[:], self.ADT[seg].t[t0:t0 + 128], self.ADT[seg], adt)
                self.LOAD(xs[:], self.XST[seg].t[t0:t0 + 128, :], self.XST[seg], xs)
                self.LOAD(bt[:], self.BT[seg].t.rearrange("(g n) t -> n g t", n=128)[:, :, t0:t0 + 128], self.BT[seg], bt)
                self.LOAD(ct[:], self.CT[seg].t.rearrange("(g n) t -> n g t", n=128)[:, :, t0:t0 + 128], self.CT[seg], ct)
                self.LOAD(btm[:], self.BTM[seg].t[t0:t0 + 128, :], self.BTM[seg], btm)
                a = adt[:, 0, d * 16:(d + 1) * 16]
                dtv = adt[:, 1, d * 16:(d + 1) * 16]
                pc = pb[0]
                self.MM(pc[:, 0:16], Tm, a, True, True, [self.cst_f, adt], [pc])
                self.MM(pc[:, 16:32], ones_f, a, True, True, [self.cst_f, adt], [pc])
                negc, tw, etot = smt[:, 0, :], smt[:, 1, :], smt[:, 2, :]
                self.TS(kb.dve, negc, pc[:, 0:16], -1.0, None, ALU.mult, None, [pc], [smt])
                self.TT(kb.dve, tw, pc[:, 16:32], negc, ALU.add, [pc, smt], [smt])
                self.ACT(tw, tw, AF.Exp, [smt], [smt])
                self.ACT(etot, pc[:, 16:32], AF.Exp, [pc], [smt])
                self.TT(kb.pool, Yt[:], Tm.unsqueeze(1).to_broadcast([128, 16, 128]),
                        a.unsqueeze(2).to_broadcast([128, 16, 128]), ALU.mult, [self.cst_f, adt], [Yt])
                for g in range(2):
                    self.MM(pb[3][:, g * 128:(g + 1) * 128], bt[:, g, :], ct[:, g, :], True, True, [bt, ct], [pb[3]])
                self.TT(kb.dve, GM[:], pb[3][:, 0:256].rearrange("p (g l) -> p g l", g=2),
                        Tm.unsqueeze(1).to_broadcast([128, 2, 128]), ALU.mult, [pb[3], self.cst_f], [GM])
                self.TT(kb.dve, xdt[:].rearrange("p (h q) -> p h q", q=64), xs[:].rearrange("p (h q) -> p h q", q=64),
                        dtv.unsqueeze(2).to_broadcast([128, 16, 64]), ALU.mult, [xs, adt], [xdt])
                self.TT(kb.pool, xdtw[:].rearrange("p (h q) -> p h q", q=64), xdt[:].rearrange("p (h q) -> p h q", q=64),
                        tw.unsqueeze(2).to_broadcast([128, 16, 64]), ALU.mult, [xdt, smt], [xdtw])
                for j in range(4):
                    prc = pb[1 + j % 2]
                    self.MM(prc[:, :512], ones_f, Yt[:, 4 * j:4 * j + 4, :].rearrange("p h l -> p (h l)"), True, True,
                            [self.cst_f, Yt], [prc])
                    for hh in range(4):
                        h = 4 * j + hh
                        g = h // 8
                        rc = prc[:, hh * 128:(hh + 1) * 128]
                        E, Er, Mh, Cs = Es[h % 2], Ers[h % 2], Mhs[h % 4], Css[h % 4]
                        self.ACT(E[:], rc, AF.Exp, [prc, smt], [E], bias=negc[:, h:h + 1])
                        self.STT(Mh[:], E[:], 1.0, GM[:, g, :], ALU.min, ALU.mult, [E, GM], [Mh])
                        self.ACT(Er[:], rc, AF.Exp, [prc], [Er])
                        self.TT(kb.dve, Cs[:], ct[:, g, :], Er[:], ALU.mult, [ct, Er], [Cs])
                        c = h // 2
                        po = (h % 2) * 64
                        yb = pb[4 + c // 4]
                        yc = (c % 4) * 128
                        self.MM(yb[po:po + 64, yc:yc + 128], xdt[:, h * 64:(h + 1) * 64], Mh[:], True, False, [xdt, Mh], [yb])
                        self.MM(yb[po:po + 64, yc:yc + 128], Hb[:, g, (h % 8) * 64:(h % 8 + 1) * 64], Cs[:], False, True, [Hb, Cs], [yb])
                self.ACT(yst[:, 0:4, :], pb[4][:, :512].rearrange("p (c l) -> p c l", c=4), AF.Copy, [pb[4]], [yst])
                self.COPY(kb.dve, yst[:, 4:8, :], pb[5][:, :512].rearrange("p (c l) -> p c l", c=4), [pb[5]], [yst])
                self.STORE(self.cm(self.Y[d][seg])[:, :, t0:t0 + 128], yst[:], yst, self.Y[d][seg])
                for g in range(2):
                    self.MM(pb[6 + g][:, :512], btm[:, g * 128:(g + 1) * 128], xdtw[:, g * 512:(g + 1) * 512], True, True,
                            [btm, xdtw], [pb[6 + g]])
                    self.TT(kb.dve, HT[:, g, :].rearrange("p (h q) -> p h q", q=64), HT[:, g, :].rearrange("p (h q) -> p h q", q=64),
                            etot[:, g * 8:(g + 1) * 8].unsqueeze(2).to_broadcast([128, 8, 64]), ALU.mult, [HT, smt], [HT])
                    self.TT(kb.dve, HT[:, g, :], HT[:, g, :], pb[6 + g][:, :512], ALU.add, [HT, pb[6 + g]], [HT])
                self.ACT(Hb[:], HT[:], AF.Copy, [HT], [Hb])
        kb.phase_end()

    def gla_scan(self, e):
        kb = self.kb
        pb = self.pb
        kb.phase_begin()
        S = kb.sbuf("S", [128, 8, 128], F32)
        Sb = kb.sbuf("Sb", [128, 8, 128], BF16)
        zb = kb.sbuf("zb", [128, 128], BF16)
        kb.op(kb.dve, lambda en: en.memset(zb[:], 0.0), [], [zb])
        rp = kb.sbuf("rp", [128, 1024], F32)
        self.LOAD(rp[:], self.in_rp[:], self.in_rp, rp)
        qs = self.pool("q", 2, [128, 8, 128], BF16)
        lfs = self.pool("lf", 2, [128, 8, 128], F32)
        kks = self.pool("kk", 2, [128, 8, 128], BF16)
        vs = self.pool("v", 2, [128, 1024], BF16)
        F = kb.sbuf("F", [128, 8, 128], F32)
        G = kb.sbuf("G", [128, 8, 128], F32)
        D1 = kb.sbuf("D1", [128, 8, 128], F32)
        D2 = kb.sbuf("D2", [128, 8, 128], F32)
        Ex = self.pool("Ex", 2, [128, 8, 128], F32)
        qh = kb.sbuf("qh", [128, 8, 128], BF16)
        kh = kb.sbuf("kh", [128, 8, 128], BF16)
        qt = kb.sbuf("qt", [128, 8, 128], BF16)
        kt = kb.sbuf("kt", [128, 8, 128], BF16)
        ktm = kb.sbuf("ktm", [128, 1024], BF16)
        ktmz = kb.sbuf("ktmz", [128, 1024], BF16)
        AM = kb.sbuf("AM", [128, 8, 128], BF16)
        dd = kb.sbuf("dd", [128, 8, 4], F32)
        osts = self.pool("ost", 2, [128, 8, 128], F32)

        def v4(b):
            return b[:].rearrange("p h (s j) -> p h s j", j=32)

        def fl(b):
            return b[:].rearrange("p h t -> p (h t)")
        it = 0
        for d in range(2):
            kb.op(kb.dve, lambda en: en.memset(S[:], 0.0), [], [S])
            kb.op(kb.dve, lambda en: en.memset(Sb[:], 0.0), [], [Sb])
            mask = self.cst_f[:, 5 + d, :]
            for (seg, t0) in self.chunk_order(d):
                q, lf, kk, v, ost = qs[it % 2], lfs[it % 2], kks[it % 2], vs[it % 2], osts[it % 2]
                it += 1
                self.LOAD(q[:], self.cm(self.QS[seg])[:, :, t0:t0 + 128], self.QS[seg], q)
                self.LOAD(lf[:], self.cm(self.LF[d][seg])[:, :, t0:t0 + 128], self.LF[d][seg], lf)
                self.LOAD(kk[:], self.cm(self.KK[d][seg])[:, :, t0:t0 + 128], self.KK[d][seg], kk)
                self.LOAD(v[:], self.V[seg].t[t0:t0 + 128, :], self.V[seg], v)
                kb.op(kb.dve, lambda en, lf=lf: en.tensor_tensor_scan(out=fl(F), data0=rp[:], data1=fl(lf), initial=0.0,
                                                                      op0=ALU.mult, op1=ALU.add), [rp, lf], [F])
                Fl = v4(F)[:, :, :, 31:32]
                if d == 0:
                    Gb = F
                else:
                    self.TT(kb.pool, G[:], lf[:], F[:], ALU.subtract, [lf, F], [G])
                    self.TT(kb.dve, v4(G), v4(G), Fl.to_broadcast([128, 8, 4, 32]), ALU.add, [G, F], [G])
                    Gb = G
                self.TT(kb.dve, v4(D1), v4(Gb), v4(Gb)[:, :, :, 16:17].to_broadcast([128, 8, 4, 32]), ALU.subtract, [Gb], [D1])
                self.TS(kb.pool, D1[:], D1[:], 40.0, -40.0, ALU.min, ALU.max, [D1], [D1])
                self.TT(kb.dve, v4(D2), Fl.to_broadcast([128, 8, 4, 32]), v4(Gb), ALU.subtract, [F, Gb], [D2])
                E1, E2 = Ex[0], Ex[1]
                self.ACT(E1[:], D1[:], AF.Exp, [D1], [E1])
                self.TT(kb.dve, qh[:], q[:], E1[:], ALU.mult, [q, E1], [qh])
                self.ACT(E2[:], D1[:], AF.Exp, [D1], [E2], scale=-1.0)
                self.TT(kb.pool, kh[:], kk[:], E2[:], ALU.mult, [kk, E2], [kh])
                self.ACT(E1[:], Gb[:], AF.Exp, [Gb], [E1])
                self.TT(kb.dve, qt[:], q[:], E1[:], ALU.mult, [q, E1], [qt])
                self.ACT(E2[:], D2[:], AF.Exp, [D2], [E2])
                self.TT(kb.pool, kt[:], kk[:], E2[:], ALU.mult, [kk, E2], [kt])
                self.ACT(dd[:], v4(F)[:, :, :, 31], AF.Exp, [F], [dd])
                for h in range(8):
                    self.MM(pb[6 + h // 4][:, (h % 4) * 128:(h % 4 + 1) * 128], kt[:, h, :], self.ident_b, True, True,
                            [kt, self.cst_b], [pb[6 + h // 4]])
                self.ACT(ktm[:, 0:512], pb[6][:, :512], AF.Copy, [pb[6]], [ktm])
                self.COPY(kb.dve, ktm[:, 512:1024], pb[7][:, :512], [pb[7]], [ktm])
                self.ACT(ktmz[64:128, 0:512], pb[6][64:128, :512], AF.Copy, [pb[6]], [ktmz])
                self.COPY(kb.dve, ktmz[64:128, 512:1024], pb[7][64:128, :512], [pb[7]], [ktmz])
                kb.op(kb.pool, lambda en: en.memset(ktmz[64:96, :], 0.0), [], [ktmz])
                for h in range(8):
                    self.MM(pb[h // 4][:, (h % 4) * 128:(h % 4 + 1) * 128], kh[:, h, :], qh[:, h, :], True, True, [kh, qh], [pb[h // 4]])
                for half in range(2):
                    self.TT(kb.dve, AM[:, 4 * half:4 * half + 4, :], pb[half][:, :512].rearrange("p (h t) -> p h t", h=4),
                            mask.unsqueeze(1).to_broadcast([128, 4, 128]), ALU.mult, [pb[half], self.cst_f], [AM])
                for half in range(2):
                    self.MM(pb[2 + half][:, :512], zb[:], AM[:, 4 * half:4 * half + 4, :].rearrange("p h t -> p (h t)"), True, False,
                            [zb, AM], [pb[2 + half]])
                for h in range(8):
                    self.MM(pb[2 + h // 4][:, (h % 4) * 128:(h % 4 + 1) * 128], v[:, h * 128:(h + 1) * 128], AM[:, h, :], False, False,
                            [v, AM], [pb[2 + h // 4]])
                subs = [0, 1, 2, 3] if d == 0 else [3, 2, 1, 0]
                for si, i in enumerate(subs):
                    for h in range(8):
                        c0 = (h % 4) * 128 + 32 * i
                        self.MM(pb[2 + h // 4][:, c0:c0 + 32], Sb[:, h, :], qt[:, h, 32 * i:32 * i + 32], False, si == 3,
                                [Sb, qt], [pb[2 + h // 4]])
                    for h in range(8):
                        if i < 3:
                            kl, vr = ktm[32 * i:32 * i + 32, h * 128:(h + 1) * 128], v[32 * i:32 * i + 32, h * 128:(h + 1) * 128]
                        else:
                            kl, vr = ktmz[64:128, h * 128:(h + 1) * 128], v[64:128, h * 128:(h + 1) * 128]
                        self.MM(pb[4 + h // 4][:, (h % 4) * 128:(h % 4 + 1) * 128], kl, vr, True, True, [ktm, ktmz, v], [pb[4 + h // 4]])
                    for h in range(8):
                        self.STT(S[:, h, :], S[:, h, :], dd[:, h, i:i + 1], pb[4 + h // 4][:, (h % 4) * 128:(h % 4 + 1) * 128],
                                 ALU.mult, ALU.add, [S, dd, pb[4 + h // 4]], [S])
                    self.ACT(Sb[:], S[:], AF.Copy, [S], [Sb])
                self.ACT(ost[:, 0:4, :], pb[2][:, :512].rearrange("p (h t) -> p h t", h=4), AF.Copy, [pb[2]], [ost])
                self.COPY(kb.dve, ost[:, 4:8, :], pb[3][:, :512].rearrange("p (h t) -> p h t", h=4), [pb[3]], [ost])
                self.STORE(self.cm(self.O[d][seg])[:, :, t0:t0 + 128], ost[:], ost, self.O[d][seg])
        kb.phase_end()

    def merge(self, l, e, ctx_out):
        kb = self.kb
        kb.phase_begin()
        wo = kb.sbuf("wo", [128, 16, D], BF16)
        self.load_w(wo, self.in_mix_out, self.in_mix_out[e], 16, D, blk=128)
        NT = 256
        y0 = kb.sbuf("y0", [128, KC, NT], F32)
        y1 = kb.sbuf("y1", [128, KC, NT], F32)
        o0 = kb.sbuf("o0", [128, KC, NT], F32)
        o1 = kb.sbuf("o1", [128, KC, NT], F32)
        xs = kb.sbuf("xsc", [128, KC, NT], BF16)
        zs = kb.sbuf("zsc", [128, KC, NT], BF16)
        gs = kb.sbuf("gsc", [128, KC, NT], BF16)
        xts = self.pool("xt", 2, [128, KC, NT], F32)
        sq = kb.sbuf("sq", [128, KC, NT], BF16)
        cat = kb.sbuf("cat", [128, 16, NT], BF16)
        rstd = kb.sbuf("rstd", [128, 2, NT], F32)
        tmp = self.pool("mt", 2, [128, NT], F32)
        rt = kb.sbuf("rt", [128, KC, NT], F32)
        self.rtmp = self.pool("rtmp", 2, [128, 512], F32)
        self.rti = 0
        tl = []
        for seg, Xb, Tn in self.segs(ctx_out):
            for i in range(Tn // NT):
                tl.append((seg, Xb, i * NT, NT))
        for it, (seg, Xb, t0, N) in enumerate(tl):
            xt = xts[it % 2]
            sl = slice(t0, t0 + N)
            self.LOAD(y0[:], self.cm(self.Y[0][seg])[:, :, sl], self.Y[0][seg], y0)
            self.LOAD(y1[:], self.cm(self.Y[1][seg])[:, :, sl], self.Y[1][seg], y1)
            self.LOAD(o0[:], self.cm(self.O[0][seg])[:, :, sl], self.O[0][seg], o0)
            self.LOAD(o1[:], self.cm(self.O[1][seg])[:, :, sl], self.O[1][seg], o1)
            self.LOAD(xs[:], self.cm(self.XSC[seg])[:, :, sl], self.XSC[seg], xs)
            self.LOAD(zs[:], self.cm(self.ZS[seg])[:, :, sl], self.ZS[seg], zs)
            self.LOAD(gs[:], self.cm(self.GS[seg])[:, :, sl], self.GS[seg], gs)
            self.LOAD(xt[:], self.cm(Xb)[:, :, sl], Xb, xt)
            self.TT(kb.pool, y0[:], y0[:], y1[:], ALU.add, [y0, y1], [y0])
            for c in range(KC):
                self.STT(y0[:, c, :], xs[:, c, :], self.ppc(f"sdd{e}", c), y0[:, c, :], ALU.mult, ALU.add, [xs, y0, self.ppb], [y0])
            self.TT(kb.dve, y0[:], y0[:], zs[:], ALU.mult, [y0, zs], [y0])
            self.ACT(sq[:], y0[:], AF.Square, [y0], [sq])
            pst = self.ps()
            for g in range(2):
                for c in range(4 * g, 4 * g + 4):
                    self.MM(pst[:, g * N:(g + 1) * N], self.onesm_b, sq[:, c, :], c == 4 * g, c == 4 * g + 3, [self.cst_b, sq], [pst])
            self.ACT(rstd[:], pst[:, :2 * N].rearrange("p (g n) -> p g n", g=2), AF.Ln, [pst, self.eps_t], [rstd], bias=self.eps_ap, scale=2.0)
            self.ACT(rstd[:], rstd[:], AF.Exp, [rstd], [rstd], scale=-0.5)
            for c in range(KC):
                self.STT(cat[:, c, :], y0[:, c, :], self.ppc(f"snw{e}", c), rstd[:, c // 4, :], ALU.mult, ALU.mult,
                         [y0, rstd, self.ppb], [cat])
            self.TT(kb.pool, o0[:], o0[:], o1[:], ALU.add, [o0, o1], [o0])
            self.ACT(sq[:], o0[:], AF.Square, [o0], [sq])
            for c2 in range(0, KC, 2):
                pso = self.ps()
                for j in range(2):
                    self.MM(pso[:, j * N:(j + 1) * N], self.onesm_b, sq[:, c2 + j, :], True, True, [self.cst_b, sq], [pso])
                self.ACT(rstd[:], pso[:, :2 * N].rearrange("p (g n) -> p g n", g=2), AF.Ln, [pso, self.eps_t], [rstd], bias=self.eps_ap, scale=8.0)
                self.ACT(rstd[:], rstd[:], AF.Exp, [rstd], [rstd], scale=-0.5)
                for j in range(2):
                    c = c2 + j
                    tm = tmp[j]
                    self.STT(tm[:], o0[:, c, :], self.ppc(f"hnw{e}", c), rstd[:, j, :], ALU.mult, ALU.mult, [o0, rstd, self.ppb], [tm])
                    self.TT(kb.pool, cat[:, 8 + c, :], tm[:], gs[:, c, :], ALU.mult, [tm, gs], [cat])
            for co in range(KC):
                py = self.ps()
                for k in range(16):
                    self.MM(py[:, :N], wo[:, k, co * 128:(co + 1) * 128], cat[:, k, :], k == 0, k == 15, [wo, cat], [py])
                self.residual(py, co, xt, rt, N, l, 2, seg)
            self.ln_tile(rt, N, f"plg{l}_0", f"plb{l}_0", AF.Identity, xt)
            self.STORE(self.cm(Xb)[:, :, sl], xt[:], xt, Xb)
        kb.phase_end()

    def alloc_work(self):
        kb = self.kb
        T, M = self.T, self.M
        self.eps_t = kb.sbuf("eps_t", [128, 1], F32)
        kb.op(kb.dve, lambda e: e.memset(self.eps_t[:], EPS), [], [self.eps_t])
        self.eps_ap = self.eps_t[:, 0:1]
        self.one_t = kb.sbuf("one_t", [128, 1], F32)
        kb.op(kb.dve, lambda e: e.memset(self.one_t[:], 1.0), [], [self.one_t])
        self.one_ap = self.one_t[:, 0:1]
        self.ln_alloc()
        self.lbb = kb.sbuf("lbb", [128, 3, 2, 8], F32)
        self.lbv, self.oml, self.noml = self.lbb[:, 0], self.lbb[:, 1], self.lbb[:, 2]
        kb.op(kb.dve, lambda e: e.memset(self.lbb[:], 0.0), [], [self.lbb])
        self.TT(kb.dve, self.lbb[:, 0, 1, :], self.ppc("lbr1"), self.ppc("lbr0"), ALU.subtract, [self.ppb], [self.lbb])
        self.ACT(self.lbb[:, 0, 1, :], self.lbb[:, 0, 1, :], AF.Sigmoid, [self.lbb], [self.lbb])
        self.TS(kb.dve, self.lbb[:, 1, :, :], self.lbb[:, 0, :, :], -1.0, 1.0, ALU.mult, ALU.add, [self.lbb], [self.lbb])
        self.TS(kb.dve, self.lbb[:, 2, :, :], self.lbb[:, 1, :, :], -1.0, None, ALU.mult, None, [self.lbb], [self.lbb])
        self.acb = kb.sbuf("acb", [128, 2, 32], F32)
        self.acoef = self.acb.t
        self.ACT(self.acb[:].rearrange("p e c -> p (e c)"), self.tmb[:, 1, :], AF.Exp, [self.tmb], [self.acb])
        self.TS(kb.dve, self.acb[:], self.acb[:], -1.0, None, ALU.mult, None, [self.acb], [self.acb])
        TM = (T, M)
        self.ZS = [self.scratch(f"ZS{i}", [D, TM[i]], BF16) for i in range(2)]
        self.QS = [self.scratch(f"QS{i}", [D, TM[i]], BF16) for i in range(2)]
        self.GS = [self.scratch(f"GS{i}", [D, TM[i]], BF16) for i in range(2)]
        self.XBC = [self.scratch(f"XBC{i}", [1536, TM[i] + 4], BF16) for i in range(2)]
        self.ADT = [self.scratch(f"ADT{i}", [TM[i], 2, 32], F32) for i in range(2)]
        self.LF = [[self.scratch(f"LF{d}{i}", [D, TM[i]], F32) for i in range(2)] for d in range(2)]
        self.KK = [[self.scratch(f"KK{d}{i}", [D, TM[i]], BF16) for i in range(2)] for d in range(2)]
        self.V = [self.scratch(f"V{i}", [TM[i], D], BF16) for i in range(2)]
        self.XSC = [self.scratch(f"XSC{i}", [D, TM[i]], BF16) for i in range(2)]
        self.XST = [self.scratch(f"XST{i}", [TM[i], D], BF16) for i in range(2)]
        self.BT = [self.scratch(f"BT{i}", [256, TM[i]], BF16) for i in range(2)]
        self.CT = [self.scratch(f"CT{i}", [256, TM[i]], BF16) for i in range(2)]
        self.BTM = [self.scratch(f"BTM{i}", [TM[i], 256], BF16) for i in range(2)]
        self.Y = [[self.scratch(f"Y{d}{i}", [D, TM[i]], F32) for i in range(2)] for d in range(2)]
        self.O = [[self.scratch(f"O{d}{i}", [D, TM[i]], F32) for i in range(2)] for d in range(2)]
        self.GLp = [self.scratch("GL0", [D, T + 30], BF16), self.scratch("GL1", [D, M + 30], BF16)]
        self.CU = [self.scratch("CU0", [D, T], F32), self.scratch("CU1", [D, M], F32)]
        self.FV = [self.scratch("FV0", [D_FF, T], BF16), self.scratch("FV1", [D_FF, M], BF16)]
        self.FG = [self.scratch("FG0", [D_FF, T], BF16), self.scratch("FG1", [D_FF, M], BF16)]
        self.U = [self.scratch("U0", [D_FF, T], BF16), self.scratch("U1", [D_FF, M], BF16)]

    def copy_in(self):
        kb = self.kb
        kb.phase_begin()
        xts = self.pool("xt", 2, [128, KC, 512], F32)
        it = 0
        for src, dst, Tn in ((self.in_x, self.X, self.T), (self.in_ctx, self.XC, self.M)):
            for i in range((Tn + 511) // 512):
                t0 = i * 512
                N = min(512, Tn - t0)
                xt = xts[it % 2]
                it += 1
                self.LOAD(xt[:, :, :N], self.cm(src)[:, :, t0:t0 + N], src, xt)
                self.STORE(self.cm(dst)[:, :, t0:t0 + N], xt[:, :, :N], xt, dst)
        z = kb.sbuf("zpad", [128, KC, 16], BF16)
        kb.op(kb.dve, lambda e: e.memset(z[:], 0.0), [], [z])
        z12 = kb.sbuf("zpad12", [128, 12, 2], BF16)
        kb.op(kb.dve, lambda e: e.memset(z12[:], 0.0), [], [z12])
        for seg, Tn in ((0, self.T), (1, self.M)):
            v = self.cm(self.GLp[seg])
            self.STORE(v[:, :, 0:15], z[:, :, 0:15], z, self.GLp[seg])
            self.STORE(v[:, :, 15 + Tn:30 + Tn], z[:, :, 0:15], z, self.GLp[seg])
            xv = self.cm(self.XBC[seg])
            self.STORE(xv[:, :, 0:2], z12[:, :, 0:2], z12, self.XBC[seg])
            self.STORE(xv[:, :, 2 + Tn:4 + Tn], z12[:, :, 0:2], z12, self.XBC[seg])
        kb.phase_end()

    def build(self, pp_off, npp):
        self.setup(pp_off, npp)
        self.alloc_work()
        self.ada()
        self.copy_in()
        nl = len(self.layers)
        for li, l in enumerate(self.layers):
            ctx_next = any(j % 2 == 0 for j in range(l + 1, DEPTH))
            if l % 2 == 0:
                self.even_mixer(l, ctx_next)
            else:
                self.conformer(l, ctx_next)
            self.ffn(l, ctx_next, final=(li == nl - 1))
        self.kb.finish([self.last_tok])


def build_program(T, M, layers, pp_off, npp, dbg=()):
    nc = bass.Bass("TRN2", target_bir_lowering=False)
    with ExitStack() as st:
        g = Gen(nc, st, T, M, layers, dbg)
        g.build(pp_off, npp)
    return nc, g


def make_in_maps(inp, T, M, nb):
    pp = pack_pp(inp)
    ppa = pp.pack()
    cst = make_consts()
    maps = []
    for b in range(nb):
        m = {
            "x_cm": np.ascontiguousarray(inp["x"][b].T.astype(np.float32)),
            "ctx_cm": np.ascontiguousarray(inp["ctx"][b].T.astype(np.float32)),
            "cvec": np.ascontiguousarray(np.stack([_cp(inp["c"][b]), _cp(inp["c_ctx"])], axis=2)),
            "pp": ppa,
            "cmat": cst["cmat"],
            "rp": cst["rp"],
            "tmb": np.ascontiguousarray(np.broadcast_to(np.stack([
                np.asarray(inp["ssd_dt_bias"], np.float32).reshape(64),
                np.asarray(inp["ssd_a_log"], np.float32).reshape(64)], axis=0)[None], (128, 2, 64))),
            "mix_w_in": np.asarray(inp["mix_w_in"], np.float32),
            "mix_w_out": np.asarray(inp["mix_w_out"], np.float32),
            "ada_w": np.asarray(inp["ada_w"], np.float32),
            "conf_w1": np.asarray(inp["conf_w1"], np.float32),
            "conf_w2": np.asarray(inp["conf_w2"], np.float32),
            "ffn_w_up": np.asarray(inp["ffn_w_up"], np.float32),
            "ffn_w_down": np.asarray(inp["ffn_w_down"], np.float32),
        }
        maps.append(m)
    return maps, pp


def kernel(**inputs):
    inp = {k: np.asarray(v) for k, v in inputs.items()}
    B, T, _ = inp["x"].shape
    M = inp["ctx"].shape[1]
    maps, pp = make_in_maps(inp, T, M, B)
    nc, g = build_program(T, M, list(range(DEPTH)), pp.off, pp.n)
    in_maps = [maps[i % B] for i in range(8)]
    res = run_bass_kernel_spmd(nc, in_maps, core_ids=list(range(8)))
    out = np.stack([res.results[b]["y_cm"].T for b in range(B)], axis=0)
    return np.ascontiguousarray(out.astype(np.float32))
```

```python
from contextlib import ExitStack
import numpy as np
import concourse.bass as bass
import concourse.mybir as mybir
from concourse.bass_utils import run_bass_kernel_spmd

F32 = mybir.dt.float32
BF16 = mybir.dt.bfloat16
AF = mybir.ActivationFunctionType
ALU = mybir.AluOpType

D = 1024
KC = 8
DEPTH = 4
D_FF = 2816
FC = 22
GRID_W = 64
IN_TOTAL = 7712
ALPHA = (2 * DEPTH) ** 0.25
EPS = 1e-5
CONF_K = 31
import os
SAME_SYNC = os.environ.get("SAME_SYNC", "1") == "1"


class Tok:
    __slots__ = ("sem", "value")

    def __init__(self, sem, value):
        self.sem = sem
        self.value = value


class DSem:
    def __init__(self, h):
        self.h = h
        self.count = 0


class Buf:
    def __init__(self, kb, name, t=None, dram=False):
        self.kb = kb
        self.name = name
        self.t = t
        self.dram = dram
        self.writers = []
        self.readers = []
        self.dsem = {}
        self.inflight = {}

    def __getitem__(self, idx):
        return self.t[idx]


class Eng:
    def __init__(self, name, same_sync):
        self.name = name
        self.ops = []
        self.count = 0
        self.sem = None
        self.waited = {}
        self.same_sync = same_sync


class KB:
    def __init__(self, nc, stack):
        self.nc = nc
        self.stack = stack
        self.pe = Eng("tensor", False)
        self.act = Eng("scalar", SAME_SYNC)
        self.dve = Eng("vector", SAME_SYNC)
        self.pool = Eng("gpsimd", True)
        self.sp = Eng("sync", False)
        self.engs = [self.pe, self.act, self.dve, self.pool, self.sp]
        self.nsem = 0
        self.scopes = False
        self.dsem_free = []
        self.dsem_used = []
        self.pstack = None
        self.pbufs = []
        self.dram_bufs = []
        for e in self.engs:
            e.sem = self.new_sem("c_" + e.name)
            e.h = getattr(nc, e.name)

    def phase_begin(self, name=None):
        self.pstack = ExitStack()
        self.pbufs = []
        self.nph = getattr(self, "nph", 0) + 1
        if self.scopes:
            self.pstack.enter_context(self.nc.named_scope(f"ph{self.nph:03d}_{name or ''}"))

    def phase_end(self):
        toks = [Tok(e.sem, e.count) for e in self.engs if e.count]
        toks += [Tok(d.h, d.count) for d in self.dsem_used + self.dsem_free if d.count]
        for e in self.engs:
            self._emit_waits(e, toks)
        for b in self.pbufs:
            for d in b.dsem.values():
                self.dsem_used.remove(d)
                self.dsem_free.append(d)
            b.dsem = {}
        for b in self.dram_bufs:
            b.writers = []
            b.readers = []
        self.pstack.close()
        self.pstack = None
        self.pbufs = []

    def get_dsem(self, q):
        free = [d for d in self.dsem_free if d.q == q]
        if free:
            d = free[-1]
            self.dsem_free.remove(d)
        else:
            d = DSem(self.new_sem(f"d{self.nsem}"))
            d.q = q
        self.dsem_used.append(d)
        return d

    def new_sem(self, name):
        self.nsem += 1
        return self.stack.enter_context(self.nc.semaphore(name))

    def sbuf(self, name, shape, dtype):
        st = self.pstack if self.pstack is not None else self.stack
        self.nsb = getattr(self, "nsb", 0) + 1
        t = st.enter_context(self.nc.sbuf_tensor(f"{name}_{self.nsb}", list(shape), dtype))
        b = Buf(self, name, t)
        if self.pstack is not None:
            self.pbufs.append(b)
        return b

    def psum(self, name, shape, dtype):
        t = self.stack.enter_context(self.nc.psum_tensor(name, list(shape), dtype))
        return Buf(self, name, t)

    def dram(self, name, shape, dtype, kind="Internal"):
        t = self.nc.dram_tensor(name, list(shape), dtype, kind=kind)
        b = Buf(self, name, t.ap(), dram=True)
        self.dram_bufs.append(b)
        return b

    def share_dsem(self, bufs):
        return bufs

    def _deps(self, reads, writes):
        deps = []
        for b in reads:
            deps += b.writers
        for b in writes:
            deps += b.writers
            deps += b.readers
        return deps

    def _emit_waits(self, eng, deps):
        need = {}
        for t in deps:
            if t.sem is eng.sem and not eng.same_sync:
                continue
            k = id(t.sem)
            if k not in need or need[k].value < t.value:
                need[k] = t
        for k, t in need.items():
            if eng.waited.get(k, 0) >= t.value:
                continue
            eng.waited[k] = t.value
            eng.h.wait_ge(t.sem, t.value)

    def _commit(self, tok, reads, writes):
        for b in reads:
            b.readers = [r for r in b.readers if r.sem is not tok.sem] + [tok]
        for b in writes:
            b.writers = [tok]
            b.readers = []

    def op(self, eng, fn, reads=(), writes=()):
        reads = [b for b in reads if b is not None]
        writes = [b for b in writes if b is not None]
        self._emit_waits(eng, self._deps(reads, writes))
        eng.count += 1
        tok = Tok(eng.sem, eng.count)
        fn(eng.h).then_inc(eng.sem, 1)
        self._commit(tok, reads, writes)
        return tok

    def dma(self, eng, out_ap, in_ap, src, dst, **kw):
        sb = dst if not dst.dram else src
        q = eng.name
        if q not in sb.dsem:
            sb.dsem[q] = self.get_dsem(q)
        ds = sb.dsem[q]
        deps = self._deps([src], [dst])
        self._emit_waits(eng, deps)
        ds.count += 16
        tok = Tok(ds.h, ds.count)
        depset = set(id(d) for d in deps)
        keep = []
        for t in sb.inflight.get(q, []):
            if id(t) not in depset:
                t.value = tok.value
                keep.append(t)
        keep.append(tok)
        sb.inflight[q] = keep[-8:]
        eng.h.dma_start(out=out_ap, in_=in_ap, **kw).then_inc(ds.h, 16)
        self.ndma = getattr(self, "ndma", 0) + 1
        self._commit(tok, [src], [dst])
        return tok

    def finish(self, final_toks):
        self._emit_waits(self.sp, final_toks)


def _cp(v):
    v = np.asarray(v, np.float32)
    return np.ascontiguousarray(v.reshape(-1, 128).T)


class PP:
    def __init__(self):
        self.cols = []
        self.off = {}
        self.n = 0

    def add(self, name, arr):
        arr = np.asarray(arr, np.float32)
        assert arr.shape[0] == 128
        arr = arr.reshape(128, -1)
        self.off[name] = (self.n, arr.shape[1])
        self.n += arr.shape[1]
        self.cols.append(arr)

    def pack(self):
        return np.ascontiguousarray(np.concatenate(self.cols, axis=1))


def pack_pp(inp):
    pp = PP()
    for l in range(DEPTH):
        pp.add(f"ada_b{l}", _cp(inp["ada_b"][l]))
        for j in range(2):
            pp.add(f"plg{l}_{j}", _cp(inp["post_ln_g"][l, j]))
            pp.add(f"plb{l}_{j}", _cp(inp["post_ln_b"][l, j]))
        w = inp["ffn_conv_w"][l].reshape(9, D_FF)
        pp.add(f"fcw{l}", np.stack([_cp(w[k]) for k in range(9)], axis=1))
        pp.add(f"fcb{l}", _cp(inp["ffn_conv_b"][l]))
    for o in range(2):
        pp.add(f"cb1{o}", _cp(inp["conf_b1"][o]))
        w = inp["conf_dw_w"][o]
        pp.add(f"cdw{o}", np.stack([_cp(w[k]) for k in range(CONF_K)], axis=1))
        pp.add(f"cdb{o}", _cp(inp["conf_dw_b"][o]))
        pp.add(f"clg{o}", _cp(inp["conf_ln_g"][o]))
        pp.add(f"clb{o}", _cp(inp["conf_ln_b"][o]))
        pp.add(f"cb2{o}", _cp(inp["conf_b2"][o]))
    for e in range(2):
        w = inp["ssd_conv_w"][e]
        pp.add(f"scw{e}", np.stack([_cp(w[k]) for k in range(5)], axis=1))
        pp.add(f"scb{e}", _cp(inp["ssd_conv_b"][e]))
        pp.add(f"lbr{e}", _cp(inp["hg_lb_raw"][e]))
        pp.add(f"hnw{e}", _cp(inp["hg_norm_w"][e]))
        pp.add(f"snw{e}", _cp(inp["ssd_norm_w"][e]))
        pp.add(f"sdd{e}", _cp(np.repeat(inp["ssd_d"][e], 64)))
    return pp


def make_consts():
    c = {}
    c["ident"] = np.eye(128, dtype=np.float32)
    c["onesm"] = np.full((128, 128), 1.0 / 1024, np.float32)
    i = np.arange(128)
    tf = (i[:, None] <= i[None, :]).astype(np.float32)
    tb = (i[:, None] >= i[None, :]).astype(np.float32)
    blk = (i[:, None] // 32 == i[None, :] // 32).astype(np.float32)
    rp = np.ones((128, 1024), np.float32)
    rp[:, ::32] = 0.0
    c["cmat"] = np.ascontiguousarray(np.stack([c["ident"], c["onesm"], tf, tb, np.ones((128, 128), np.float32),
                                               tf * blk, tb * blk], axis=1))
    c["rp"] = rp
    return c


class Gen:
    def __init__(self, nc, stack, T, M, layers, dbg=()):
        self.nc = nc
        self.T = T
        self.M = M
        self.layers = layers
        self.dbg = set(dbg)
        self.kb = KB(nc, stack)
        self.kb.scopes = bool(dbg) and "scopes" in dbg
        self.stack = stack
        self.dbg_out = {}
        self.uid = 0

    def ext_in(self, name, shape, dtype=F32):
        ap = self.nc.dram_tensor(name, list(shape), dtype, kind="ExternalInput").ap()
        return Buf(self.kb, name, ap, dram=True)

    def scratch(self, name, shape, dtype):
        kind = "ExternalOutput" if name in self.dbg else "Internal"
        b = self.kb.dram(name, shape, dtype, kind=kind)
        return b

    def pool(self, name, n, shape, dtype):
        return [self.kb.sbuf(f"{name}{i}", shape, dtype) for i in range(n)]

    def ps(self):
        b = self.psb[self.psi % len(self.psb)]
        self.psi += 1
        return b

    def ACT(self, out, in_, func, reads, writes, bias=None, scale=None):
        kw = {}
        if bias is not None:
            kw["bias"] = bias
        if scale is not None:
            kw["scale"] = scale
        return self.kb.op(self.kb.act, lambda e: e.activation(out=out, in_=in_, func=func, **kw), reads, writes)

    def TS(self, eng, out, in0, s1, s2, op0, op1, reads, writes):
        if s2 is None:
            return self.kb.op(eng, lambda e: e.tensor_scalar(out=out, in0=in0, scalar1=s1, scalar2=None, op0=op0), reads, writes)
        return self.kb.op(eng, lambda e: e.tensor_scalar(out=out, in0=in0, scalar1=s1, scalar2=s2, op0=op0, op1=op1), reads, writes)

    def TT(self, eng, out, in0, in1, op, reads, writes):
        return self.kb.op(eng, lambda e: e.tensor_tensor(out=out, in0=in0, in1=in1, op=op), reads, writes)

    def STT(self, out, in0, scalar, in1, op0, op1, reads, writes):
        return self.kb.op(self.kb.dve, lambda e: e.scalar_tensor_tensor(out=out, in0=in0, scalar=scalar, in1=in1, op0=op0, op1=op1), reads, writes)

    def COPY(self, eng, out, in_, reads, writes):
        return self.kb.op(eng, lambda e: e.tensor_copy(out=out, in_=in_), reads, writes)

    def MM(self, out, lhsT, rhs, start, stop, reads, writes):
        return self.kb.op(self.kb.pe, lambda e: e.matmul(out, lhsT=lhsT, rhs=rhs, start=start, stop=stop), reads, writes)

    def LOAD(self, dst_ap, src_ap, src, dst, **kw):
        return self.kb.dma(self.kb.sp, dst_ap, src_ap, src, dst, **kw)

    def STORE(self, dst_ap, src_ap, src, dst, **kw):
        return self.kb.dma(self.kb.pool, dst_ap, src_ap, src, dst, **kw)

    def ppc(self, name, idx=None):
        off, n = self.pp_off[name]
        if idx is None:
            return self.ppt[:, off:off + n]
        return self.ppt[:, off + idx:off + idx + 1]

    def setup(self, pp_off, npp):
        kb = self.kb
        T, M = self.T, self.M
        self.pp_off = pp_off
        self.in_x = self.ext_in("x_cm", [D, T])
        self.in_ctx = self.ext_in("ctx_cm", [D, M])
        self.in_cvec = self.ext_in("cvec", [128, KC, 2])
        self.in_pp = self.ext_in("pp", [128, npp])
        self.in_cmat = self.ext_in("cmat", [128, 7, 128])
        self.in_rp = self.ext_in("rp", [128, 1024])
        self.in_tmb = self.ext_in("tmb", [128, 2, 64])
        self.in_mix_in = self.ext_in("mix_w_in", [2, D, IN_TOTAL])
        self.in_mix_out = self.ext_in("mix_w_out", [2, 2 * D, D])
        self.in_ada_w = self.ext_in("ada_w", [DEPTH, D, 6 * D])
        self.in_conf_w1 = self.ext_in("conf_w1", [2, D, 2 * D])
        self.in_conf_w2 = self.ext_in("conf_w2", [2, D, D])
        self.in_ffn_up = self.ext_in("ffn_w_up", [DEPTH, D, 2 * D_FF])
        self.in_ffn_down = self.ext_in("ffn_w_down", [DEPTH, D_FF, D])
        self.out_y = Buf(kb, "y_cm", self.nc.dram_tensor("y_cm", [D, T], F32, kind="ExternalOutput").ap(), dram=True)

        self.pb = [kb.psum(f"psb{i}", [128, 512], F32) for i in range(8)]
        self.psb = self.pb[:6]
        self.psi = 0
        self.ps_mean = self.pb[6]
        self.ps_msq = self.pb[7]

        self.ppb = kb.sbuf("ppt", [128, npp], F32)
        self.ppt = self.ppb.t
        self.LOAD(self.ppb[:], self.in_pp[:], self.in_pp, self.ppb)
        self.cst_f = kb.sbuf("cst_f", [128, 7, 128], F32)
        self.LOAD(self.cst_f[:], self.in_cmat[:], self.in_cmat, self.cst_f)
        self.cst_b = kb.sbuf("cst_b", [128, 7, 128], BF16)
        self.COPY(kb.dve, self.cst_b[:], self.cst_f[:], [self.cst_f], [self.cst_b])
        self.ident_b = self.cst_b[:, 0, :]
        self.onesm_b = self.cst_b[:, 1, :]
        self.tmb = kb.sbuf("tmb", [128, 2, 64], F32)
        self.LOAD(self.tmb[:], self.in_tmb[:], self.in_tmb, self.tmb)

        self.X = self.scratch("X", [D, T], F32)
        self.XC = self.scratch("XC", [D, M], F32)

        self.mod = kb.sbuf("mod", [128, DEPTH, 48, 2], F32)
        self.sc1 = kb.sbuf("sc1", [128, DEPTH, 2, KC, 2], F32)

    def ada(self):
        kb = self.kb
        kb.phase_begin("ada")
        cv = kb.sbuf("cv", [128, KC, 2], F32)
        self.LOAD(cv[:], self.in_cvec[:], self.in_cvec, cv)
        sv = kb.sbuf("sv", [128, KC, 2], F32)
        self.ACT(sv[:], cv[:], AF.Silu, [cv], [sv])
        wst = self.pool("adaw", 2, [128, KC, 512], F32)
        it = 0
        for l in self.layers:
            pacc = self.ps()
            for nb in range(12):
                w = wst[it % 2]
                it += 1
                self.LOAD(w[:], self.in_ada_w[l].rearrange("(c p) n -> p c n", p=128)[:, :, nb * 512:(nb + 1) * 512],
                          self.in_ada_w, w)
                for cc in range(4):
                    col = nb * 4 + cc
                    for k in range(KC):
                        self.MM(pacc[:, col * 2:col * 2 + 2], w[:, k, cc * 128:(cc + 1) * 128], sv[:, k, :],
                                k == 0, k == KC - 1, [w, sv], [pacc])
            ab = self.ppc(f"ada_b{l}")
            self.TT(kb.dve, self.mod[:, l, :, :], pacc[:, 0:96].rearrange("p (c t) -> p c t", t=2),
                    ab.unsqueeze(2).to_broadcast([128, 48, 2]), ALU.add, [pacc, self.ppb], [self.mod])
            for j, si in enumerate((1, 4)):
                self.TS(kb.dve, self.sc1[:, l, j, :, :], self.mod[:, l, si * 8:(si + 1) * 8, :], 1.0, None,
                        ALU.add, None, [self.mod], [self.sc1])
        kb.phase_end()

    def mvec(self, l, j, c, seg):
        return self.mod[:, l, j * 8 + c, seg:seg + 1]

    def ln_alloc(self):
        kb = self.kb
        self.ln_rb = kb.sbuf("ln_rb", [128, KC, 512], BF16)
        self.ln_rsq = kb.sbuf("ln_rsq", [128, KC, 512], BF16)
        self.ln_mean = kb.sbuf("ln_mean", [128, 512], F32)
        self.ln_t = kb.sbuf("ln_t", [128, 512], F32)
        self.ln_rstd = kb.sbuf("ln_rstd", [128, 512], F32)
        self.ln_xc = kb.sbuf("ln_xc", [128, KC, 512], F32)

    def ln_tile(self, r, N, gname, bname, func, out):
        kb = self.kb
        rb, rsq, mean, tt, rstd, xc = self.ln_rb, self.ln_rsq, self.ln_mean, self.ln_t, self.ln_rstd, self.ln_xc
        self.COPY(kb.pool, rb[:, :, :N], r[:, :, :N], [r], [rb])
        self.ACT(rsq[:, :, :N], r[:, :, :N], AF.Square, [r], [rsq])
        pm, pq = self.ps_mean, self.ps_msq
        for c in range(KC):
            self.MM(pm[:, :N], self.onesm_b, rb[:, c, :N], c == 0, c == KC - 1, [self.cst_b, rb], [pm])
        for c in range(KC):
            self.MM(pq[:, :N], self.onesm_b, rsq[:, c, :N], c == 0, c == KC - 1, [self.cst_b, rsq], [pq])
        self.ACT(mean[:, :N], pm[:, :N], AF.Copy, [pm], [mean])
        self.ACT(tt[:, :N], pm[:, :N], AF.Square, [pm], [tt])
        self.TT(kb.dve, tt[:, :N], pq[:, :N], tt[:, :N], ALU.subtract, [pq, tt], [tt])
        self.ACT(tt[:, :N], tt[:, :N], AF.Ln, [tt, self.eps_t], [tt], bias=self.eps_ap)
        self.ACT(rstd[:, :N], tt[:, :N], AF.Exp, [tt], [rstd], scale=-0.5)
        self.TT(kb.dve, xc[:, :, :N], r[:, :, :N], mean[:, :N].unsqueeze(1).to_broadcast([128, KC, N]),
                ALU.subtract, [r, mean], [xc])
        self.TT(kb.dve, xc[:, :, :N], xc[:, :, :N], rstd[:, :N].unsqueeze(1).to_broadcast([128, KC, N]),
                ALU.mult, [xc, rstd], [xc])
        for c in range(KC):
            self.ACT(out[:, c, :N], xc[:, c, :N], func, [xc, self.ppb], [out],
                     bias=self.ppc(bname, c), scale=self.ppc(gname, c))

    def load_w(self, dst, src_buf, src_ap, kchunks, ncols, blk=256, dstbuf=None):
        kb = self.kb
        dstbuf = dstbuf if dstbuf is not None else dst
        if getattr(self, "_ws_phase", None) is not kb.pstack or self._ws_blk < blk:
            self._ws = self.pool("wstage", 2, [128, 8, blk], F32)
            self._ws_phase = kb.pstack
            self._ws_blk = blk
        wstage = self._ws
        v = src_ap.rearrange("(c p) n -> p c n", p=128)
        i = 0
        for k0 in range(0, kchunks, 8):
            k1 = min(k0 + 8, kchunks)
            for n0 in range(0, ncols, blk):
                n1 = min(n0 + blk, ncols)
                st = wstage[i % 2]
                self.LOAD(st[:, :k1 - k0, :n1 - n0], v[:, k0:k1, n0:n1], src_buf, st)
                eng = kb.pool if i % 2 == 0 else kb.dve
                i += 1
                self.COPY(eng, dst[:, k0:k1, n0:n1], st[:, :k1 - k0, :n1 - n0], [st], [dstbuf])

    def segs(self, with_ctx):
        s = [(0, self.X, self.T)]
        if with_ctx:
            s.append((1, self.XC, self.M))
        return s

    def tiles(self, with_ctx):
        out = []
        for seg, Xb, Tn in self.segs(with_ctx):
            for i in range((Tn + 511) // 512):
                out.append((seg, Xb, i * 512, min(512, Tn - i * 512)))
        return out

    @staticmethod
    def cm(buf):
        return buf.t.rearrange("(c p) t -> p c t", p=128)

    def modulate(self, xt, hb, N, l, j, seg):
        for c in range(KC):
            self.ACT(hb[:, c, :N], xt[:, c, :N], AF.Identity, [xt, self.sc1, self.mod], [hb],
                     bias=self.mvec(l, 3 * j, c, seg), scale=self.sc1[:, l, j, c, seg:seg + 1])

    def residual(self, pst, c, xt, rt, N, l, gate_j, seg, bias_ap=None):
        kb = self.kb
        g = self.mvec(l, gate_j, c, seg)
        tmp = self.rtmp[self.rti % 2]
        self.rti += 1
        if bias_ap is not None:
            self.TS(kb.dve, tmp[:, :N], pst[:, :N], bias_ap, g, ALU.add, ALU.mult, [pst, self.mod, self.ppb], [tmp])
        else:
            self.TS(kb.dve, tmp[:, :N], pst[:, :N], g, None, ALU.mult, None, [pst, self.mod], [tmp])
        self.STT(rt[:, c, :N], xt[:, c, :N], ALPHA, tmp[:, :N], ALU.mult, ALU.add, [xt, tmp], [rt])

    def build_diag(self, dg, n, wcol_fn):
        kb = self.kb
        for k in range(n):
            eng = kb.dve if k % 2 == 0 else kb.pool
            self.TS(eng, dg[:, k, :], self.ident_b, wcol_fn(k), None, ALU.mult, None, [self.cst_b, self.ppb], [dg])

    def prologue_hb(self, l, jmod, with_ctx):
        kb = self.kb
        tl = self.tiles(with_ctx)
        xts = self.pool("xt", 2, [128, KC, 512], F32)
        hbs = []
        for it, (seg, Xb, t0, N) in enumerate(tl):
            hb = kb.sbuf(f"hbt{it}", [128, KC, N], BF16)
            xt = xts[it % 2]
            self.LOAD(xt[:, :, :N], self.cm(Xb)[:, :, t0:t0 + N], Xb, xt)
            self.modulate(xt, hb, N, l, jmod, seg)
            hbs.append(hb)
        return tl, hbs

    def wblocks(self, src_buf, w_ap, blocks, depth=2):
        kb = self.kb
        wst = self.pool("wst", depth + 1, [128, KC, 256], F32)
        wbk = self.pool("wbk", depth + 2, [128, KC, 256], BF16)
        v = w_ap.rearrange("(c p) n -> p c n", p=128)
        issued = []

        def issue(i):
            st, wb = wst[i % len(wst)], wbk[i % len(wbk)]
            o = 0
            for (c0, n) in blocks[i]:
                self.LOAD(st[:, :, o:o + n], v[:, :, c0:c0 + n], src_buf, st)
                o += n
            self.COPY(kb.pool, wb[:, :, :o], st[:, :, :o], [st], [wb])
            issued.append(wb)
        for i in range(min(depth, len(blocks))):
            issue(i)
        for i in range(len(blocks)):
            if i + depth < len(blocks):
                issue(i + depth)
            yield issued[i], blocks[i]

    def conformer(self, l, with_ctx):
        kb = self.kb
        o = l // 2
        kb.phase_begin("cf_w1")
        tl, hbs = self.prologue_hb(l, 0, with_ctx)
        stg = self.pool("stg", 3, [128, 512], BF16)
        sgs = self.pool("sg", 2, [128, 512], F32)
        n = 0
        for wb, blk in self.wblocks(self.in_conf_w1, self.in_conf_w1[o], [[(c * 128, 128), (D + c * 128, 128)] for c in range(KC)]):
            c = blk[0][0] // 128
            for it, (seg, Xb, t0, N) in enumerate(tl):
                hb = hbs[it]
                GL = self.GLp[seg]
                pa = self.ps()
                pg = self.ps()
                for k in range(KC):
                    self.MM(pa[:, :N], wb[:, k, 0:128], hb[:, k, :N], k == 0, k == KC - 1, [wb, hb], [pa])
                for k in range(KC):
                    self.MM(pg[:, :N], wb[:, k, 128:256], hb[:, k, :N], k == 0, k == KC - 1, [wb, hb], [pg])
                sg, st = sgs[n % 2], stg[n % 3]
                n += 1
                self.ACT(sg[:, :N], pg[:, :N], AF.Sigmoid, [pg, self.ppb], [sg], bias=self.ppc(f"cb1{o}", 8 + c))
                self.STT(st[:, :N], pa[:, :N], self.ppc(f"cb1{o}", c), sg[:, :N], ALU.add, ALU.mult, [pa, sg, self.ppb], [st])
                self.STORE(GL.t[c * 128:(c + 1) * 128, 15 + t0:15 + t0 + N], st[:, :N], st, GL)
        kb.phase_end()
        kb.phase_begin("cf_conv")
        dg = kb.sbuf("dg", [128, KC * CONF_K, 128], BF16)
        gts = self.pool("gt", 2, [128, KC, 512 + 30], BF16)
        uss = self.pool("us", 2, [128, KC, 512], F32)
        off, _ = self.pp_off[f"cdw{o}"]
        self.build_diag(dg, KC * CONF_K, lambda kk: self.ppt[:, off + (kk % CONF_K) * 8 + kk // CONF_K:off + (kk % CONF_K) * 8 + kk // CONF_K + 1])
        for it, (seg, Xb, t0, N) in enumerate(self.tiles(with_ctx)):
            GL, CU = self.GLp[seg], self.CU[seg]
            gt, us = gts[it % 2], uss[it % 2]
            self.LOAD(gt[:, :, :N + 30], self.cm(GL)[:, :, t0:t0 + N + 30], GL, gt)
            for c in range(KC):
                pc = self.ps()
                for k in range(CONF_K):
                    self.MM(pc[:, :N], dg[:, c * CONF_K + k, :], gt[:, c, k:k + N], k == 0, k == CONF_K - 1, [dg, gt], [pc])
                self.ACT(us[:, c, :N], pc[:, :N], AF.Identity, [pc, self.ppb], [us], bias=self.ppc(f"cdb{o}", c))
            self.STORE(self.cm(CU)[:, :, t0:t0 + N], us[:, :, :N], us, CU)
        kb.phase_end()
        kb.phase_begin("cf_w2")
        self.ln_alloc()
        w2 = kb.sbuf("w2", [128, KC, D], BF16)
        self.load_w(w2, self.in_conf_w2, self.in_conf_w2[o], KC, D)
        xts = self.pool("xt", 2, [128, KC, 512], F32)
        uts = self.pool("ut", 2, [128, KC, 512], F32)
        ub = kb.sbuf("ub", [128, KC, 512], BF16)
        rt = kb.sbuf("rt", [128, KC, 512], F32)
        self.rtmp = self.pool("rtmp", 2, [128, 512], F32)
        self.rti = 0
        for it, (seg, Xb, t0, N) in enumerate(self.tiles(with_ctx)):
            CU = self.CU[seg]
            xt, ut = xts[it % 2], uts[it % 2]
            self.LOAD(ut[:, :, :N], self.cm(CU)[:, :, t0:t0 + N], CU, ut)
            self.LOAD(xt[:, :, :N], self.cm(Xb)[:, :, t0:t0 + N], Xb, xt)
            self.ln_tile(ut, N, f"clg{o}", f"clb{o}", AF.Silu, ub)
            for c in range(KC):
                py = self.ps()
                for k in range(KC):
                    self.MM(py[:, :N], w2[:, k, c * 128:(c + 1) * 128], ub[:, k, :N], k == 0, k == KC - 1, [w2, ub], [py])
                self.residual(py, c, xt, rt, N, l, 2, seg, bias_ap=self.ppc(f"cb2{o}", c))
            self.ln_tile(rt, N, f"plg{l}_0", f"plb{l}_0", AF.Identity, xt)
            self.STORE(self.cm(Xb)[:, :, t0:t0 + N], xt[:, :, :N], xt, Xb)
        kb.phase_end()

    def ffn(self, l, with_ctx, final=False):
        kb = self.kb
        kb.phase_begin("ffn_up")
        tl, hbs = self.prologue_hb(l, 1, with_ctx)
        stg2 = self.pool("stg2", 3, [128, 2, 512], BF16)
        s2i = 0
        for wb, blk in self.wblocks(self.in_ffn_up, self.in_ffn_up[l], [[(b * 256, 256)] for b in range(2 * D_FF // 256)]):
            c0 = blk[0][0]
            half = c0 // D_FF
            g0 = (c0 % D_FF) // 128
            for it, (seg, Xb, t0, N) in enumerate(tl):
                hb = hbs[it]
                dst = (self.FV, self.FG)[half][seg]
                st = stg2[s2i % 3]
                s2i += 1
                for cc in range(2):
                    pa = self.ps()
                    for k in range(KC):
                        self.MM(pa[:, :N], wb[:, k, cc * 128:(cc + 1) * 128], hb[:, k, :N], k == 0, k == KC - 1, [wb, hb], [pa])
                    if cc == 0:
                        self.ACT(st[:, cc, :N], pa[:, :N], AF.Copy, [pa], [st])
                    else:
                        self.COPY(kb.dve, st[:, cc, :N], pa[:, :N], [pa], [st])
                self.STORE(self.cm(dst)[:, g0:g0 + 2, t0:t0 + N], st[:, :, :N], st, dst)
        kb.phase_end()
        kb.phase_begin("ffn_conv")
        GC = 11
        dg = kb.sbuf("dg", [128, GC * 9, 128], BF16)
        fgs = self.pool("fg", 2, [128, GC, 640], BF16)
        vts = self.pool("vt", 2, [128, GC, 512], BF16)
        sts = self.pool("stc", 2, [128, GC, 512], BF16)
        ges = self.pool("ge", 2, [128, 512], F32)
        off, _ = self.pp_off[f"fcw{l}"]
        it = 0
        gi = 0
        taps = [(0, 0)] + [(dr, dc) for dr in (-1, 0, 1) for dc in (-1, 0, 1) if (dr, dc) != (0, 0)]
        for g0 in range(0, FC, GC):
            self.build_diag(dg, GC * 9, lambda kk: self.ppt[:, off + (kk % 9) * FC + g0 + kk // 9:off + (kk % 9) * FC + g0 + kk // 9 + 1])
            for seg, Xb, Tn in self.segs(with_ctx):
                FV, FG, U = self.FV[seg], self.FG[seg], self.U[seg]
                if seg == 0:
                    W, R = GRID_W, Tn // GRID_W
                    rows_per = 512 // W
                else:
                    W, R = Tn, 1
                    rows_per = 1
                for i in range((R + rows_per - 1) // rows_per):
                    r0 = i * rows_per
                    nr = min(rows_per, R - r0)
                    lo = max(r0 - 1, 0)
                    hi = min(r0 + nr + 1, R)
                    N = nr * W
                    fg, vt, st = fgs[it % 2], vts[it % 2], sts[it % 2]
                    it += 1
                    self.LOAD(fg[:, :, :(hi - lo) * W], self.cm(FG)[:, g0:g0 + GC, lo * W:hi * W], FG, fg)
                    self.LOAD(vt[:, :, :N], self.cm(FV)[:, g0:g0 + GC, r0 * W:r0 * W + N], FV, vt)
                    valid = []
                    for dr, dc in taps:
                        j0 = max(0, -(r0 + dr))
                        j1 = min(nr, R - r0 - dr)
                        if j1 > j0:
                            valid.append((dr, dc, j0, j1))
                    for cc in range(GC):
                        c = g0 + cc
                        pc = self.ps()
                        pcv = pc[:, :N].rearrange("p (r w) -> p r w", w=W)
                        fgv = fg[:, cc, :(hi - lo) * W].rearrange("p (r w) -> p r w", w=W)
                        for ti, (dr, dc, j0, j1) in enumerate(valid):
                            k = (dr + 1) * 3 + (dc + 1)
                            oc0, oc1 = max(0, -dc), W - max(0, dc)
                            sr0 = r0 + j0 + dr - lo
                            self.MM(pcv[:, j0:j1, oc0:oc1], dg[:, cc * 9 + k, :],
                                    fgv[:, sr0:sr0 + (j1 - j0), oc0 + dc:oc1 + dc],
                                    ti == 0, ti == len(valid) - 1, [dg, fg], [pc])
                        ge = ges[gi % 2]
                        gi += 1
                        self.ACT(ge[:, :N], pc[:, :N], AF.Gelu, [pc, self.ppb], [ge], bias=self.ppc(f"fcb{l}", c))
                        self.TT(kb.dve, st[:, cc, :N], ge[:, :N], vt[:, cc, :N], ALU.mult, [ge, vt], [st])
                    self.STORE(self.cm(U)[:, g0:g0 + GC, r0 * W:r0 * W + N], st[:, :, :N], st, U)
        kb.phase_end()
        kb.phase_begin("ffn_down")
        self.ln_alloc()
        wd = kb.sbuf("wd", [128, FC, D], BF16)
        self.load_w(wd, self.in_ffn_down, self.in_ffn_down[l], FC, D, blk=128)
        xts = self.pool("xt", 2, [128, KC, 512], F32)
        ut = kb.sbuf("ut", [128, FC, 512], BF16)
        rt = kb.sbuf("rt", [128, KC, 512], F32)
        self.rtmp = self.pool("rtmp", 2, [128, 512], F32)
        self.rti = 0
        for it, (seg, Xb, t0, N) in enumerate(self.tiles(with_ctx)):
            U = self.U[seg]
            xt = xts[it % 2]
            self.LOAD(ut[:, :, :N], self.cm(U)[:, :, t0:t0 + N], U, ut)
            self.LOAD(xt[:, :, :N], self.cm(Xb)[:, :, t0:t0 + N], Xb, xt)
            for c in range(KC):
                py = self.ps()
                for k in range(FC):
                    self.MM(py[:, :N], wd[:, k, c * 128:(c + 1) * 128], ut[:, k, :N], k == 0, k == FC - 1, [wd, ut], [py])
                self.residual(py, c, xt, rt, N, l, 5, seg)
            self.ln_tile(rt, N, f"plg{l}_1", f"plb{l}_1", AF.Identity, xt)
            dstb = self.out_y if (final and seg == 0) else Xb
            self.last_tok = self.STORE(self.cm(dstb)[:, :, t0:t0 + N], xt[:, :, :N], xt, dstb)
        kb.phase_end()

    def chunk_order(self, d):
        out = []
        for seg, Tn in ((1, self.M), (0, self.T)):
            idx = list(range(Tn // 128))
            if d == 1:
                idx = idx[::-1]
            out += [(seg, i * 128) for i in idx]
        return out

    def even_mixer(self, l, ctx_out):
        kb = self.kb
        e = l // 2
        win = self.in_mix_in[e]
        lbv, oml, noml = self.lbv[:, e, :], self.oml[:, e, :], self.noml[:, e, :]
        all_tiles = self.tiles(True)
        kb.phase_begin("E123")
        tl, hbs = self.prologue_hb(l, 0, True)
        st2 = self.pool("st2", 3, [128, 2, 512], BF16)
        sgs = self.pool("sg", 2, [128, 512], F32)
        tls = self.pool("tl", 2, [128, 512], F32)
        lfs = self.pool("lf", 3, [128, 512], F32)
        kks = self.pool("kk", 3, [128, 512], BF16)
        stv = self.pool("stv", 2, [128, 4, 256], BF16)
        dts = self.pool("dts", 2, [128, 4, 2, 32], F32)
        x1s = self.pool("x1", 2, [128, 32], F32)
        blocks = []
        for fam, c0, n in (("z", 0, 1024), ("xbc", 1024, 1536), ("dt", 2560, 32), ("q", 2592, 1024), ("f", 3616, 2048),
                           ("v", 5664, 1024), ("g", 6688, 1024)):
            for o in range(0, n, 256):
                blocks.append((fam, c0, o, min(256, n - o)))
        cnt = 0
        for wb, blk in self.wblocks(self.in_mix_in, win, [[(c0 + o, n)] for (_, c0, o, n) in blocks]):
            fam, c0, o, n = blocks[cnt]
            cnt += 1
            for it, (seg, Xb, t0, N) in enumerate(tl):
                hb = hbs[it]
                if fam in ("z", "q", "g", "xbc"):
                    st = st2[(cnt + it) % 3]
                    for cc in range(2):
                        pa = self.ps()
                        for k in range(KC):
                            self.MM(pa[:, :N], wb[:, k, cc * 128:(cc + 1) * 128], hb[:, k, :N], k == 0, k == KC - 1, [wb, hb], [pa])
                        if fam == "xbc":
                            if cc == 0:
                                self.COPY(kb.dve, st[:, cc, :N], pa[:, :N], [pa], [st])
                            else:
                                self.ACT(st[:, cc, :N], pa[:, :N], AF.Copy, [pa], [st])
                        else:
                            self.ACT(st[:, cc, :N], pa[:, :N], AF.Silu, [pa], [st])
                    g0 = o // 128
                    if fam == "xbc":
                        self.STORE(self.cm(self.XBC[seg])[:, g0:g0 + 2, 2 + t0:2 + t0 + N], st[:, :, :N], st, self.XBC[seg])
                    else:
                        dst = {"z": self.ZS, "q": self.QS, "g": self.GS}[fam][seg]
                        self.STORE(self.cm(dst)[:, g0:g0 + 2, t0:t0 + N], st[:, :, :N], st, dst)
                elif fam == "f":
                    for cc in range(2):
                        col = o + cc * 128
                        d, h = col // 1024, (col % 1024) // 128
                        pa = self.ps()
                        for k in range(KC):
                            self.MM(pa[:, :N], wb[:, k, cc * 128:(cc + 1) * 128], hb[:, k, :N], k == 0, k == KC - 1, [wb, hb], [pa])
                        nn = (cnt * 16 + it * 2 + cc)
                        sg, tl_, lf, kk = sgs[nn % 2], tls[nn % 2], lfs[nn % 3], kks[nn % 3]
                        self.ACT(sg[:, :N], pa[:, :N], AF.Sigmoid, [pa], [sg])
                        self.TS(kb.dve, tl_[:, :N], sg[:, :N], oml[:, h:h + 1], lbv[:, h:h + 1], ALU.mult, ALU.add, [sg, self.lbb], [tl_])
                        self.ACT(lf[:, :N], tl_[:, :N], AF.Ln, [tl_], [lf])
                        self.TS(kb.dve, kk[:, :N], sg[:, :N], noml[:, h:h + 1], oml[:, h:h + 1], ALU.mult, ALU.add, [sg, self.lbb], [kk])
                        self.STORE(self.LF[d][seg].t[h * 128:(h + 1) * 128, t0:t0 + N], lf[:, :N], lf, self.LF[d][seg])
                        self.STORE(self.KK[d][seg].t[h * 128:(h + 1) * 128, t0:t0 + N], kk[:, :N], kk, self.KK[d][seg])
                elif fam == "v":
                    sv = stv[it % 2]
                    ns = N // 128
                    for sub in range(ns):
                        ts = slice(sub * 128, (sub + 1) * 128)
                        pv = self.ps()
                        for k in range(KC):
                            self.MM(pv[:, :256], hb[:, k, ts], wb[:, k, 0:256], k == 0, k == KC - 1, [hb, wb], [pv])
                        if sub % 2 == 0:
                            self.COPY(kb.dve, sv[:, sub, :], pv[:, :256], [pv], [sv])
                        else:
                            self.ACT(sv[:, sub, :], pv[:, :256], AF.Copy, [pv], [sv])
                    self.STORE(self.V[seg].t[t0:t0 + N, o:o + 256].rearrange("(s p) c -> p s c", p=128), sv[:, :ns, :], sv, self.V[seg])
                else:
                    dtt = dts[it % 2]
                    ns = N // 128
                    for sub in range(ns):
                        ts = slice(sub * 128, (sub + 1) * 128)
                        pdt = self.ps()
                        for k in range(KC):
                            self.MM(pdt[:, 0:32], hb[:, k, ts], wb[:, k, 0:32], k == 0, k == KC - 1, [hb, wb], [pdt])
                        x1 = x1s[sub % 2]
                        self.TT(kb.dve, x1[:], pdt[:, 0:32], self.tmb[:, 0, e * 32:(e + 1) * 32], ALU.add, [pdt, self.tmb], [x1])
                        self.ACT(x1[:], x1[:], AF.Exp, [x1], [x1])
                        self.ACT(dtt[:, sub, 1, :], x1[:], AF.Ln, [x1, self.one_t], [dtt], bias=self.one_ap)
                        self.TT(kb.dve, dtt[:, sub, 0, :], dtt[:, sub, 1, :], self.acoef[:, e, :], ALU.mult, [dtt, self.acb], [dtt])
                    self.STORE(self.ADT[seg].t[t0:t0 + N].rearrange("(s p) a c -> p s a c", p=128), dtt[:, :ns], dtt, self.ADT[seg])
        kb.phase_end()
        kb.phase_begin("E4")
        xins = self.pool("xin", 3, [128, 516], BF16)
        accs = self.pool("acc", 2, [128, 512], F32)
        xos = self.pool("xo", 3, [128, 512], BF16)
        tms = self.pool("tm", 3, [128, 4, 128], BF16)
        woff, _ = self.pp_off[f"scw{e}"]
        n = 0
        for c in range(12):
            for (seg, Xb, t0, N) in all_tiles:
                xin, acc, xo, tm = xins[n % 3], accs[n % 2], xos[n % 3], tms[n % 3]
                n += 1
                self.LOAD(xin[:, :N + 4], self.XBC[seg].t[c * 128:(c + 1) * 128, t0:t0 + N + 4], self.XBC[seg], xin)
                self.TS(kb.dve, acc[:, :N], xin[:, 0:N], self.ppt[:, woff + c:woff + c + 1], None, ALU.mult, None, [xin, self.ppb], [acc])
                for k in range(1, 5):
                    self.STT(acc[:, :N], xin[:, k:k + N], self.ppt[:, woff + k * 12 + c:woff + k * 12 + c + 1], acc[:, :N],
                             ALU.mult, ALU.add, [xin, acc, self.ppb], [acc])
                self.ACT(xo[:, :N], acc[:, :N], AF.Silu, [acc, self.ppb], [xo], bias=self.ppc(f"scb{e}", c))
                if c < 8:
                    self.STORE(self.XSC[seg].t[c * 128:(c + 1) * 128, t0:t0 + N], xo[:, :N], xo, self.XSC[seg])
                elif c < 10:
                    self.STORE(self.BT[seg].t[(c - 8) * 128:(c - 7) * 128, t0:t0 + N], xo[:, :N], xo, self.BT[seg])
                else:
                    self.STORE(self.CT[seg].t[(c - 10) * 128:(c - 9) * 128, t0:t0 + N], xo[:, :N], xo, self.CT[seg])
                if c < 10:
                    pt = self.ps()
                    ns = N // 128
                    for sub in range(ns):
                        self.MM(pt[:, sub * 128:(sub + 1) * 128], xo[:, sub * 128:(sub + 1) * 128], self.ident_b, True, True,
                                [xo, self.cst_b], [pt])
                    self.ACT(tm[:, :ns, :], pt[:, :N].rearrange("p (s c) -> p s c", c=128), AF.Copy, [pt], [tm])
                    if c < 8:
                        dst, cc, wd_ = self.XST[seg], c, D
                    else:
                        dst, cc, wd_ = self.BTM[seg], c - 8, 256
                    self.STORE(dst.t[t0:t0 + N, cc * 128:(cc + 1) * 128].rearrange("(s p) c -> p s c", p=128), tm[:, :ns, :], tm, dst)
        kb.phase_end()
        self.ssd_scan(e)
        self.gla_scan(e)
        self.merge(l, e, ctx_out)

    def ssd_scan(self, e):
        kb = self.kb
        pb = self.pb
        kb.phase_begin("ssd")
        HT = kb.sbuf("HT", [128, 2, 512], F32)
        Hb = kb.sbuf("Hb", [128, 2, 512], BF16)
        adts = self.pool("adt", 2, [128, 2, 32], F32)
        xss = self.pool("xs", 2, [128, 1024], BF16)
        bts = self.pool("bt", 2, [128, 2, 128], BF16)
        cts = self.pool("ct", 2, [128, 2, 128], BF16)
        btms = self.pool("btm", 2, [128, 256], BF16)
        sm = self.pool("sm", 2, [128, 4, 16], F32)
        Yt = kb.sbuf("Yt", [128, 16, 128], F32)
        GM = kb.sbuf("GM", [128, 2, 128], F32)
        Es = self.pool("E", 2, [128, 128], F32)
        Ers = self.pool("Er", 2, [128, 128], F32)
        Mhs = self.pool("Mh", 4, [128, 128], BF16)
        Css = self.pool("Cs", 4, [128, 128], BF16)
        xdt = kb.sbuf("xdt", [128, 1024], BF16)
        xdtw = kb.sbuf("xdtw", [128, 1024], BF16)
        ysts = self.pool("yst", 2, [128, KC, 128], F32)
        ones_f = self.cst_f[:, 4, :]
        it = 0
        for d in range(2):
            kb.op(kb.dve, lambda en: en.memset(HT[:], 0.0), [], [HT])
            kb.op(kb.dve, lambda en: en.memset(Hb[:], 0.0), [], [Hb])
            Tm = self.cst_f[:, 2 + d, :]
            for (seg, t0) in self.chunk_order(d):
                adt, xs, bt, ct, btm, smt, yst = adts[it % 2], xss[it % 2], bts[it % 2], cts[it % 2], btms[it % 2], sm[it % 2], ysts[it % 2]
                it += 1
                self.LOAD(adt[:], self.ADT[seg].t[t0:t0 + 128], self.ADT[seg], adt)
                self.LOAD(xs[:], self.XST[seg].t[t0:t0 + 128, :], self.XST[seg], xs)
                self.LOAD(bt[:], self.BT[seg].t.rearrange("(g n) t -> n g t", n=128)[:, :, t0:t0 + 128], self.BT[seg], bt)
                self.LOAD(ct[:], self.CT[seg].t.rearrange("(g n) t -> n g t", n=128)[:, :, t0:t0 + 128], self.CT[seg], ct)
                self.LOAD(btm[:], self.BTM[seg].t[t0:t0 + 128, :], self.BTM[seg], btm)
                a = adt[:, 0, d * 16:(d + 1) * 16]
                dtv = adt[:, 1, d * 16:(d + 1) * 16]
                pc = pb[0]
                self.MM(pc[:, 0:16], Tm, a, True, True, [self.cst_f, adt], [pc])
                self.MM(pc[:, 16:32], ones_f, a, True, True, [self.cst_f, adt], [pc])
                negc, tw, etot = smt[:, 0, :], smt[:, 1, :], smt[:, 2, :]
                self.TS(kb.dve, negc, pc[:, 0:16], -1.0, None, ALU.mult, None, [pc], [smt])
                self.TT(kb.dve, tw, pc[:, 16:32], negc, ALU.add, [pc, smt], [smt])
                self.ACT(tw, tw, AF.Exp, [smt], [smt])
                self.ACT(etot, pc[:, 16:32], AF.Exp, [pc], [smt])
                self.TT(kb.pool, Yt[:], Tm.unsqueeze(1).to_broadcast([128, 16, 128]),
                        a.unsqueeze(2).to_broadcast([128, 16, 128]), ALU.mult, [self.cst_f, adt], [Yt])
                for g in range(2):
                    self.MM(pb[3][:, g * 128:(g + 1) * 128], bt[:, g, :], ct[:, g, :], True, True, [bt, ct], [pb[3]])
                self.TT(kb.dve, GM[:], pb[3][:, 0:256].rearrange("p (g l) -> p g l", g=2),
                        Tm.unsqueeze(1).to_broadcast([128, 2, 128]), ALU.mult, [pb[3], self.cst_f], [GM])
                self.TT(kb.dve, xdt[:].rearrange("p (h q) -> p h q", q=64), xs[:].rearrange("p (h q) -> p h q", q=64),
                        dtv.unsqueeze(2).to_broadcast([128, 16, 64]), ALU.mult, [xs, adt], [xdt])
                self.TT(kb.pool, xdtw[:].rearrange("p (h q) -> p h q", q=64), xdt[:].rearrange("p (h q) -> p h q", q=64),
                        tw.unsqueeze(2).to_broadcast([128, 16, 64]), ALU.mult, [xdt, smt], [xdtw])
                for j in range(4):
                    prc = pb[1 + j % 2]
                    self.MM(prc[:, :512], ones_f, Yt[:, 4 * j:4 * j + 4, :].rearrange("p h l -> p (h l)"), True, True,
                            [self.cst_f, Yt], [prc])
                    for hh in range(4):
                        h = 4 * j + hh
                        g = h // 8
                        rc = prc[:, hh * 128:(hh + 1) * 128]
                        E, Er, Mh, Cs = Es[h % 2], Ers[h % 2], Mhs[h % 4], Css[h % 4]
                        self.ACT(E[:], rc, AF.Exp, [prc, smt], [E], bias=negc[:, h:h + 1])
                        self.STT(Mh[:], E[:], 1.0, GM[:, g, :], ALU.min, ALU.mult, [E, GM], [Mh])
                        self.ACT(Er[:], rc, AF.Exp, [prc], [Er])
                        self.TT(kb.dve, Cs[:], ct[:, g, :], Er[:], ALU.mult, [ct, Er], [Cs])
                        c = h // 2
                        po = (h % 2) * 64
                        yb = pb[4 + c // 4]
                        yc = (c % 4) * 128
                        self.MM(yb[po:po + 64, yc:yc + 128], xdt[:, h * 64:(h + 1) * 64], Mh[:], True, False, [xdt, Mh], [yb])
                        self.MM(yb[po:po + 64, yc:yc + 128], Hb[:, g, (h % 8) * 64:(h % 8 + 1) * 64], Cs[:], False, True, [Hb, Cs], [yb])
                self.ACT(yst[:, 0:4, :], pb[4][:, :512].rearrange("p (c l) -> p c l", c=4), AF.Copy, [pb[4]], [yst])
                self.COPY(kb.dve, yst[:, 4:8, :], pb[5][:, :512].rearrange("p (c l) -> p c l", c=4), [pb[5]], [yst])
                self.STORE(self.cm(self.Y[d][seg])[:, :, t0:t0 + 128], yst[:], yst, self.Y[d][seg])
                for g in range(2):
                    self.MM(pb[6 + g][:, :512], btm[:, g * 128:(g + 1) * 128], xdtw[:, g * 512:(g + 1) * 512], True, True,
                            [btm, xdtw], [pb[6 + g]])
                    self.TT(kb.dve, HT[:, g, :].rearrange("p (h q) -> p h q", q=64), HT[:, g, :].rearrange("p (h q) -> p h q", q=64),
                            etot[:, g * 8:(g + 1) * 8].unsqueeze(2).to_broadcast([128, 8, 64]), ALU.mult, [HT, smt], [HT])
                    self.TT(kb.dve, HT[:, g, :], HT[:, g, :], pb[6 + g][:, :512], ALU.add, [HT, pb[6 + g]], [HT])
                self.ACT(Hb[:], HT[:], AF.Copy, [HT], [Hb])
        kb.phase_end()

    def gla_scan(self, e):
        kb = self.kb
        pb = self.pb
        kb.phase_begin("gla")
        S = kb.sbuf("S", [128, 8, 128], F32)
        Sb = kb.sbuf("Sb", [128, 8, 128], BF16)
        zb = kb.sbuf("zb", [128, 128], BF16)
        kb.op(kb.dve, lambda en: en.memset(zb[:], 0.0), [], [zb])
        rp = kb.sbuf("rp", [128, 1024], F32)
        self.LOAD(rp[:], self.in_rp[:], self.in_rp, rp)
        qs = self.pool("q", 2, [128, 8, 128], BF16)
        lfs = self.pool("lf", 2, [128, 8, 128], F32)
        kks = self.pool("kk", 2, [128, 8, 128], BF16)
        vs = self.pool("v", 2, [128, 1024], BF16)
        F = kb.sbuf("F", [128, 8, 128], F32)
        G = kb.sbuf("G", [128, 8, 128], F32)
        D1 = kb.sbuf("D1", [128, 8, 128], F32)
        D2 = kb.sbuf("D2", [128, 8, 128], F32)
        Ex = self.pool("Ex", 2, [128, 8, 128], F32)
        qh = kb.sbuf("qh", [128, 8, 128], BF16)
        kh = kb.sbuf("kh", [128, 8, 128], BF16)
        qt = kb.sbuf("qt", [128, 8, 128], BF16)
        kt = kb.sbuf("kt", [128, 8, 128], BF16)
        ktm = kb.sbuf("ktm", [128, 1024], BF16)
        ktmz = kb.sbuf("ktmz", [128, 1024], BF16)
        AM = kb.sbuf("AM", [128, 8, 128], BF16)
        dd = kb.sbuf("dd", [128, 8, 4], F32)
        osts = self.pool("ost", 2, [128, 8, 128], F32)

        def v4(b):
            return b[:].rearrange("p h (s j) -> p h s j", j=32)

        def fl(b):
            return b[:].rearrange("p h t -> p (h t)")
        it = 0
        for d in range(2):
            kb.op(kb.dve, lambda en: en.memset(S[:], 0.0), [], [S])
            kb.op(kb.dve, lambda en: en.memset(Sb[:], 0.0), [], [Sb])
            mask = self.cst_f[:, 5 + d, :]
            for (seg, t0) in self.chunk_order(d):
                q, lf, kk, v, ost = qs[it % 2], lfs[it % 2], kks[it % 2], vs[it % 2], osts[it % 2]
                it += 1
                self.LOAD(q[:], self.cm(self.QS[seg])[:, :, t0:t0 + 128], self.QS[seg], q)
                self.LOAD(lf[:], self.cm(self.LF[d][seg])[:, :, t0:t0 + 128], self.LF[d][seg], lf)
                self.LOAD(kk[:], self.cm(self.KK[d][seg])[:, :, t0:t0 + 128], self.KK[d][seg], kk)
                self.LOAD(v[:], self.V[seg].t[t0:t0 + 128, :], self.V[seg], v)
                kb.op(kb.dve, lambda en, lf=lf: en.tensor_tensor_scan(out=fl(F), data0=rp[:], data1=fl(lf), initial=0.0,
                                                                      op0=ALU.mult, op1=ALU.add), [rp, lf], [F])
                Fl = v4(F)[:, :, :, 31:32]
                if d == 0:
                    Gb = F
                else:
                    self.TT(kb.pool, G[:], lf[:], F[:], ALU.subtract, [lf, F], [G])
                    self.TT(kb.dve, v4(G), v4(G), Fl.to_broadcast([128, 8, 4, 32]), ALU.add, [G, F], [G])
                    Gb = G
                self.TT(kb.dve, v4(D1), v4(Gb), v4(Gb)[:, :, :, 16:17].to_broadcast([128, 8, 4, 32]), ALU.subtract, [Gb], [D1])
                self.TS(kb.pool, D1[:], D1[:], 40.0, -40.0, ALU.min, ALU.max, [D1], [D1])
                self.TT(kb.dve, v4(D2), Fl.to_broadcast([128, 8, 4, 32]), v4(Gb), ALU.subtract, [F, Gb], [D2])
                E1, E2 = Ex[0], Ex[1]
                self.ACT(E1[:], D1[:], AF.Exp, [D1], [E1])
                self.TT(kb.dve, qh[:], q[:], E1[:], ALU.mult, [q, E1], [qh])
                self.ACT(E2[:], D1[:], AF.Exp, [D1], [E2], scale=-1.0)
                self.TT(kb.pool, kh[:], kk[:], E2[:], ALU.mult, [kk, E2], [kh])
                self.ACT(E1[:], Gb[:], AF.Exp, [Gb], [E1])
                self.TT(kb.dve, qt[:], q[:], E1[:], ALU.mult, [q, E1], [qt])
                self.ACT(E2[:], D2[:], AF.Exp, [D2], [E2])
                self.TT(kb.pool, kt[:], kk[:], E2[:], ALU.mult, [kk, E2], [kt])
                self.ACT(dd[:], v4(F)[:, :, :, 31], AF.Exp, [F], [dd])
                for h in range(8):
                    self.MM(pb[6 + h // 4][:, (h % 4) * 128:(h % 4 + 1) * 128], kt[:, h, :], self.ident_b, True, True,
                            [kt, self.cst_b], [pb[6 + h // 4]])
                self.ACT(ktm[:, 0:512], pb[6][:, :512], AF.Copy, [pb[6]], [ktm])
                self.COPY(kb.dve, ktm[:, 512:1024], pb[7][:, :512], [pb[7]], [ktm])
                self.ACT(ktmz[64:128, 0:512], pb[6][64:128, :512], AF.Copy, [pb[6]], [ktmz])
                self.COPY(kb.dve, ktmz[64:128, 512:1024], pb[7][64:128, :512], [pb[7]], [ktmz])
                kb.op(kb.pool, lambda en: en.memset(ktmz[64:96, :], 0.0), [], [ktmz])
                for h in range(8):
                    self.MM(pb[h // 4][:, (h % 4) * 128:(h % 4 + 1) * 128], kh[:, h, :], qh[:, h, :], True, True, [kh, qh], [pb[h // 4]])
                for half in range(2):
                    self.TT(kb.dve, AM[:, 4 * half:4 * half + 4, :], pb[half][:, :512].rearrange("p (h t) -> p h t", h=4),
                            mask.unsqueeze(1).to_broadcast([128, 4, 128]), ALU.mult, [pb[half], self.cst_f], [AM])
                for half in range(2):
                    self.MM(pb[2 + half][:, :512], zb[:], AM[:, 4 * half:4 * half + 4, :].rearrange("p h t -> p (h t)"), True, False,
                            [zb, AM], [pb[2 + half]])
                for h in range(8):
                    self.MM(pb[2 + h // 4][:, (h % 4) * 128:(h % 4 + 1) * 128], v[:, h * 128:(h + 1) * 128], AM[:, h, :], False, False,
                            [v, AM], [pb[2 + h // 4]])
                subs = [0, 1, 2, 3] if d == 0 else [3, 2, 1, 0]
                for si, i in enumerate(subs):
                    for h in range(8):
                        c0 = (h % 4) * 128 + 32 * i
                        self.MM(pb[2 + h // 4][:, c0:c0 + 32], Sb[:, h, :], qt[:, h, 32 * i:32 * i + 32], False, si == 3,
                                [Sb, qt], [pb[2 + h // 4]])
                    for h in range(8):
                        if i < 3:
                            kl, vr = ktm[32 * i:32 * i + 32, h * 128:(h + 1) * 128], v[32 * i:32 * i + 32, h * 128:(h + 1) * 128]
                        else:
                            kl, vr = ktmz[64:128, h * 128:(h + 1) * 128], v[64:128, h * 128:(h + 1) * 128]
                        self.MM(pb[4 + h // 4][:, (h % 4) * 128:(h % 4 + 1) * 128], kl, vr, True, True, [ktm, ktmz, v], [pb[4 + h // 4]])
                    for h in range(8):
                        self.STT(S[:, h, :], S[:, h, :], dd[:, h, i:i + 1], pb[4 + h // 4][:, (h % 4) * 128:(h % 4 + 1) * 128],
                                 ALU.mult, ALU.add, [S, dd, pb[4 + h // 4]], [S])
                    self.ACT(Sb[:], S[:], AF.Copy, [S], [Sb])
                self.ACT(ost[:, 0:4, :], pb[2][:, :512].rearrange("p (h t) -> p h t", h=4), AF.Copy, [pb[2]], [ost])
                self.COPY(kb.dve, ost[:, 4:8, :], pb[3][:, :512].rearrange("p (h t) -> p h t", h=4), [pb[3]], [ost])
                self.STORE(self.cm(self.O[d][seg])[:, :, t0:t0 + 128], ost[:], ost, self.O[d][seg])
        kb.phase_end()

    def merge(self, l, e, ctx_out):
        kb = self.kb
        kb.phase_begin("merge")
        self.ln_alloc()
        wo = kb.sbuf("wo", [128, 16, D], BF16)
        self.load_w(wo, self.in_mix_out, self.in_mix_out[e], 16, D, blk=128)
        NT = 256
        y0 = kb.sbuf("y0", [128, KC, NT], F32)
        y1 = kb.sbuf("y1", [128, KC, NT], F32)
        o0 = kb.sbuf("o0", [128, KC, NT], F32)
        o1 = kb.sbuf("o1", [128, KC, NT], F32)
        xs = kb.sbuf("xsc", [128, KC, NT], BF16)
        zs = kb.sbuf("zsc", [128, KC, NT], BF16)
        gs = kb.sbuf("gsc", [128, KC, NT], BF16)
        xts = self.pool("xt", 2, [128, KC, NT], F32)
        sq = kb.sbuf("sq", [128, KC, NT], BF16)
        cat = kb.sbuf("cat", [128, 16, NT], BF16)
        rstd = kb.sbuf("rstd", [128, 2, NT], F32)
        tmp = self.pool("mt", 2, [128, NT], F32)
        rt = kb.sbuf("rt", [128, KC, NT], F32)
        self.rtmp = self.pool("rtmp", 2, [128, 512], F32)
        self.rti = 0
        tl = []
        for seg, Xb, Tn in self.segs(ctx_out):
            for i in range(Tn // NT):
                tl.append((seg, Xb, i * NT, NT))
        for it, (seg, Xb, t0, N) in enumerate(tl):
            xt = xts[it % 2]
            sl = slice(t0, t0 + N)
            self.LOAD(y0[:], self.cm(self.Y[0][seg])[:, :, sl], self.Y[0][seg], y0)
            self.LOAD(y1[:], self.cm(self.Y[1][seg])[:, :, sl], self.Y[1][seg], y1)
            self.LOAD(o0[:], self.cm(self.O[0][seg])[:, :, sl], self.O[0][seg], o0)
            self.LOAD(o1[:], self.cm(self.O[1][seg])[:, :, sl], self.O[1][seg], o1)
            self.LOAD(xs[:], self.cm(self.XSC[seg])[:, :, sl], self.XSC[seg], xs)
            self.LOAD(zs[:], self.cm(self.ZS[seg])[:, :, sl], self.ZS[seg], zs)
            self.LOAD(gs[:], self.cm(self.GS[seg])[:, :, sl], self.GS[seg], gs)
            self.LOAD(xt[:], self.cm(Xb)[:, :, sl], Xb, xt)
            self.TT(kb.pool, y0[:], y0[:], y1[:], ALU.add, [y0, y1], [y0])
            for c in range(KC):
                self.STT(y0[:, c, :], xs[:, c, :], self.ppc(f"sdd{e}", c), y0[:, c, :], ALU.mult, ALU.add, [xs, y0, self.ppb], [y0])
            self.TT(kb.dve, y0[:], y0[:], zs[:], ALU.mult, [y0, zs], [y0])
            self.ACT(sq[:], y0[:], AF.Square, [y0], [sq])
            pst = self.ps()
            for g in range(2):
                for c in range(4 * g, 4 * g + 4):
                    self.MM(pst[:, g * N:(g + 1) * N], self.onesm_b, sq[:, c, :], c == 4 * g, c == 4 * g + 3, [self.cst_b, sq], [pst])
            self.ACT(rstd[:], pst[:, :2 * N].rearrange("p (g n) -> p g n", g=2), AF.Ln, [pst, self.eps_t], [rstd], bias=self.eps_ap, scale=2.0)
            self.ACT(rstd[:], rstd[:], AF.Exp, [rstd], [rstd], scale=-0.5)
            for c in range(KC):
                self.STT(cat[:, c, :], y0[:, c, :], self.ppc(f"snw{e}", c), rstd[:, c // 4, :], ALU.mult, ALU.mult,
                         [y0, rstd, self.ppb], [cat])
            self.TT(kb.pool, o0[:], o0[:], o1[:], ALU.add, [o0, o1], [o0])
            self.ACT(sq[:], o0[:], AF.Square, [o0], [sq])
            for c2 in range(0, KC, 2):
                pso = self.ps()
                for j in range(2):
                    self.MM(pso[:, j * N:(j + 1) * N], self.onesm_b, sq[:, c2 + j, :], True, True, [self.cst_b, sq], [pso])
                self.ACT(rstd[:], pso[:, :2 * N].rearrange("p (g n) -> p g n", g=2), AF.Ln, [pso, self.eps_t], [rstd], bias=self.eps_ap, scale=8.0)
                self.ACT(rstd[:], rstd[:], AF.Exp, [rstd], [rstd], scale=-0.5)
                for j in range(2):
                    c = c2 + j
                    tm = tmp[j]
                    self.STT(tm[:], o0[:, c, :], self.ppc(f"hnw{e}", c), rstd[:, j, :], ALU.mult, ALU.mult, [o0, rstd, self.ppb], [tm])
                    self.TT(kb.pool, cat[:, 8 + c, :], tm[:], gs[:, c, :], ALU.mult, [tm, gs], [cat])
            for co in range(KC):
                py = self.ps()
                for k in range(16):
                    self.MM(py[:, :N], wo[:, k, co * 128:(co + 1) * 128], cat[:, k, :], k == 0, k == 15, [wo, cat], [py])
                self.residual(py, co, xt, rt, N, l, 2, seg)
            self.ln_tile(rt, N, f"plg{l}_0", f"plb{l}_0", AF.Identity, xt)
            self.STORE(self.cm(Xb)[:, :, sl], xt[:], xt, Xb)
        kb.phase_end()

    def alloc_work(self):
        kb = self.kb
        T, M = self.T, self.M
        self.eps_t = kb.sbuf("eps_t", [128, 1], F32)
        kb.op(kb.dve, lambda e: e.memset(self.eps_t[:], EPS), [], [self.eps_t])
        self.eps_ap = self.eps_t[:, 0:1]
        self.one_t = kb.sbuf("one_t", [128, 1], F32)
        kb.op(kb.dve, lambda e: e.memset(self.one_t[:], 1.0), [], [self.one_t])
        self.one_ap = self.one_t[:, 0:1]
        self.lbb = kb.sbuf("lbb", [128, 3, 2, 8], F32)
        self.lbv, self.oml, self.noml = self.lbb[:, 0], self.lbb[:, 1], self.lbb[:, 2]
        kb.op(kb.dve, lambda e: e.memset(self.lbb[:], 0.0), [], [self.lbb])
        self.TT(kb.dve, self.lbb[:, 0, 1, :], self.ppc("lbr1"), self.ppc("lbr0"), ALU.subtract, [self.ppb], [self.lbb])
        self.ACT(self.lbb[:, 0, 1, :], self.lbb[:, 0, 1, :], AF.Sigmoid, [self.lbb], [self.lbb])
        self.TS(kb.dve, self.lbb[:, 1, :, :], self.lbb[:, 0, :, :], -1.0, 1.0, ALU.mult, ALU.add, [self.lbb], [self.lbb])
        self.TS(kb.dve, self.lbb[:, 2, :, :], self.lbb[:, 1, :, :], -1.0, None, ALU.mult, None, [self.lbb], [self.lbb])
        self.acb = kb.sbuf("acb", [128, 2, 32], F32)
        self.acoef = self.acb.t
        self.ACT(self.acb[:].rearrange("p e c -> p (e c)"), self.tmb[:, 1, :], AF.Exp, [self.tmb], [self.acb])
        self.TS(kb.dve, self.acb[:], self.acb[:], -1.0, None, ALU.mult, None, [self.acb], [self.acb])
        TM = (T, M)
        self.ZS = [self.scratch(f"ZS{i}", [D, TM[i]], BF16) for i in range(2)]
        self.QS = [self.scratch(f"QS{i}", [D, TM[i]], BF16) for i in range(2)]
        self.GS = [self.scratch(f"GS{i}", [D, TM[i]], BF16) for i in range(2)]
        self.XBC = [self.scratch(f"XBC{i}", [1536, TM[i] + 4], BF16) for i in range(2)]
        self.ADT = [self.scratch(f"ADT{i}", [TM[i], 2, 32], F32) for i in range(2)]
        self.LF = [[self.scratch(f"LF{d}{i}", [D, TM[i]], F32) for i in range(2)] for d in range(2)]
        self.KK = [[self.scratch(f"KK{d}{i}", [D, TM[i]], BF16) for i in range(2)] for d in range(2)]
        self.V = [self.scratch(f"V{i}", [TM[i], D], BF16) for i in range(2)]
        self.XSC = [self.scratch(f"XSC{i}", [D, TM[i]], BF16) for i in range(2)]
        self.XST = [self.scratch(f"XST{i}", [TM[i], D], BF16) for i in range(2)]
        self.BT = [self.scratch(f"BT{i}", [256, TM[i]], BF16) for i in range(2)]
        self.CT = [self.scratch(f"CT{i}", [256, TM[i]], BF16) for i in range(2)]
        self.BTM = [self.scratch(f"BTM{i}", [TM[i], 256], BF16) for i in range(2)]
        self.Y = [[self.scratch(f"Y{d}{i}", [D, TM[i]], F32) for i in range(2)] for d in range(2)]
        self.O = [[self.scratch(f"O{d}{i}", [D, TM[i]], F32) for i in range(2)] for d in range(2)]
        self.GLp = [self.scratch("GL0", [D, T + 30], BF16), self.scratch("GL1", [D, M + 30], BF16)]
        self.CU = [self.scratch("CU0", [D, T], F32), self.scratch("CU1", [D, M], F32)]
        self.FV = [self.scratch("FV0", [D_FF, T], BF16), self.scratch("FV1", [D_FF, M], BF16)]
        self.FG = [self.scratch("FG0", [D_FF, T], BF16), self.scratch("FG1", [D_FF, M], BF16)]
        self.U = [self.scratch("U0", [D_FF, T], BF16), self.scratch("U1", [D_FF, M], BF16)]

    def copy_in(self):
        kb = self.kb
        kb.phase_begin("copy_in")
        xts = self.pool("xt", 2, [128, KC, 512], F32)
        it = 0
        for src, dst, Tn in ((self.in_x, self.X, self.T), (self.in_ctx, self.XC, self.M)):
            for i in range((Tn + 511) // 512):
                t0 = i * 512
                N = min(512, Tn - t0)
                xt = xts[it % 2]
                it += 1
                self.LOAD(xt[:, :, :N], self.cm(src)[:, :, t0:t0 + N], src, xt)
                self.STORE(self.cm(dst)[:, :, t0:t0 + N], xt[:, :, :N], xt, dst)
        z = kb.sbuf("zpad", [128, KC, 16], BF16)
        kb.op(kb.dve, lambda e: e.memset(z[:], 0.0), [], [z])
        z12 = kb.sbuf("zpad12", [128, 12, 2], BF16)
        kb.op(kb.dve, lambda e: e.memset(z12[:], 0.0), [], [z12])
        for seg, Tn in ((0, self.T), (1, self.M)):
            v = self.cm(self.GLp[seg])
            self.STORE(v[:, :, 0:15], z[:, :, 0:15], z, self.GLp[seg])
            self.STORE(v[:, :, 15 + Tn:30 + Tn], z[:, :, 0:15], z, self.GLp[seg])
            xv = self.cm(self.XBC[seg])
            self.STORE(xv[:, :, 0:2], z12[:, :, 0:2], z12, self.XBC[seg])
            self.STORE(xv[:, :, 2 + Tn:4 + Tn], z12[:, :, 0:2], z12, self.XBC[seg])
        kb.phase_end()

    def build(self, pp_off, npp):
        self.setup(pp_off, npp)
        self.alloc_work()
        self.ada()
        self.copy_in()
        nl = len(self.layers)
        for li, l in enumerate(self.layers):
            ctx_next = any(j % 2 == 0 for j in range(l + 1, DEPTH))
            if l % 2 == 0:
                self.even_mixer(l, ctx_next)
            else:
                self.conformer(l, ctx_next)
            self.ffn(l, ctx_next, final=(li == nl - 1))
        self.kb.finish([self.last_tok])


def build_program(T, M, layers, pp_off, npp, dbg=()):
    nc = bass.Bass("TRN2", target_bir_lowering=False)
    with ExitStack() as st:
        g = Gen(nc, st, T, M, layers, dbg)
        g.build(pp_off, npp)
    return nc, g


def make_in_maps(inp, T, M, nb):
    pp = pack_pp(inp)
    ppa = pp.pack()
    cst = make_consts()
    maps = []
    for b in range(nb):
        m = {
            "x_cm": np.ascontiguousarray(inp["x"][b].T.astype(np.float32)),
            "ctx_cm": np.ascontiguousarray(inp["ctx"][b].T.astype(np.float32)),
            "cvec": np.ascontiguousarray(np.stack([_cp(inp["c"][b]), _cp(inp["c_ctx"])], axis=2)),
            "pp": ppa,
            "cmat": cst["cmat"],
            "rp": cst["rp"],
            "tmb": np.ascontiguousarray(np.broadcast_to(np.stack([
                np.asarray(inp["ssd_dt_bias"], np.float32).reshape(64),
                np.asarray(inp["ssd_a_log"], np.float32).reshape(64)], axis=0)[None], (128, 2, 64))),
            "mix_w_in": np.asarray(inp["mix_w_in"], np.float32),
            "mix_w_out": np.asarray(inp["mix_w_out"], np.float32),
            "ada_w": np.asarray(inp["ada_w"], np.float32),
            "conf_w1": np.asarray(inp["conf_w1"], np.float32),
            "conf_w2": np.asarray(inp["conf_w2"], np.float32),
            "ffn_w_up": np.asarray(inp["ffn_w_up"], np.float32),
            "ffn_w_down": np.asarray(inp["ffn_w_down"], np.float32),
        }
        maps.append(m)
    return maps, pp


def kernel(**inputs):
    inp = {k: np.asarray(v) for k, v in inputs.items()}
    B, T, _ = inp["x"].shape
    M = inp["ctx"].shape[1]
    maps, pp = make_in_maps(inp, T, M, B)
    nc, g = build_program(T, M, list(range(DEPTH)), pp.off, pp.n)
    in_maps = [maps[i % B] for i in range(8)]
    res = run_bass_kernel_spmd(nc, in_maps, core_ids=list(range(8)))
    out = np.stack([res.results[b]["y_cm"].T for b in range(B)], axis=0)
    return np.ascontiguousarray(out.astype(np.float32))
```

```python
from contextlib import ExitStack
import numpy as np
import concourse.bass as bass
import concourse.mybir as mybir
from concourse.bass_utils import run_bass_kernel_spmd

F32 = mybir.dt.float32
BF16 = mybir.dt.bfloat16
AF = mybir.ActivationFunctionType
ALU = mybir.AluOpType

D = 1024
KC = 8
DEPTH = 4
D_FF = 2816
FC = 22
GRID_W = 64
IN_TOTAL = 7712
ALPHA = (2 * DEPTH) ** 0.25
EPS = 1e-5
CONF_K = 31
import os
SAME_SYNC = os.environ.get("SAME_SYNC", "1") == "1"


class Tok:
    __slots__ = ("sem", "value")

    def __init__(self, sem, value):
        self.sem = sem
        self.value = value


class DSem:
    def __init__(self, h):
        self.h = h
        self.count = 0


class Buf:
    def __init__(self, kb, name, t=None, dram=False):
        self.kb = kb
        self.name = name
        self.t = t
        self.dram = dram
        self.writers = []
        self.readers = []
        self.dsem = {}
        self.inflight = {}

    def __getitem__(self, idx):
        return self.t[idx]


class Eng:
    def __init__(self, name, same_sync):
        self.name = name
        self.ops = []
        self.count = 0
        self.sem = None
        self.waited = {}
        self.same_sync = same_sync


class KB:
    def __init__(self, nc, stack):
        self.nc = nc
        self.stack = stack
        self.pe = Eng("tensor", False)
        self.act = Eng("scalar", SAME_SYNC)
        self.dve = Eng("vector", SAME_SYNC)
        self.pool = Eng("gpsimd", True)
        self.sp = Eng("sync", False)
        self.engs = [self.pe, self.act, self.dve, self.pool, self.sp]
        self.nsem = 0
        self.on_phase_end = None
        self.scopes = False
        self.dsem_free = []
        self.dsem_used = []
        self.pstack = None
        self.pbufs = []
        self.dram_bufs = []
        for e in self.engs:
            e.sem = self.new_sem("c_" + e.name)
            e.h = getattr(nc, e.name)

    def phase_begin(self, name=None):
        self.pstack = ExitStack()
        self.pbufs = []
        self.nph = getattr(self, "nph", 0) + 1
        if self.scopes:
            self.pstack.enter_context(self.nc.named_scope(f"ph{self.nph:03d}_{name or ''}"))

    def phase_end(self):
        toks = [Tok(e.sem, e.count) for e in self.engs if e.count]
        toks += [Tok(d.h, d.count) for d in self.dsem_used + self.dsem_free if d.count]
        for e in self.engs:
            self._emit_waits(e, toks)
        for b in self.pbufs:
            for d in b.dsem.values():
                self.dsem_used.remove(d)
                self.dsem_free.append(d)
            b.dsem = {}
        for b in self.dram_bufs:
            b.writers = []
            b.readers = []
        self.pstack.close()
        self.pstack = None
        self.pbufs = []
        if self.on_phase_end is not None:
            self.on_phase_end()

    def get_dsem(self, q):
        free = [d for d in self.dsem_free if d.q == q]
        if free:
            d = free[-1]
            self.dsem_free.remove(d)
        else:
            d = DSem(self.new_sem(f"d{self.nsem}"))
            d.q = q
        self.dsem_used.append(d)
        return d

    def new_sem(self, name):
        self.nsem += 1
        return self.stack.enter_context(self.nc.semaphore(name))

    def sbuf(self, name, shape, dtype):
        st = self.pstack if self.pstack is not None else self.stack
        self.nsb = getattr(self, "nsb", 0) + 1
        t = st.enter_context(self.nc.sbuf_tensor(f"{name}_{self.nsb}", list(shape), dtype))
        b = Buf(self, name, t)
        if self.pstack is not None:
            self.pbufs.append(b)
        return b

    def psum(self, name, shape, dtype):
        t = self.stack.enter_context(self.nc.psum_tensor(name, list(shape), dtype))
        return Buf(self, name, t)

    def dram(self, name, shape, dtype, kind="Internal"):
        t = self.nc.dram_tensor(name, list(shape), dtype, kind=kind)
        b = Buf(self, name, t.ap(), dram=True)
        self.dram_bufs.append(b)
        return b

    def share_dsem(self, bufs):
        return bufs

    def _deps(self, reads, writes):
        deps = []
        for b in reads:
            deps += b.writers
        for b in writes:
            deps += b.writers
            deps += b.readers
        return deps

    def _emit_waits(self, eng, deps):
        need = {}
        for t in deps:
            if t.sem is eng.sem and not eng.same_sync:
                continue
            k = id(t.sem)
            if k not in need or need[k].value < t.value:
                need[k] = t
        for k, t in need.items():
            if eng.waited.get(k, 0) >= t.value:
                continue
            eng.waited[k] = t.value
            eng.h.wait_ge(t.sem, t.value)

    def _commit(self, tok, reads, writes):
        for b in reads:
            b.readers = [r for r in b.readers if r.sem is not tok.sem] + [tok]
        for b in writes:
            b.writers = [tok]
            b.readers = []

    def op(self, eng, fn, reads=(), writes=()):
        reads = [b for b in reads if b is not None]
        writes = [b for b in writes if b is not None]
        self._emit_waits(eng, self._deps(reads, writes))
        eng.count += 1
        tok = Tok(eng.sem, eng.count)
        fn(eng.h).then_inc(eng.sem, 1)
        self._commit(tok, reads, writes)
        return tok

    def dma(self, eng, out_ap, in_ap, src, dst, **kw):
        sb = dst if not dst.dram else src
        q = eng.name
        if q not in sb.dsem:
            sb.dsem[q] = self.get_dsem(q)
        ds = sb.dsem[q]
        deps = self._deps([src], [dst])
        self._emit_waits(eng, deps)
        ds.count += 16
        tok = Tok(ds.h, ds.count)
        depset = set(id(d) for d in deps)
        keep = []
        for t in sb.inflight.get(q, []):
            if id(t) not in depset:
                t.value = tok.value
                keep.append(t)
        keep.append(tok)
        sb.inflight[q] = keep[-8:]
        eng.h.dma_start(out=out_ap, in_=in_ap, **kw).then_inc(ds.h, 16)
        self.ndma = getattr(self, "ndma", 0) + 1
        self._commit(tok, [src], [dst])
        return tok

    def finish(self, final_toks):
        self._emit_waits(self.sp, final_toks)


def _cp(v):
    v = np.asarray(v, np.float32)
    return np.ascontiguousarray(v.reshape(-1, 128).T)


class PP:
    def __init__(self):
        self.cols = []
        self.off = {}
        self.n = 0

    def add(self, name, arr):
        arr = np.asarray(arr, np.float32)
        assert arr.shape[0] == 128
        arr = arr.reshape(128, -1)
        self.off[name] = (self.n, arr.shape[1])
        self.n += arr.shape[1]
        self.cols.append(arr)

    def pack(self):
        return np.ascontiguousarray(np.concatenate(self.cols, axis=1))


def pack_pp(inp):
    pp = PP()
    for l in range(DEPTH):
        pp.add(f"ada_b{l}", _cp(inp["ada_b"][l]))
        for j in range(2):
            pp.add(f"plg{l}_{j}", _cp(inp["post_ln_g"][l, j]))
            pp.add(f"plb{l}_{j}", _cp(inp["post_ln_b"][l, j]))
        w = inp["ffn_conv_w"][l].reshape(9, D_FF)
        pp.add(f"fcw{l}", np.stack([_cp(w[k]) for k in range(9)], axis=1))
        pp.add(f"fcb{l}", _cp(inp["ffn_conv_b"][l]))
    for o in range(2):
        pp.add(f"cb1{o}", _cp(inp["conf_b1"][o]))
        w = inp["conf_dw_w"][o]
        pp.add(f"cdw{o}", np.stack([_cp(w[k]) for k in range(CONF_K)], axis=1))
        pp.add(f"cdb{o}", _cp(inp["conf_dw_b"][o]))
        pp.add(f"clg{o}", _cp(inp["conf_ln_g"][o]))
        pp.add(f"clb{o}", _cp(inp["conf_ln_b"][o]))
        pp.add(f"cb2{o}", _cp(inp["conf_b2"][o]))
    for e in range(2):
        w = inp["ssd_conv_w"][e]
        pp.add(f"scw{e}", np.stack([_cp(w[k]) for k in range(5)], axis=1))
        pp.add(f"scb{e}", _cp(inp["ssd_conv_b"][e]))
        pp.add(f"lbr{e}", _cp(inp["hg_lb_raw"][e]))
        pp.add(f"hnw{e}", _cp(inp["hg_norm_w"][e]))
        pp.add(f"snw{e}", _cp(inp["ssd_norm_w"][e]))
        pp.add(f"sdd{e}", _cp(np.repeat(inp["ssd_d"][e], 64)))
    return pp


def make_consts():
    c = {}
    c["ident"] = np.eye(128, dtype=np.float32)
    c["onesm"] = np.full((128, 128), 1.0 / 1024, np.float32)
    i = np.arange(128)
    tf = (i[:, None] <= i[None, :]).astype(np.float32)
    tb = (i[:, None] >= i[None, :]).astype(np.float32)
    blk = (i[:, None] // 32 == i[None, :] // 32).astype(np.float32)
    rp = np.ones((128, 1024), np.float32)
    rp[:, ::32] = 0.0
    c["cmat"] = np.ascontiguousarray(np.stack([c["ident"], c["onesm"], tf, tb, np.ones((128, 128), np.float32),
                                               tf * blk, tb * blk], axis=1))
    c["rp"] = rp
    return c


class Gen:
    def __init__(self, nc, stack, T, M, layers, dbg=()):
        self.nc = nc
        self.T = T
        self.M = M
        self.layers = layers
        self.dbg = set(dbg)
        self.kb = KB(nc, stack)
        self.kb.scopes = bool(dbg) and "scopes" in dbg
        self.kb.on_phase_end = lambda: setattr(self, "swapq", False)
        self.stack = stack
        self.dbg_out = {}
        self.uid = 0
        self.swapq = False

    def ext_in(self, name, shape, dtype=F32):
        ap = self.nc.dram_tensor(name, list(shape), dtype, kind="ExternalInput").ap()
        return Buf(self.kb, name, ap, dram=True)

    def scratch(self, name, shape, dtype):
        kind = "ExternalOutput" if name in self.dbg else "Internal"
        b = self.kb.dram(name, shape, dtype, kind=kind)
        return b

    def pool(self, name, n, shape, dtype):
        return [self.kb.sbuf(f"{name}{i}", shape, dtype) for i in range(n)]

    def ps(self):
        b = self.psb[self.psi % len(self.psb)]
        self.psi += 1
        return b

    def ACT(self, out, in_, func, reads, writes, bias=None, scale=None):
        kw = {}
        if bias is not None:
            kw["bias"] = bias
        if scale is not None:
            kw["scale"] = scale
        return self.kb.op(self.kb.act, lambda e: e.activation(out=out, in_=in_, func=func, **kw), reads, writes)

    def TS(self, eng, out, in0, s1, s2, op0, op1, reads, writes):
        if s2 is None:
            return self.kb.op(eng, lambda e: e.tensor_scalar(out=out, in0=in0, scalar1=s1, scalar2=None, op0=op0), reads, writes)
        return self.kb.op(eng, lambda e: e.tensor_scalar(out=out, in0=in0, scalar1=s1, scalar2=s2, op0=op0, op1=op1), reads, writes)

    def TT(self, eng, out, in0, in1, op, reads, writes):
        return self.kb.op(eng, lambda e: e.tensor_tensor(out=out, in0=in0, in1=in1, op=op), reads, writes)

    def STT(self, out, in0, scalar, in1, op0, op1, reads, writes):
        return self.kb.op(self.kb.dve, lambda e: e.scalar_tensor_tensor(out=out, in0=in0, scalar=scalar, in1=in1, op0=op0, op1=op1), reads, writes)

    def COPY(self, eng, out, in_, reads, writes):
        return self.kb.op(eng, lambda e: e.tensor_copy(out=out, in_=in_), reads, writes)

    def MM(self, out, lhsT, rhs, start, stop, reads, writes):
        return self.kb.op(self.kb.pe, lambda e: e.matmul(out, lhsT=lhsT, rhs=rhs, start=start, stop=stop), reads, writes)

    def LOAD(self, dst_ap, src_ap, src, dst, **kw):
        q = self.kb.pool if self.swapq else self.kb.sp
        return self.kb.dma(q, dst_ap, src_ap, src, dst, **kw)

    def STORE(self, dst_ap, src_ap, src, dst, **kw):
        q = self.kb.sp if self.swapq else self.kb.pool
        return self.kb.dma(q, dst_ap, src_ap, src, dst, **kw)

    def ppc(self, name, idx=None):
        off, n = self.pp_off[name]
        if idx is None:
            return self.ppt[:, off:off + n]
        return self.ppt[:, off + idx:off + idx + 1]

    def setup(self, pp_off, npp):
        kb = self.kb
        T, M = self.T, self.M
        self.pp_off = pp_off
        self.in_x = self.ext_in("x_cm", [D, T])
        self.in_ctx = self.ext_in("ctx_cm", [D, M])
        self.in_cvec = self.ext_in("cvec", [128, KC, 2])
        self.in_pp = self.ext_in("pp", [128, npp])
        self.in_cmat = self.ext_in("cmat", [128, 7, 128])
        self.in_rp = self.ext_in("rp", [128, 1024])
        self.in_tmb = self.ext_in("tmb", [128, 2, 64])
        self.in_mix_in = self.ext_in("mix_w_in", [2, D, IN_TOTAL])
        self.in_mix_out = self.ext_in("mix_w_out", [2, 2 * D, D])
        self.in_ada_w = self.ext_in("ada_w", [DEPTH, D, 6 * D])
        self.in_conf_w1 = self.ext_in("conf_w1", [2, D, 2 * D])
        self.in_conf_w2 = self.ext_in("conf_w2", [2, D, D])
        self.in_ffn_up = self.ext_in("ffn_w_up", [DEPTH, D, 2 * D_FF])
        self.in_ffn_down = self.ext_in("ffn_w_down", [DEPTH, D_FF, D])
        self.out_y = Buf(kb, "y_cm", self.nc.dram_tensor("y_cm", [D, T], F32, kind="ExternalOutput").ap(), dram=True)

        self.pb = [kb.psum(f"psb{i}", [128, 512], F32) for i in range(8)]
        self.psb = self.pb[:6]
        self.psi = 0
        self.ps_mean = self.pb[6]
        self.ps_msq = self.pb[7]

        self.ppb = kb.sbuf("ppt", [128, npp], F32)
        self.ppt = self.ppb.t
        self.LOAD(self.ppb[:], self.in_pp[:], self.in_pp, self.ppb)
        self.cst_f = kb.sbuf("cst_f", [128, 7, 128], F32)
        self.LOAD(self.cst_f[:], self.in_cmat[:], self.in_cmat, self.cst_f)
        self.cst_b = kb.sbuf("cst_b", [128, 7, 128], BF16)
        self.COPY(kb.dve, self.cst_b[:], self.cst_f[:], [self.cst_f], [self.cst_b])
        self.ident_b = self.cst_b[:, 0, :]
        self.onesm_b = self.cst_b[:, 1, :]
        self.tmb = kb.sbuf("tmb", [128, 2, 64], F32)
        self.LOAD(self.tmb[:], self.in_tmb[:], self.in_tmb, self.tmb)

        self.X = self.scratch("X", [D, T], F32)
        self.XC = self.scratch("XC", [D, M], F32)

        self.mod = kb.sbuf("mod", [128, DEPTH, 48, 2], F32)
        self.sc1 = kb.sbuf("sc1", [128, DEPTH, 2, KC, 2], F32)

    def ada(self):
        kb = self.kb
        kb.phase_begin("ada")
        cv = kb.sbuf("cv", [128, KC, 2], F32)
        self.LOAD(cv[:], self.in_cvec[:], self.in_cvec, cv)
        sv = kb.sbuf("sv", [128, KC, 2], F32)
        self.ACT(sv[:], cv[:], AF.Silu, [cv], [sv])
        wst = self.pool("adaw", 2, [128, KC, 512], F32)
        it = 0
        for l in self.layers:
            pacc = self.ps()
            for nb in range(12):
                w = wst[it % 2]
                it += 1
                self.LOAD(w[:], self.in_ada_w[l].rearrange("(c p) n -> p c n", p=128)[:, :, nb * 512:(nb + 1) * 512],
                          self.in_ada_w, w)
                for cc in range(4):
                    col = nb * 4 + cc
                    for k in range(KC):
                        self.MM(pacc[:, col * 2:col * 2 + 2], w[:, k, cc * 128:(cc + 1) * 128], sv[:, k, :],
                                k == 0, k == KC - 1, [w, sv], [pacc])
            ab = self.ppc(f"ada_b{l}")
            self.TT(kb.dve, self.mod[:, l, :, :], pacc[:, 0:96].rearrange("p (c t) -> p c t", t=2),
                    ab.unsqueeze(2).to_broadcast([128, 48, 2]), ALU.add, [pacc, self.ppb], [self.mod])
            for j, si in enumerate((1, 4)):
                self.TS(kb.dve, self.sc1[:, l, j, :, :], self.mod[:, l, si * 8:(si + 1) * 8, :], 1.0, None,
                        ALU.add, None, [self.mod], [self.sc1])
        kb.phase_end()

    def mvec(self, l, j, c, seg):
        return self.mod[:, l, j * 8 + c, seg:seg + 1]

    def ln_alloc(self):
        kb = self.kb
        self.ln_rb = kb.sbuf("ln_rb", [128, KC, 512], BF16)
        self.ln_rsq = kb.sbuf("ln_rsq", [128, KC, 512], BF16)
        self.ln_mean = kb.sbuf("ln_mean", [128, 512], F32)
        self.ln_t = kb.sbuf("ln_t", [128, 512], F32)
        self.ln_rstd = kb.sbuf("ln_rstd", [128, 512], F32)
        self.ln_xc = kb.sbuf("ln_xc", [128, KC, 512], F32)

    def ln_tile(self, r, N, gname, bname, func, out):
        kb = self.kb
        rb, rsq, mean, tt, rstd, xc = self.ln_rb, self.ln_rsq, self.ln_mean, self.ln_t, self.ln_rstd, self.ln_xc
        self.COPY(kb.dve, rb[:, :, :N], r[:, :, :N], [r], [rb])
        self.ACT(rsq[:, :, :N], r[:, :, :N], AF.Square, [r], [rsq])
        pm, pq = self.ps_mean, self.ps_msq
        for c in range(KC):
            self.MM(pm[:, :N], self.onesm_b, rb[:, c, :N], c == 0, c == KC - 1, [self.cst_b, rb], [pm])
        for c in range(KC):
            self.MM(pq[:, :N], self.onesm_b, rsq[:, c, :N], c == 0, c == KC - 1, [self.cst_b, rsq], [pq])
        self.ACT(mean[:, :N], pm[:, :N], AF.Copy, [pm], [mean])
        self.ACT(tt[:, :N], pm[:, :N], AF.Square, [pm], [tt])
        self.TT(kb.dve, tt[:, :N], pq[:, :N], tt[:, :N], ALU.subtract, [pq, tt], [tt])
        self.ACT(tt[:, :N], tt[:, :N], AF.Ln, [tt, self.eps_t], [tt], bias=self.eps_ap)
        self.ACT(rstd[:, :N], tt[:, :N], AF.Exp, [tt], [rstd], scale=-0.5)
        self.TT(kb.dve, xc[:, :, :N], r[:, :, :N], mean[:, :N].unsqueeze(1).to_broadcast([128, KC, N]),
                ALU.subtract, [r, mean], [xc])
        self.TT(kb.dve, xc[:, :, :N], xc[:, :, :N], rstd[:, :N].unsqueeze(1).to_broadcast([128, KC, N]),
                ALU.mult, [xc, rstd], [xc])
        for c in range(KC):
            self.ACT(out[:, c, :N], xc[:, c, :N], func, [xc, self.ppb], [out],
                     bias=self.ppc(bname, c), scale=self.ppc(gname, c))

    def load_w(self, dst, src_buf, src_ap, kchunks, ncols, blk=256, dstbuf=None):
        kb = self.kb
        dstbuf = dstbuf if dstbuf is not None else dst
        if getattr(self, "_ws_phase", None) is not kb.pstack or self._ws_blk < blk:
            self._ws = self.pool("wstage", 2, [128, 8, blk], F32)
            self._ws_phase = kb.pstack
            self._ws_blk = blk
        wstage = self._ws
        v = src_ap.rearrange("(c p) n -> p c n", p=128)
        i = 0
        for k0 in range(0, kchunks, 8):
            k1 = min(k0 + 8, kchunks)
            for n0 in range(0, ncols, blk):
                n1 = min(n0 + blk, ncols)
                st = wstage[i % 2]
                self.LOAD(st[:, :k1 - k0, :n1 - n0], v[:, k0:k1, n0:n1], src_buf, st)
                eng = kb.pool if i % 2 == 0 else kb.dve
                i += 1
                self.COPY(eng, dst[:, k0:k1, n0:n1], st[:, :k1 - k0, :n1 - n0], [st], [dstbuf])

    def segs(self, with_ctx):
        s = [(0, self.X, self.T)]
        if with_ctx:
            s.append((1, self.XC, self.M))
        return s

    def tiles(self, with_ctx):
        out = []
        for seg, Xb, Tn in self.segs(with_ctx):
            for i in range((Tn + 511) // 512):
                out.append((seg, Xb, i * 512, min(512, Tn - i * 512)))
        return out

    @staticmethod
    def cm(buf):
        return buf.t.rearrange("(c p) t -> p c t", p=128)

    def modulate(self, xt, hb, N, l, j, seg):
        for c in range(KC):
            self.ACT(hb[:, c, :N], xt[:, c, :N], AF.Identity, [xt, self.sc1, self.mod], [hb],
                     bias=self.mvec(l, 3 * j, c, seg), scale=self.sc1[:, l, j, c, seg:seg + 1])

    def residual(self, pst, c, xt, rt, N, l, gate_j, seg, bias_ap=None):
        kb = self.kb
        g = self.mvec(l, gate_j, c, seg)
        tmp = self.rtmp[self.rti % 2]
        self.rti += 1
        if bias_ap is not None:
            self.TS(kb.dve, tmp[:, :N], pst[:, :N], bias_ap, g, ALU.add, ALU.mult, [pst, self.mod, self.ppb], [tmp])
        else:
            self.TS(kb.dve, tmp[:, :N], pst[:, :N], g, None, ALU.mult, None, [pst, self.mod], [tmp])
        self.STT(rt[:, c, :N], xt[:, c, :N], ALPHA, tmp[:, :N], ALU.mult, ALU.add, [xt, tmp], [rt])

    def build_diag(self, dg, n, wcol_fn):
        kb = self.kb
        for k in range(n):
            self.TS(kb.dve, dg[:, k, :], self.ident_b, wcol_fn(k), None, ALU.mult, None, [self.cst_b, self.ppb], [dg])

    def prologue_hb(self, l, jmod, with_ctx):
        kb = self.kb
        tl = self.tiles(with_ctx)
        xts = self.pool("xt", 2, [128, KC, 512], F32)
        hbs = []
        for it, (seg, Xb, t0, N) in enumerate(tl):
            hb = kb.sbuf(f"hbt{it}", [128, KC, N], BF16)
            xt = xts[it % 2]
            self.LOAD(xt[:, :, :N], self.cm(Xb)[:, :, t0:t0 + N], Xb, xt)
            self.modulate(xt, hb, N, l, jmod, seg)
            hbs.append(hb)
        return tl, hbs

    def wblocks(self, src_buf, w_ap, blocks, depth=2):
        kb = self.kb
        wst = self.pool("wst", depth + 1, [128, KC, 256], F32)
        wbk = self.pool("wbk", depth + 2, [128, KC, 256], BF16)
        v = w_ap.rearrange("(c p) n -> p c n", p=128)
        issued = []

        def issue(i):
            st, wb = wst[i % len(wst)], wbk[i % len(wbk)]
            o = 0
            for (c0, n) in blocks[i]:
                self.LOAD(st[:, :, o:o + n], v[:, :, c0:c0 + n], src_buf, st)
                o += n
            self.COPY(kb.dve if i % 2 == 0 else kb.pool, wb[:, :, :o], st[:, :, :o], [st], [wb])
            issued.append(wb)
        for i in range(min(depth, len(blocks))):
            issue(i)
        for i in range(len(blocks)):
            if i + depth < len(blocks):
                issue(i + depth)
            yield issued[i], blocks[i]

    def conformer(self, l, with_ctx):
        kb = self.kb
        o = l // 2
        kb.phase_begin("cf_w1")
        self.swapq = True
        tl, hbs = self.prologue_hb(l, 0, with_ctx)
        stg = self.pool("stg", 3, [128, 512], BF16)
        sgs = self.pool("sg", 2, [128, 512], F32)
        n = 0
        for wb, blk in self.wblocks(self.in_conf_w1, self.in_conf_w1[o], [[(c * 128, 128), (D + c * 128, 128)] for c in range(KC)]):
            c = blk[0][0] // 128
            for it, (seg, Xb, t0, N) in enumerate(tl):
                hb = hbs[it]
                GL = self.GLp[seg]
                pa = self.ps()
                pg = self.ps()
                for k in range(KC):
                    self.MM(pa[:, :N], wb[:, k, 0:128], hb[:, k, :N], k == 0, k == KC - 1, [wb, hb], [pa])
                for k in range(KC):
                    self.MM(pg[:, :N], wb[:, k, 128:256], hb[:, k, :N], k == 0, k == KC - 1, [wb, hb], [pg])
                sg, st = sgs[n % 2], stg[n % 3]
                n += 1
                self.ACT(sg[:, :N], pg[:, :N], AF.Sigmoid, [pg, self.ppb], [sg], bias=self.ppc(f"cb1{o}", 8 + c))
                self.STT(st[:, :N], pa[:, :N], self.ppc(f"cb1{o}", c), sg[:, :N], ALU.add, ALU.mult, [pa, sg, self.ppb], [st])
                self.STORE(GL.t[c * 128:(c + 1) * 128, 15 + t0:15 + t0 + N], st[:, :N], st, GL)
        kb.phase_end()
        kb.phase_begin("cf_conv")
        dg = kb.sbuf("dg", [128, KC * CONF_K, 128], BF16)
        gts = self.pool("gt", 2, [128, KC, 512 + 30], BF16)
        uss = self.pool("us", 2, [128, KC, 512], F32)
        off, _ = self.pp_off[f"cdw{o}"]
        self.build_diag(dg, KC * CONF_K, lambda kk: self.ppt[:, off + (kk % CONF_K) * 8 + kk // CONF_K:off + (kk % CONF_K) * 8 + kk // CONF_K + 1])
        for it, (seg, Xb, t0, N) in enumerate(self.tiles(with_ctx)):
            GL, CU = self.GLp[seg], self.CU[seg]
            gt, us = gts[it % 2], uss[it % 2]
            self.LOAD(gt[:, :, :N + 30], self.cm(GL)[:, :, t0:t0 + N + 30], GL, gt)
            for c in range(KC):
                pc = self.ps()
                for k in range(CONF_K):
                    self.MM(pc[:, :N], dg[:, c * CONF_K + k, :], gt[:, c, k:k + N], k == 0, k == CONF_K - 1, [dg, gt], [pc])
                self.ACT(us[:, c, :N], pc[:, :N], AF.Identity, [pc, self.ppb], [us], bias=self.ppc(f"cdb{o}", c))
            self.STORE(self.cm(CU)[:, :, t0:t0 + N], us[:, :, :N], us, CU)
        kb.phase_end()
        kb.phase_begin("cf_w2")
        self.ln_alloc()
        w2 = kb.sbuf("w2", [128, KC, D], BF16)
        self.load_w(w2, self.in_conf_w2, self.in_conf_w2[o], KC, D)
        xts = self.pool("xt", 2, [128, KC, 512], F32)
        uts = self.pool("ut", 2, [128, KC, 512], F32)
        ub = kb.sbuf("ub", [128, KC, 512], BF16)
        rt = kb.sbuf("rt", [128, KC, 512], F32)
        self.rtmp = self.pool("rtmp", 2, [128, 512], F32)
        self.rti = 0
        for it, (seg, Xb, t0, N) in enumerate(self.tiles(with_ctx)):
            CU = self.CU[seg]
            xt, ut = xts[it % 2], uts[it % 2]
            self.LOAD(ut[:, :, :N], self.cm(CU)[:, :, t0:t0 + N], CU, ut)
            self.LOAD(xt[:, :, :N], self.cm(Xb)[:, :, t0:t0 + N], Xb, xt)
            self.ln_tile(ut, N, f"clg{o}", f"clb{o}", AF.Silu, ub)
            for c in range(KC):
                py = self.ps()
                for k in range(KC):
                    self.MM(py[:, :N], w2[:, k, c * 128:(c + 1) * 128], ub[:, k, :N], k == 0, k == KC - 1, [w2, ub], [py])
                self.residual(py, c, xt, rt, N, l, 2, seg, bias_ap=self.ppc(f"cb2{o}", c))
            self.ln_tile(rt, N, f"plg{l}_0", f"plb{l}_0", AF.Identity, xt)
            self.STORE(self.cm(Xb)[:, :, t0:t0 + N], xt[:, :, :N], xt, Xb)
        kb.phase_end()

    def ffn(self, l, with_ctx, final=False):
        kb = self.kb
        kb.phase_begin("ffn_up")
        self.swapq = True
        tl, hbs = self.prologue_hb(l, 1, with_ctx)
        stg2 = self.pool("stg2", 3, [128, 2, 512], BF16)
        s2i = 0
        for wb, blk in self.wblocks(self.in_ffn_up, self.in_ffn_up[l], [[(b * 256, 256)] for b in range(2 * D_FF // 256)]):
            c0 = blk[0][0]
            half = c0 // D_FF
            g0 = (c0 % D_FF) // 128
            for it, (seg, Xb, t0, N) in enumerate(tl):
                hb = hbs[it]
                dst = (self.FV, self.FG)[half][seg]
                st = stg2[s2i % 3]
                s2i += 1
                for cc in range(2):
                    pa = self.ps()
                    for k in range(KC):
                        self.MM(pa[:, :N], wb[:, k, cc * 128:(cc + 1) * 128], hb[:, k, :N], k == 0, k == KC - 1, [wb, hb], [pa])
                    if cc == 0:
                        self.ACT(st[:, cc, :N], pa[:, :N], AF.Copy, [pa], [st])
                    else:
                        self.COPY(kb.dve, st[:, cc, :N], pa[:, :N], [pa], [st])
                self.STORE(self.cm(dst)[:, g0:g0 + 2, t0:t0 + N], st[:, :, :N], st, dst)
        kb.phase_end()
        kb.phase_begin("ffn_conv")
        GC = 11
        dg = kb.sbuf("dg", [128, GC * 9, 128], BF16)
        fgs = self.pool("fg", 2, [128, GC, 640], BF16)
        vts = self.pool("vt", 2, [128, GC, 512], BF16)
        sts = self.pool("stc", 2, [128, GC, 512], BF16)
        ges = self.pool("ge", 2, [128, 512], F32)
        off, _ = self.pp_off[f"fcw{l}"]
        it = 0
        gi = 0
        taps = [(0, 0)] + [(dr, dc) for dr in (-1, 0, 1) for dc in (-1, 0, 1) if (dr, dc) != (0, 0)]
        for g0 in range(0, FC, GC):
            self.build_diag(dg, GC * 9, lambda kk: self.ppt[:, off + (kk % 9) * FC + g0 + kk // 9:off + (kk % 9) * FC + g0 + kk // 9 + 1])
            for seg, Xb, Tn in self.segs(with_ctx):
                FV, FG, U = self.FV[seg], self.FG[seg], self.U[seg]
                if seg == 0:
                    W, R = GRID_W, Tn // GRID_W
                    rows_per = 512 // W
                else:
                    W, R = Tn, 1
                    rows_per = 1
                for i in range((R + rows_per - 1) // rows_per):
                    r0 = i * rows_per
                    nr = min(rows_per, R - r0)
                    lo = max(r0 - 1, 0)
                    hi = min(r0 + nr + 1, R)
                    N = nr * W
                    fg, vt, st = fgs[it % 2], vts[it % 2], sts[it % 2]
                    it += 1
                    self.LOAD(fg[:, :, :(hi - lo) * W], self.cm(FG)[:, g0:g0 + GC, lo * W:hi * W], FG, fg)
                    self.LOAD(vt[:, :, :N], self.cm(FV)[:, g0:g0 + GC, r0 * W:r0 * W + N], FV, vt)
                    valid = []
                    for dr, dc in taps:
                        j0 = max(0, -(r0 + dr))
                        j1 = min(nr, R - r0 - dr)
                        if j1 > j0:
                            valid.append((dr, dc, j0, j1))
                    for cc in range(GC):
                        c = g0 + cc
                        pc = self.ps()
                        pcv = pc[:, :N].rearrange("p (r w) -> p r w", w=W)
                        fgv = fg[:, cc, :(hi - lo) * W].rearrange("p (r w) -> p r w", w=W)
                        for ti, (dr, dc, j0, j1) in enumerate(valid):
                            k = (dr + 1) * 3 + (dc + 1)
                            oc0, oc1 = max(0, -dc), W - max(0, dc)
                            sr0 = r0 + j0 + dr - lo
                            self.MM(pcv[:, j0:j1, oc0:oc1], dg[:, cc * 9 + k, :],
                                    fgv[:, sr0:sr0 + (j1 - j0), oc0 + dc:oc1 + dc],
                                    ti == 0, ti == len(valid) - 1, [dg, fg], [pc])
                        ge = ges[gi % 2]
                        gi += 1
                        self.ACT(ge[:, :N], pc[:, :N], AF.Gelu, [pc, self.ppb], [ge], bias=self.ppc(f"fcb{l}", c))
                        self.TT(kb.dve, st[:, cc, :N], ge[:, :N], vt[:, cc, :N], ALU.mult, [ge, vt], [st])
                    self.STORE(self.cm(U)[:, g0:g0 + GC, r0 * W:r0 * W + N], st[:, :, :N], st, U)
        kb.phase_end()
        kb.phase_begin("ffn_down")
        self.ln_alloc()
        wd = kb.sbuf("wd", [128, FC, D], BF16)
        self.load_w(wd, self.in_ffn_down, self.in_ffn_down[l], FC, D, blk=128)
        xts = self.pool("xt", 2, [128, KC, 512], F32)
        ut = kb.sbuf("ut", [128, FC, 512], BF16)
        rt = kb.sbuf("rt", [128, KC, 512], F32)
        self.rtmp = self.pool("rtmp", 2, [128, 512], F32)
        self.rti = 0
        for it, (seg, Xb, t0, N) in enumerate(self.tiles(with_ctx)):
            U = self.U[seg]
            xt = xts[it % 2]
            self.LOAD(ut[:, :, :N], self.cm(U)[:, :, t0:t0 + N], U, ut)
            self.LOAD(xt[:, :, :N], self.cm(Xb)[:, :, t0:t0 + N], Xb, xt)
            for c in range(KC):
                py = self.ps()
                for k in range(FC):
                    self.MM(py[:, :N], wd[:, k, c * 128:(c + 1) * 128], ut[:, k, :N], k == 0, k == FC - 1, [wd, ut], [py])
                self.residual(py, c, xt, rt, N, l, 5, seg)
            self.ln_tile(rt, N, f"plg{l}_1", f"plb{l}_1", AF.Identity, xt)
            dstb = self.out_y if (final and seg == 0) else Xb
            self.last_tok = self.STORE(self.cm(dstb)[:, :, t0:t0 + N], xt[:, :, :N], xt, dstb)
        kb.phase_end()

    def chunk_order(self, d):
        out = []
        for seg, Tn in ((1, self.M), (0, self.T)):
            idx = list(range(Tn // 128))
            if d == 1:
                idx = idx[::-1]
            out += [(seg, i * 128) for i in idx]
        return out

    def even_mixer(self, l, ctx_out):
        kb = self.kb
        e = l // 2
        win = self.in_mix_in[e]
        lbv, oml, noml = self.lbv[:, e, :], self.oml[:, e, :], self.noml[:, e, :]
        all_tiles = self.tiles(True)
        kb.phase_begin("E123")
        self.swapq = True
        tl, hbs = self.prologue_hb(l, 0, True)
        st2 = self.pool("st2", 3, [128, 2, 512], BF16)
        sgs = self.pool("sg", 2, [128, 512], F32)
        tls = self.pool("tl", 2, [128, 512], F32)
        lfs = self.pool("lf", 3, [128, 512], F32)
        kks = self.pool("kk", 3, [128, 512], BF16)
        stv = self.pool("stv", 2, [128, 4, 256], BF16)
        dts = self.pool("dts", 2, [128, 4, 2, 32], F32)
        x1s = self.pool("x1", 2, [128, 32], F32)
        blocks = []
        for fam, c0, n in (("z", 0, 1024), ("xbc", 1024, 1536), ("dt", 2560, 32), ("q", 2592, 1024), ("f", 3616, 2048),
                           ("v", 5664, 1024), ("g", 6688, 1024)):
            for o in range(0, n, 256):
                blocks.append((fam, c0, o, min(256, n - o)))
        cnt = 0
        for wb, blk in self.wblocks(self.in_mix_in, win, [[(c0 + o, n)] for (_, c0, o, n) in blocks]):
            fam, c0, o, n = blocks[cnt]
            cnt += 1
            for it, (seg, Xb, t0, N) in enumerate(tl):
                hb = hbs[it]
                if fam in ("z", "q", "g", "xbc"):
                    st = st2[(cnt + it) % 3]
                    for cc in range(2):
                        pa = self.ps()
                        for k in range(KC):
                            self.MM(pa[:, :N], wb[:, k, cc * 128:(cc + 1) * 128], hb[:, k, :N], k == 0, k == KC - 1, [wb, hb], [pa])
                        if fam == "xbc":
                            if cc == 0:
                                self.COPY(kb.dve, st[:, cc, :N], pa[:, :N], [pa], [st])
                            else:
                                self.ACT(st[:, cc, :N], pa[:, :N], AF.Copy, [pa], [st])
                        else:
                            self.ACT(st[:, cc, :N], pa[:, :N], AF.Silu, [pa], [st])
                    g0 = o // 128
                    if fam == "xbc":
                        self.STORE(self.cm(self.XBC[seg])[:, g0:g0 + 2, 2 + t0:2 + t0 + N], st[:, :, :N], st, self.XBC[seg])
                    else:
                        dst = {"z": self.ZS, "q": self.QS, "g": self.GS}[fam][seg]
                        self.STORE(self.cm(dst)[:, g0:g0 + 2, t0:t0 + N], st[:, :, :N], st, dst)
                elif fam == "f":
                    for cc in range(2):
                        col = o + cc * 128
                        d, h = col // 1024, (col % 1024) // 128
                        pa = self.ps()
                        for k in range(KC):
                            self.MM(pa[:, :N], wb[:, k, cc * 128:(cc + 1) * 128], hb[:, k, :N], k == 0, k == KC - 1, [wb, hb], [pa])
                        nn = (cnt * 16 + it * 2 + cc)
                        sg, tl_, lf, kk = sgs[nn % 2], tls[nn % 2], lfs[nn % 3], kks[nn % 3]
                        self.ACT(sg[:, :N], pa[:, :N], AF.Sigmoid, [pa], [sg])
                        self.TS(kb.dve, tl_[:, :N], sg[:, :N], oml[:, h:h + 1], lbv[:, h:h + 1], ALU.mult, ALU.add, [sg, self.lbb], [tl_])
                        self.ACT(lf[:, :N], tl_[:, :N], AF.Ln, [tl_], [lf])
                        self.TS(kb.dve, kk[:, :N], sg[:, :N], noml[:, h:h + 1], oml[:, h:h + 1], ALU.mult, ALU.add, [sg, self.lbb], [kk])
                        self.STORE(self.LF[d][seg].t[h * 128:(h + 1) * 128, t0:t0 + N], lf[:, :N], lf, self.LF[d][seg])
                        self.STORE(self.KK[d][seg].t[h * 128:(h + 1) * 128, t0:t0 + N], kk[:, :N], kk, self.KK[d][seg])
                elif fam == "v":
                    sv = stv[it % 2]
                    ns = N // 128
                    for sub in range(ns):
                        ts = slice(sub * 128, (sub + 1) * 128)
                        pv = self.ps()
                        for k in range(KC):
                            self.MM(pv[:, :256], hb[:, k, ts], wb[:, k, 0:256], k == 0, k == KC - 1, [hb, wb], [pv])
                        if sub % 2 == 0:
                            self.COPY(kb.dve, sv[:, sub, :], pv[:, :256], [pv], [sv])
                        else:
                            self.ACT(sv[:, sub, :], pv[:, :256], AF.Copy, [pv], [sv])
                    self.STORE(self.V[seg].t[t0:t0 + N, o:o + 256].rearrange("(s p) c -> p s c", p=128), sv[:, :ns, :], sv, self.V[seg])
                else:
                    dtt = dts[it % 2]
                    ns = N // 128
                    for sub in range(ns):
                        ts = slice(sub * 128, (sub + 1) * 128)
                        pdt = self.ps()
                        for k in range(KC):
                            self.MM(pdt[:, 0:32], hb[:, k, ts], wb[:, k, 0:32], k == 0, k == KC - 1, [hb, wb], [pdt])
                        x1 = x1s[sub % 2]
                        self.TT(kb.dve, x1[:], pdt[:, 0:32], self.tmb[:, 0, e * 32:(e + 1) * 32], ALU.add, [pdt, self.tmb], [x1])
                        self.ACT(x1[:], x1[:], AF.Exp, [x1], [x1])
                        self.ACT(dtt[:, sub, 1, :], x1[:], AF.Ln, [x1, self.one_t], [dtt], bias=self.one_ap)
                        self.TT(kb.dve, dtt[:, sub, 0, :], dtt[:, sub, 1, :], self.acoef[:, e, :], ALU.mult, [dtt, self.acb], [dtt])
                    self.STORE(self.ADT[seg].t[t0:t0 + N].rearrange("(s p) a c -> p s a c", p=128), dtt[:, :ns], dtt, self.ADT[seg])
        kb.phase_end()
        kb.phase_begin("E4")
        self.swapq = True
        xins = self.pool("xin", 3, [128, 516], BF16)
        accs = self.pool("acc", 2, [128, 512], F32)
        xos = self.pool("xo", 3, [128, 512], BF16)
        tms = self.pool("tm", 3, [128, 4, 128], BF16)
        woff, _ = self.pp_off[f"scw{e}"]
        n = 0
        for c in range(12):
            for (seg, Xb, t0, N) in all_tiles:
                xin, acc, xo, tm = xins[n % 3], accs[n % 2], xos[n % 3], tms[n % 3]
                n += 1
                self.LOAD(xin[:, :N + 4], self.XBC[seg].t[c * 128:(c + 1) * 128, t0:t0 + N + 4], self.XBC[seg], xin)
                self.TS(kb.dve, acc[:, :N], xin[:, 0:N], self.ppt[:, woff + c:woff + c + 1], None, ALU.mult, None, [xin, self.ppb], [acc])
                for k in range(1, 5):
                    self.STT(acc[:, :N], xin[:, k:k + N], self.ppt[:, woff + k * 12 + c:woff + k * 12 + c + 1], acc[:, :N],
                             ALU.mult, ALU.add, [xin, acc, self.ppb], [acc])
                self.ACT(xo[:, :N], acc[:, :N], AF.Silu, [acc, self.ppb], [xo], bias=self.ppc(f"scb{e}", c))
                if c < 8:
                    self.STORE(self.XSC[seg].t[c * 128:(c + 1) * 128, t0:t0 + N], xo[:, :N], xo, self.XSC[seg])
                elif c < 10:
                    self.STORE(self.BT[seg].t[(c - 8) * 128:(c - 7) * 128, t0:t0 + N], xo[:, :N], xo, self.BT[seg])
                else:
                    self.STORE(self.CT[seg].t[(c - 10) * 128:(c - 9) * 128, t0:t0 + N], xo[:, :N], xo, self.CT[seg])
                if c < 10:
                    pt = self.ps()
                    ns = N // 128
                    for sub in range(ns):
                        self.MM(pt[:, sub * 128:(sub + 1) * 128], xo[:, sub * 128:(sub + 1) * 128], self.ident_b, True, True,
                                [xo, self.cst_b], [pt])
                    self.ACT(tm[:, :ns, :], pt[:, :N].rearrange("p (s c) -> p s c", c=128), AF.Copy, [pt], [tm])
                    if c < 8:
                        dst, cc, wd_ = self.XST[seg], c, D
                    else:
                        dst, cc, wd_ = self.BTM[seg], c - 8, 256
                    self.STORE(dst.t[t0:t0 + N, cc * 128:(cc + 1) * 128].rearrange("(s p) c -> p s c", p=128), tm[:, :ns, :], tm, dst)
        kb.phase_end()
        self.ssd_scan(e)
        self.gla_scan(e)
        self.merge(l, e, ctx_out)

    def ssd_scan(self, e):
        kb = self.kb
        pb = self.pb
        kb.phase_begin("ssd")
        HT = kb.sbuf("HT", [128, 2, 512], F32)
        Hb = kb.sbuf("Hb", [128, 2, 512], BF16)
        adts = self.pool("adt", 2, [128, 2, 32], F32)
        xss = self.pool("xs", 2, [128, 1024], BF16)
        bts = self.pool("bt", 2, [128, 2, 128], BF16)
        cts = self.pool("ct", 2, [128, 2, 128], BF16)
        btms = self.pool("btm", 2, [128, 256], BF16)
        sm = self.pool("sm", 2, [128, 4, 16], F32)
        Yt = kb.sbuf("Yt", [128, 16, 128], F32)
        GM = kb.sbuf("GM", [128, 2, 128], F32)
        Es = self.pool("E", 2, [128, 128], F32)
        Ers = self.pool("Er", 2, [128, 128], F32)
        Mhs = self.pool("Mh", 4, [128, 128], BF16)
        Css = self.pool("Cs", 4, [128, 128], BF16)
        xdt = kb.sbuf("xdt", [128, 1024], BF16)
        xdtw = kb.sbuf("xdtw", [128, 1024], BF16)
        ysts = self.pool("yst", 2, [128, KC, 128], F32)
        ones_f = self.cst_f[:, 4, :]
        it = 0
        for d in range(2):
            kb.op(kb.dve, lambda en: en.memset(HT[:], 0.0), [], [HT])
            kb.op(kb.dve, lambda en: en.memset(Hb[:], 0.0), [], [Hb])
            Tm = self.cst_f[:, 2 + d, :]
            for (seg, t0) in self.chunk_order(d):
                adt, xs, bt, ct, btm, smt, yst = adts[it % 2], xss[it % 2], bts[it % 2], cts[it % 2], btms[it % 2], sm[it % 2], ysts[it % 2]
                it += 1
                self.LOAD(adt[:], self.ADT[seg].t[t0:t0 + 128], self.ADT[seg], adt)
                self.LOAD(xs[:], self.XST[seg].t[t0:t0 + 128, :], self.XST[seg], xs)
                self.LOAD(bt[:], self.BT[seg].t.rearrange("(g n) t -> n g t", n=128)[:, :, t0:t0 + 128], self.BT[seg], bt)
                self.LOAD(ct[:], self.CT[seg].t.rearrange("(g n) t -> n g t", n=128)[:, :, t0:t0 + 128], self.CT[seg], ct)
                self.LOAD(btm[:], self.BTM[seg].t[t0:t0 + 128, :], self.BTM[seg], btm)
                a = adt[:, 0, d * 16:(d + 1) * 16]
                dtv = adt[:, 1, d * 16:(d + 1) * 16]
                pc = pb[0]
                self.MM(pc[:, 0:16], Tm, a, True, True, [self.cst_f, adt], [pc])
                self.MM(pc[:, 16:32], ones_f, a, True, True, [self.cst_f, adt], [pc])
                negc, tw, etot = smt[:, 0, :], smt[:, 1, :], smt[:, 2, :]
                self.TS(kb.dve, negc, pc[:, 0:16], -1.0, None, ALU.mult, None, [pc], [smt])
                self.TT(kb.dve, tw, pc[:, 16:32], negc, ALU.add, [pc, smt], [smt])
                self.ACT(tw, tw, AF.Exp, [smt], [smt])
                self.ACT(etot, pc[:, 16:32], AF.Exp, [pc], [smt])
                self.TT(kb.pool, Yt[:], Tm.unsqueeze(1).to_broadcast([128, 16, 128]),
                        a.unsqueeze(2).to_broadcast([128, 16, 128]), ALU.mult, [self.cst_f, adt], [Yt])
                for g in range(2):
                    self.MM(pb[3][:, g * 128:(g + 1) * 128], bt[:, g, :], ct[:, g, :], True, True, [bt, ct], [pb[3]])
                self.TT(kb.dve, GM[:], pb[3][:, 0:256].rearrange("p (g l) -> p g l", g=2),
                        Tm.unsqueeze(1).to_broadcast([128, 2, 128]), ALU.mult, [pb[3], self.cst_f], [GM])
                self.TT(kb.dve, xdt[:].rearrange("p (h q) -> p h q", q=64), xs[:].rearrange("p (h q) -> p h q", q=64),
                        dtv.unsqueeze(2).to_broadcast([128, 16, 64]), ALU.mult, [xs, adt], [xdt])
                self.TT(kb.pool, xdtw[:].rearrange("p (h q) -> p h q", q=64), xdt[:].rearrange("p (h q) -> p h q", q=64),
                        tw.unsqueeze(2).to_broadcast([128, 16, 64]), ALU.mult, [xdt, smt], [xdtw])
                for j in range(4):
                    prc = pb[1 + j % 2]
                    self.MM(prc[:, :512], ones_f, Yt[:, 4 * j:4 * j + 4, :].rearrange("p h l -> p (h l)"), True, True,
                            [self.cst_f, Yt], [prc])
                    for hh in range(4):
                        h = 4 * j + hh
                        g = h // 8
                        rc = prc[:, hh * 128:(hh + 1) * 128]
                        E, Er, Mh, Cs = Es[h % 2], Ers[h % 2], Mhs[h % 4], Css[h % 4]
                        self.ACT(E[:], rc, AF.Exp, [prc, smt], [E], bias=negc[:, h:h + 1])
                        self.STT(Mh[:], E[:], 1.0, GM[:, g, :], ALU.min, ALU.mult, [E, GM], [Mh])
                        self.ACT(Er[:], rc, AF.Exp, [prc], [Er])
                        self.TT(kb.dve, Cs[:], ct[:, g, :], Er[:], ALU.mult, [ct, Er], [Cs])
                        c = h // 2
                        po = (h % 2) * 64
                        yb = pb[4 + c // 4]
                        yc = (c % 4) * 128
                        self.MM(yb[po:po + 64, yc:yc + 128], xdt[:, h * 64:(h + 1) * 64], Mh[:], True, False, [xdt, Mh], [yb])
                        self.MM(yb[po:po + 64, yc:yc + 128], Hb[:, g, (h % 8) * 64:(h % 8 + 1) * 64], Cs[:], False, True, [Hb, Cs], [yb])
                self.ACT(yst[:, 0:4, :], pb[4][:, :512].rearrange("p (c l) -> p c l", c=4), AF.Copy, [pb[4]], [yst])
                self.COPY(kb.dve, yst[:, 4:8, :], pb[5][:, :512].rearrange("p (c l) -> p c l", c=4), [pb[5]], [yst])
                self.STORE(self.cm(self.Y[d][seg])[:, :, t0:t0 + 128], yst[:], yst, self.Y[d][seg])
                for g in range(2):
                    self.MM(pb[6 + g][:, :512], btm[:, g * 128:(g + 1) * 128], xdtw[:, g * 512:(g + 1) * 512], True, True,
                            [btm, xdtw], [pb[6 + g]])
                    self.TT(kb.dve, HT[:, g, :].rearrange("p (h q) -> p h q", q=64), HT[:, g, :].rearrange("p (h q) -> p h q", q=64),
                            etot[:, g * 8:(g + 1) * 8].unsqueeze(2).to_broadcast([128, 8, 64]), ALU.mult, [HT, smt], [HT])
                    self.TT(kb.dve, HT[:, g, :], HT[:, g, :], pb[6 + g][:, :512], ALU.add, [HT, pb[6 + g]], [HT])
                self.ACT(Hb[:], HT[:], AF.Copy, [HT], [Hb])
        kb.phase_end()

    def gla_scan(self, e):
        kb = self.kb
        pb = self.pb
        kb.phase_begin("gla")
        S = kb.sbuf("S", [128, 8, 128], F32)
        Sb = kb.sbuf("Sb", [128, 8, 128], BF16)
        zb = kb.sbuf("zb", [128, 128], BF16)
        kb.op(kb.dve, lambda en: en.memset(zb[:], 0.0), [], [zb])
        rp = kb.sbuf("rp", [128, 1024], F32)
        self.LOAD(rp[:], self.in_rp[:], self.in_rp, rp)
        qs = self.pool("q", 2, [128, 8, 128], BF16)
        lfs = self.pool("lf", 2, [128, 8, 128], F32)
        kks = self.pool("kk", 2, [128, 8, 128], BF16)
        vs = self.pool("v", 2, [128, 1024], BF16)
        F = kb.sbuf("F", [128, 8, 128], F32)
        G = kb.sbuf("G", [128, 8, 128], F32)
        D1 = kb.sbuf("D1", [128, 8, 128], F32)
        D2 = kb.sbuf("D2", [128, 8, 128], F32)
        Ex = self.pool("Ex", 2, [128, 8, 128], F32)
        qh = kb.sbuf("qh", [128, 8, 128], BF16)
        kh = kb.sbuf("kh", [128, 8, 128], BF16)
        qt = kb.sbuf("qt", [128, 8, 128], BF16)
        kt = kb.sbuf("kt", [128, 8, 128], BF16)
        ktm = kb.sbuf("ktm", [128, 1024], BF16)
        ktmz = kb.sbuf("ktmz", [128, 1024], BF16)
        AM = kb.sbuf("AM", [128, 8, 128], BF16)
        dd = kb.sbuf("dd", [128, 8, 4], F32)
        osts = self.pool("ost", 2, [128, 8, 128], F32)

        def v4(b):
            return b[:].rearrange("p h (s j) -> p h s j", j=32)

        def fl(b):
            return b[:].rearrange("p h t -> p (h t)")
        it = 0
        for d in range(2):
            kb.op(kb.dve, lambda en: en.memset(S[:], 0.0), [], [S])
            kb.op(kb.dve, lambda en: en.memset(Sb[:], 0.0), [], [Sb])
            mask = self.cst_f[:, 5 + d, :]
            for (seg, t0) in self.chunk_order(d):
                q, lf, kk, v, ost = qs[it % 2], lfs[it % 2], kks[it % 2], vs[it % 2], osts[it % 2]
                it += 1
                self.LOAD(q[:], self.cm(self.QS[seg])[:, :, t0:t0 + 128], self.QS[seg], q)
                self.LOAD(lf[:], self.cm(self.LF[d][seg])[:, :, t0:t0 + 128], self.LF[d][seg], lf)
                self.LOAD(kk[:], self.cm(self.KK[d][seg])[:, :, t0:t0 + 128], self.KK[d][seg], kk)
                self.LOAD(v[:], self.V[seg].t[t0:t0 + 128, :], self.V[seg], v)
                kb.op(kb.dve, lambda en, lf=lf: en.tensor_tensor_scan(out=fl(F), data0=rp[:], data1=fl(lf), initial=0.0,
                                                                      op0=ALU.mult, op1=ALU.add), [rp, lf], [F])
                Fl = v4(F)[:, :, :, 31:32]
                if d == 0:
                    Gb = F
                else:
                    self.TT(kb.pool, G[:], lf[:], F[:], ALU.subtract, [lf, F], [G])
                    self.TT(kb.dve, v4(G), v4(G), Fl.to_broadcast([128, 8, 4, 32]), ALU.add, [G, F], [G])
                    Gb = G
                self.TT(kb.dve, v4(D1), v4(Gb), v4(Gb)[:, :, :, 16:17].to_broadcast([128, 8, 4, 32]), ALU.subtract, [Gb], [D1])
                self.TS(kb.pool, D1[:], D1[:], 40.0, -40.0, ALU.min, ALU.max, [D1], [D1])
                self.TT(kb.dve, v4(D2), Fl.to_broadcast([128, 8, 4, 32]), v4(Gb), ALU.subtract, [F, Gb], [D2])
                E1, E2 = Ex[0], Ex[1]
                self.ACT(E1[:], D1[:], AF.Exp, [D1], [E1])
                self.TT(kb.dve, qh[:], q[:], E1[:], ALU.mult, [q, E1], [qh])
                self.ACT(E2[:], D1[:], AF.Exp, [D1], [E2], scale=-1.0)
                self.TT(kb.pool, kh[:], kk[:], E2[:], ALU.mult, [kk, E2], [kh])
                self.ACT(E1[:], Gb[:], AF.Exp, [Gb], [E1])
                self.TT(kb.dve, qt[:], q[:], E1[:], ALU.mult, [q, E1], [qt])
                self.ACT(E2[:], D2[:], AF.Exp, [D2], [E2])
                self.TT(kb.pool, kt[:], kk[:], E2[:], ALU.mult, [kk, E2], [kt])
                self.ACT(dd[:], v4(F)[:, :, :, 31], AF.Exp, [F], [dd])
                for h in range(8):
                    self.MM(pb[6 + h // 4][:, (h % 4) * 128:(h % 4 + 1) * 128], kt[:, h, :], self.ident_b, True, True,
                            [kt, self.cst_b], [pb[6 + h // 4]])
                self.ACT(ktm[:, 0:512], pb[6][:, :512], AF.Copy, [pb[6]], [ktm])
                self.COPY(kb.dve, ktm[:, 512:1024], pb[7][:, :512], [pb[7]], [ktm])
                self.ACT(ktmz[64:128, 0:512], pb[6][64:128, :512], AF.Copy, [pb[6]], [ktmz])
                self.COPY(kb.dve, ktmz[64:128, 512:1024], pb[7][64:128, :512], [pb[7]], [ktmz])
                kb.op(kb.pool, lambda en: en.memset(ktmz[64:96, :], 0.0), [], [ktmz])
                for h in range(8):
                    self.MM(pb[h // 4][:, (h % 4) * 128:(h % 4 + 1) * 128], kh[:, h, :], qh[:, h, :], True, True, [kh, qh], [pb[h // 4]])
                for half in range(2):
                    self.TT(kb.dve, AM[:, 4 * half:4 * half + 4, :], pb[half][:, :512].rearrange("p (h t) -> p h t", h=4),
                            mask.unsqueeze(1).to_broadcast([128, 4, 128]), ALU.mult, [pb[half], self.cst_f], [AM])
                for half in range(2):
                    self.MM(pb[2 + half][:, :512], zb[:], AM[:, 4 * half:4 * half + 4, :].rearrange("p h t -> p (h t)"), True, False,
                            [zb, AM], [pb[2 + half]])
                for h in range(8):
                    self.MM(pb[2 + h // 4][:, (h % 4) * 128:(h % 4 + 1) * 128], v[:, h * 128:(h + 1) * 128], AM[:, h, :], False, False,
                            [v, AM], [pb[2 + h // 4]])
                subs = [0, 1, 2, 3] if d == 0 else [3, 2, 1, 0]
                for si, i in enumerate(subs):
                    for h in range(8):
                        c0 = (h % 4) * 128 + 32 * i
                        self.MM(pb[2 + h // 4][:, c0:c0 + 32], Sb[:, h, :], qt[:, h, 32 * i:32 * i + 32], False, si == 3,
                                [Sb, qt], [pb[2 + h // 4]])
                    for h in range(8):
                        if i < 3:
                            kl, vr = ktm[32 * i:32 * i + 32, h * 128:(h + 1) * 128], v[32 * i:32 * i + 32, h * 128:(h + 1) * 128]
                        else:
                            kl, vr = ktmz[64:128, h * 128:(h + 1) * 128], v[64:128, h * 128:(h + 1) * 128]
                        self.MM(pb[4 + h // 4][:, (h % 4) * 128:(h % 4 + 1) * 128], kl, vr, True, True, [ktm, ktmz, v], [pb[4 + h // 4]])
                    for h in range(8):
                        self.STT(S[:, h, :], S[:, h, :], dd[:, h, i:i + 1], pb[4 + h // 4][:, (h % 4) * 128:(h % 4 + 1) * 128],
                                 ALU.mult, ALU.add, [S, dd, pb[4 + h // 4]], [S])
                    self.ACT(Sb[:], S[:], AF.Copy, [S], [Sb])
                self.ACT(ost[:, 0:4, :], pb[2][:, :512].rearrange("p (h t) -> p h t", h=4), AF.Copy, [pb[2]], [ost])
                self.COPY(kb.dve, ost[:, 4:8, :], pb[3][:, :512].rearrange("p (h t) -> p h t", h=4), [pb[3]], [ost])
                self.STORE(self.cm(self.O[d][seg])[:, :, t0:t0 + 128], ost[:], ost, self.O[d][seg])
        kb.phase_end()

    def merge(self, l, e, ctx_out):
        kb = self.kb
        kb.phase_begin("merge")
        self.ln_alloc()
        wo = kb.sbuf("wo", [128, 16, D], BF16)
        self.load_w(wo, self.in_mix_out, self.in_mix_out[e], 16, D, blk=128)
        NT = 256
        y0 = kb.sbuf("y0", [128, KC, NT], F32)
        y1 = kb.sbuf("y1", [128, KC, NT], F32)
        o0 = kb.sbuf("o0", [128, KC, NT], F32)
        o1 = kb.sbuf("o1", [128, KC, NT], F32)
        xs = kb.sbuf("xsc", [128, KC, NT], BF16)
        zs = kb.sbuf("zsc", [128, KC, NT], BF16)
        gs = kb.sbuf("gsc", [128, KC, NT], BF16)
        xts = self.pool("xt", 2, [128, KC, NT], F32)
        sq = kb.sbuf("sq", [128, KC, NT], BF16)
        cat = kb.sbuf("cat", [128, 16, NT], BF16)
        rstd = kb.sbuf("rstd", [128, 2, NT], F32)
        tmp = self.pool("mt", 2, [128, NT], F32)
        rt = kb.sbuf("rt", [128, KC, NT], F32)
        self.rtmp = self.pool("rtmp", 2, [128, 512], F32)
        self.rti = 0
        tl = []
        for seg, Xb, Tn in self.segs(ctx_out):
            for i in range(Tn // NT):
                tl.append((seg, Xb, i * NT, NT))
        for it, (seg, Xb, t0, N) in enumerate(tl):
            xt = xts[it % 2]
            sl = slice(t0, t0 + N)
            self.LOAD(y0[:], self.cm(self.Y[0][seg])[:, :, sl], self.Y[0][seg], y0)
            self.LOAD(y1[:], self.cm(self.Y[1][seg])[:, :, sl], self.Y[1][seg], y1)
            self.LOAD(o0[:], self.cm(self.O[0][seg])[:, :, sl], self.O[0][seg], o0)
            self.LOAD(o1[:], self.cm(self.O[1][seg])[:, :, sl], self.O[1][seg], o1)
            self.LOAD(xs[:], self.cm(self.XSC[seg])[:, :, sl], self.XSC[seg], xs)
            self.LOAD(zs[:], self.cm(self.ZS[seg])[:, :, sl], self.ZS[seg], zs)
            self.LOAD(gs[:], self.cm(self.GS[seg])[:, :, sl], self.GS[seg], gs)
            self.LOAD(xt[:], self.cm(Xb)[:, :, sl], Xb, xt)
            self.TT(kb.dve, y0[:], y0[:], y1[:], ALU.add, [y0, y1], [y0])
            for c in range(KC):
                self.STT(y0[:, c, :], xs[:, c, :], self.ppc(f"sdd{e}", c), y0[:, c, :], ALU.mult, ALU.add, [xs, y0, self.ppb], [y0])
            self.TT(kb.dve, y0[:], y0[:], zs[:], ALU.mult, [y0, zs], [y0])
            self.ACT(sq[:], y0[:], AF.Square, [y0], [sq])
            pst = self.ps()
            for g in range(2):
                for c in range(4 * g, 4 * g + 4):
                    self.MM(pst[:, g * N:(g + 1) * N], self.onesm_b, sq[:, c, :], c == 4 * g, c == 4 * g + 3, [self.cst_b, sq], [pst])
            self.ACT(rstd[:], pst[:, :2 * N].rearrange("p (g n) -> p g n", g=2), AF.Ln, [pst, self.eps_t], [rstd], bias=self.eps_ap, scale=2.0)
            self.ACT(rstd[:], rstd[:], AF.Exp, [rstd], [rstd], scale=-0.5)
            for c in range(KC):
                self.STT(cat[:, c, :], y0[:, c, :], self.ppc(f"snw{e}", c), rstd[:, c // 4, :], ALU.mult, ALU.mult,
                         [y0, rstd, self.ppb], [cat])
            self.TT(kb.dve, o0[:], o0[:], o1[:], ALU.add, [o0, o1], [o0])
            self.ACT(sq[:], o0[:], AF.Square, [o0], [sq])
            for c2 in range(0, KC, 2):
                pso = self.ps()
                for j in range(2):
                    self.MM(pso[:, j * N:(j + 1) * N], self.onesm_b, sq[:, c2 + j, :], True, True, [self.cst_b, sq], [pso])
                self.ACT(rstd[:], pso[:, :2 * N].rearrange("p (g n) -> p g n", g=2), AF.Ln, [pso, self.eps_t], [rstd], bias=self.eps_ap, scale=8.0)
                self.ACT(rstd[:], rstd[:], AF.Exp, [rstd], [rstd], scale=-0.5)
                for j in range(2):
                    c = c2 + j
                    tm = tmp[j]
                    self.STT(tm[:], o0[:, c, :], self.ppc(f"hnw{e}", c), rstd[:, j, :], ALU.mult, ALU.mult, [o0, rstd, self.ppb], [tm])
                    self.TT(kb.pool, cat[:, 8 + c, :], tm[:], gs[:, c, :], ALU.mult, [tm, gs], [cat])
            for co in range(KC):
                py = self.ps()
                for k in range(16):
                    self.MM(py[:, :N], wo[:, k, co * 128:(co + 1) * 128], cat[:, k, :], k == 0, k == 15, [wo, cat], [py])
                self.residual(py, co, xt, rt, N, l, 2, seg)
            self.ln_tile(rt, N, f"plg{l}_0", f"plb{l}_0", AF.Identity, xt)
            self.STORE(self.cm(Xb)[:, :, sl], xt[:], xt, Xb)
        kb.phase_end()

    def alloc_work(self):
        kb = self.kb
        T, M = self.T, self.M
        self.eps_t = kb.sbuf("eps_t", [128, 1], F32)
        kb.op(kb.dve, lambda e: e.memset(self.eps_t[:], EPS), [], [self.eps_t])
        self.eps_ap = self.eps_t[:, 0:1]
        self.one_t = kb.sbuf("one_t", [128, 1], F32)
        kb.op(kb.dve, lambda e: e.memset(self.one_t[:], 1.0), [], [self.one_t])
        self.one_ap = self.one_t[:, 0:1]
        self.lbb = kb.sbuf("lbb", [128, 3, 2, 8], F32)
        self.lbv, self.oml, self.noml = self.lbb[:, 0], self.lbb[:, 1], self.lbb[:, 2]
        kb.op(kb.dve, lambda e: e.memset(self.lbb[:], 0.0), [], [self.lbb])
        self.TT(kb.dve, self.lbb[:, 0, 1, :], self.ppc("lbr1"), self.ppc("lbr0"), ALU.subtract, [self.ppb], [self.lbb])
        self.ACT(self.lbb[:, 0, 1, :], self.lbb[:, 0, 1, :], AF.Sigmoid, [self.lbb], [self.lbb])
        self.TS(kb.dve, self.lbb[:, 1, :, :], self.lbb[:, 0, :, :], -1.0, 1.0, ALU.mult, ALU.add, [self.lbb], [self.lbb])
        self.TS(kb.dve, self.lbb[:, 2, :, :], self.lbb[:, 1, :, :], -1.0, None, ALU.mult, None, [self.lbb], [self.lbb])
        self.acb = kb.sbuf("acb", [128, 2, 32], F32)
        self.acoef = self.acb.t
        self.ACT(self.acb[:].rearrange("p e c -> p (e c)"), self.tmb[:, 1, :], AF.Exp, [self.tmb], [self.acb])
        self.TS(kb.dve, self.acb[:], self.acb[:], -1.0, None, ALU.mult, None, [self.acb], [self.acb])
        TM = (T, M)
        self.ZS = [self.scratch(f"ZS{i}", [D, TM[i]], BF16) for i in range(2)]
        self.QS = [self.scratch(f"QS{i}", [D, TM[i]], BF16) for i in range(2)]
        self.GS = [self.scratch(f"GS{i}", [D, TM[i]], BF16) for i in range(2)]
        self.XBC = [self.scratch(f"XBC{i}", [1536, TM[i] + 4], BF16) for i in range(2)]
        self.ADT = [self.scratch(f"ADT{i}", [TM[i], 2, 32], F32) for i in range(2)]
        self.LF = [[self.scratch(f"LF{d}{i}", [D, TM[i]], F32) for i in range(2)] for d in range(2)]
        self.KK = [[self.scratch(f"KK{d}{i}", [D, TM[i]], BF16) for i in range(2)] for d in range(2)]
        self.V = [self.scratch(f"V{i}", [TM[i], D], BF16) for i in range(2)]
        self.XSC = [self.scratch(f"XSC{i}", [D, TM[i]], BF16) for i in range(2)]
        self.XST = [self.scratch(f"XST{i}", [TM[i], D], BF16) for i in range(2)]
        self.BT = [self.scratch(f"BT{i}", [256, TM[i]], BF16) for i in range(2)]
        self.CT = [self.scratch(f"CT{i}", [256, TM[i]], BF16) for i in range(2)]
        self.BTM = [self.scratch(f"BTM{i}", [TM[i], 256], BF16) for i in range(2)]
        self.Y = [[self.scratch(f"Y{d}{i}", [D, TM[i]], F32) for i in range(2)] for d in range(2)]
        self.O = [[self.scratch(f"O{d}{i}", [D, TM[i]], F32) for i in range(2)] for d in range(2)]
        self.GLp = [self.scratch("GL0", [D, T + 30], BF16), self.scratch("GL1", [D, M + 30], BF16)]
        self.CU = [self.scratch("CU0", [D, T], F32), self.scratch("CU1", [D, M], F32)]
        self.FV = [self.scratch("FV0", [D_FF, T], BF16), self.scratch("FV1", [D_FF, M], BF16)]
        self.FG = [self.scratch("FG0", [D_FF, T], BF16), self.scratch("FG1", [D_FF, M], BF16)]
        self.U = [self.scratch("U0", [D_FF, T], BF16), self.scratch("U1", [D_FF, M], BF16)]

    def copy_in(self):
        kb = self.kb
        kb.phase_begin("copy_in")
        xts = self.pool("xt", 2, [128, KC, 512], F32)
        it = 0
        for src, dst, Tn in ((self.in_x, self.X, self.T), (self.in_ctx, self.XC, self.M)):
            for i in range((Tn + 511) // 512):
                t0 = i * 512
                N = min(512, Tn - t0)
                xt = xts[it % 2]
                it += 1
                self.LOAD(xt[:, :, :N], self.cm(src)[:, :, t0:t0 + N], src, xt)
                self.STORE(self.cm(dst)[:, :, t0:t0 + N], xt[:, :, :N], xt, dst)
        z = kb.sbuf("zpad", [128, KC, 16], BF16)
        kb.op(kb.dve, lambda e: e.memset(z[:], 0.0), [], [z])
        z12 = kb.sbuf("zpad12", [128, 12, 2], BF16)
        kb.op(kb.dve, lambda e: e.memset(z12[:], 0.0), [], [z12])
        for seg, Tn in ((0, self.T), (1, self.M)):
            v = self.cm(self.GLp[seg])
            self.STORE(v[:, :, 0:15], z[:, :, 0:15], z, self.GLp[seg])
            self.STORE(v[:, :, 15 + Tn:30 + Tn], z[:, :, 0:15], z, self.GLp[seg])
            xv = self.cm(self.XBC[seg])
            self.STORE(xv[:, :, 0:2], z12[:, :, 0:2], z12, self.XBC[seg])
            self.STORE(xv[:, :, 2 + Tn:4 + Tn], z12[:, :, 0:2], z12, self.XBC[seg])
        kb.phase_end()

    def build(self, pp_off, npp):
        self.setup(pp_off, npp)
        self.alloc_work()
        self.ada()
        self.copy_in()
        nl = len(self.layers)
        for li, l in enumerate(self.layers):
            ctx_next = any(j % 2 == 0 for j in range(l + 1, DEPTH))
            if l % 2 == 0:
                self.even_mixer(l, ctx_next)
            else:
                self.conformer(l, ctx_next)
            self.ffn(l, ctx_next, final=(li == nl - 1))
        self.kb.finish([self.last_tok])


def build_program(T, M, layers, pp_off, npp, dbg=()):
    nc = bass.Bass("TRN2", target_bir_lowering=False)
    with ExitStack() as st:
        g = Gen(nc, st, T, M, layers, dbg)
        g.build(pp_off, npp)
    return nc, g


def make_in_maps(inp, T, M, nb):
    pp = pack_pp(inp)
    ppa = pp.pack()
    cst = make_consts()
    maps = []
    for b in range(nb):
        m = {
            "x_cm": np.ascontiguousarray(inp["x"][b].T.astype(np.float32)),
            "ctx_cm": np.ascontiguousarray(inp["ctx"][b].T.astype(np.float32)),
            "cvec": np.ascontiguousarray(np.stack([_cp(inp["c"][b]), _cp(inp["c_ctx"])], axis=2)),
            "pp": ppa,
            "cmat": cst["cmat"],
            "rp": cst["rp"],
            "tmb": np.ascontiguousarray(np.broadcast_to(np.stack([
                np.asarray(inp["ssd_dt_bias"], np.float32).reshape(64),
                np.asarray(inp["ssd_a_log"], np.float32).reshape(64)], axis=0)[None], (128, 2, 64))),
            "mix_w_in": np.asarray(inp["mix_w_in"], np.float32),
            "mix_w_out": np.asarray(inp["mix_w_out"], np.float32),
            "ada_w": np.asarray(inp["ada_w"], np.float32),
            "conf_w1": np.asarray(inp["conf_w1"], np.float32),
            "conf_w2": np.asarray(inp["conf_w2"], np.float32),
            "ffn_w_up": np.asarray(inp["ffn_w_up"], np.float32),
            "ffn_w_down": np.asarray(inp["ffn_w_down"], np.float32),
        }
        maps.append(m)
    return maps, pp


def kernel(**inputs):
    inp = {k: np.asarray(v) for k, v in inputs.items()}
    B, T, _ = inp["x"].shape
    M = inp["ctx"].shape[1]
    maps, pp = make_in_maps(inp, T, M, B)
    nc, g = build_program(T, M, list(range(DEPTH)), pp.off, pp.n)
    in_maps = [maps[i % B] for i in range(8)]
    res = run_bass_kernel_spmd(nc, in_maps, core_ids=list(range(8)))
    out = np.stack([res.results[b]["y_cm"].T for b in range(B)], axis=0)
    return np.ascontiguousarray(out.astype(np.float32))
```
